# Optimizing a Trainium2 kernel written in Bass

```python
import jax, jax.numpy as jnp
from jax import lax
import numpy as np

D_MODEL = 1024
BATCH = 16
SEQ = 2048
DEPTH = 1

CHUNK = 64
Q_BLOCK = 128
N_MEM = 256
RMS_EPS = 1e-6
FFN_HIDDEN = 2816
MLA_HEADS = 8
Q_LORA = 384
KV_LORA = 256
NOPE_DIM = 128
ROPE_DIM = 64
V_DIM = 128
ROPE_THETA = 10000.0
RNN_WIDTH = 1024
RNN_BLOCKS = 8
RNN_BLOCK_DIM = RNN_WIDTH // RNN_BLOCKS
CONV_WIDTH = 4
LRU_C = 8.0
MEM_HEADS = 4
MEM_HEAD_DIM = 256
MEM_WIDTH = MEM_HEADS * MEM_HEAD_DIM
N_BRANCH = 3
BRANCH_WIDTH = MLA_HEADS * V_DIM
IN_SPLITS = (Q_LORA, KV_LORA, ROPE_DIM, RNN_WIDTH, RNN_WIDTH, MEM_WIDTH, N_BRANCH * D_MODEL)
IN_WIDTH = sum(IN_SPLITS)
SPLIT_POINTS = [int(p) for p in np.cumsum(IN_SPLITS)[:-1]]

kernel_name = 'hybrid_mla_rglru_memxattn_macaron'


def rms_norm(x, g):
    xf = x.astype(jnp.float32)
    y = xf * lax.rsqrt(jnp.mean(xf * xf, axis=-1, keepdims=True) + RMS_EPS)
    return (y * g.astype(jnp.float32)).astype(x.dtype)


def swiglu_half_step(x, g, w_in, w_down):
    h = rms_norm(x, g)
    gate, up = jnp.split(h @ w_in, 2, axis=-1)
    return x + 0.5 * ((jax.nn.silu(gate) * up) @ w_down)


def rotary_tables(seq_len, dtype):
    pos = jnp.arange(seq_len, dtype=jnp.float32)
    inv_freq = 1.0 / (ROPE_THETA ** (jnp.arange(0, ROPE_DIM, 2, dtype=jnp.float32) / ROPE_DIM))
    ang = pos[:, None] * inv_freq[None, :]
    return jnp.cos(ang).astype(dtype), jnp.sin(ang).astype(dtype)


def rotary(x, cos, sin):
    x1, x2 = jnp.split(x, 2, axis=-1)
    return jnp.concatenate([x1 * cos - x2 * sin, x2 * cos + x1 * sin], axis=-1)


def block_causal_attention(q, k, v):
    seq = q.shape[1]
    scale = q.shape[-1] ** -0.5
    outs = []
    for blk in range(seq // Q_BLOCK):
        q_lo, q_hi = blk * Q_BLOCK, (blk + 1) * Q_BLOCK
        qb = q[:, q_lo:q_hi]
        kb = k[:, :q_hi]
        vb = v[:, :q_hi]
        s = jnp.einsum('bqhd,bkhd->bhqk', qb, kb).astype(jnp.float32) * scale
        q_chunk = jnp.arange(q_lo, q_hi) // CHUNK
        k_chunk = jnp.arange(q_hi) // CHUNK
        s = jnp.where(k_chunk[None, :] <= q_chunk[:, None], s, -jnp.inf)
        p = jax.nn.softmax(s, axis=-1).astype(vb.dtype)
        outs.append(jnp.einsum('bhqk,bkhd->bqhd', p, vb))
    return jnp.concatenate(outs, axis=1)


def mla_branch(z_cq, z_ckv, z_kr, q_norm, w_uq, kv_norm, w_ukv):
    bsz, seq = z_cq.shape[:2]
    cos, sin = rotary_tables(seq, z_cq.dtype)
    q = (rms_norm(z_cq, q_norm) @ w_uq).reshape(bsz, seq, MLA_HEADS, NOPE_DIM + ROPE_DIM)
    q_nope, q_rope = jnp.split(q, [NOPE_DIM], axis=-1)
    q = jnp.concatenate([q_nope, rotary(q_rope, cos[:, None, :], sin[:, None, :])], axis=-1)
    kv = (rms_norm(z_ckv, kv_norm) @ w_ukv).reshape(bsz, seq, MLA_HEADS, NOPE_DIM + V_DIM)
    k_nope, v = jnp.split(kv, [NOPE_DIM], axis=-1)
    k_rope = rotary(z_kr, cos, sin)
    k_rope = jnp.broadcast_to(k_rope[:, :, None, :], (bsz, seq, MLA_HEADS, ROPE_DIM))
    k = jnp.concatenate([k_nope, k_rope], axis=-1)
    o = block_causal_attention(q, k, v)
    return o.reshape(bsz, seq, MLA_HEADS * V_DIM)


def _linear_scan_combine(left, right):
    a_l, b_l = left
    a_r, b_r = right
    return a_l * a_r, a_r * b_l + b_r


def rglru_branch(z_x, z_g, conv_w, conv_b, w_rg_a, b_rg_a, w_rg_i, b_rg_i, lru_lambda):
    bsz, seq = z_x.shape[:2]
    xc = lax.conv_general_dilated(
        z_x, conv_w, window_strides=(1,), padding=[(CONV_WIDTH - 1, 0)],
        dimension_numbers=('NWC', 'WIO', 'NWC'), feature_group_count=RNN_WIDTH) + conv_b
    xr = xc.reshape(bsz, seq, RNN_BLOCKS, RNN_BLOCK_DIM)
    r = jax.nn.sigmoid(jnp.einsum('bsnd,nde->bsne', xr, w_rg_a) + b_rg_a).reshape(bsz, seq, RNN_WIDTH)
    i = jax.nn.sigmoid(jnp.einsum('bsnd,nde->bsne', xr, w_rg_i) + b_rg_i).reshape(bsz, seq, RNN_WIDTH)
    log_a = -LRU_C * r.astype(jnp.float32) * jax.nn.softplus(-lru_lambda.astype(jnp.float32))
    a = jnp.exp(log_a)
    u = jnp.sqrt(-jnp.expm1(2.0 * log_a)) * (i * xc).astype(jnp.float32)
    _, h = lax.associative_scan(_linear_scan_combine, (a, u), axis=1)
    return h.astype(z_x.dtype) * jax.nn.gelu(z_g)


def memory_branch(z_mq, mem, mem_norm, w_mem_kv):
    bsz, seq = z_mq.shape[:2]
    q = z_mq.reshape(bsz, seq, MEM_HEADS, MEM_HEAD_DIM)
    k, v = jnp.split(rms_norm(mem, mem_norm) @ w_mem_kv, 2, axis=-1)
    k = k.reshape(bsz, -1, MEM_HEADS, MEM_HEAD_DIM)
    v = v.reshape(bsz, -1, MEM_HEADS, MEM_HEAD_DIM)
    s = jnp.einsum('bshd,bmhd->bhsm', q, k).astype(jnp.float32) * (MEM_HEAD_DIM ** -0.5)
    p = jax.nn.softmax(s, axis=-1).astype(v.dtype)
    return jnp.einsum('bhsm,bmhd->bshd', p, v).reshape(bsz, seq, MEM_WIDTH)


def setup_inputs(seed: int = 0) -> dict:
    key = jax.random.key(seed)
    ks = jax.random.split(key, 32)
    L = DEPTH

    def dense(k, shape, fan_in):
        return jax.random.normal(k, shape, jnp.float32) * fan_in ** -0.5

    def gain(k, shape):
        return 1.0 + 0.02 * jax.random.normal(k, shape, jnp.float32)

    def small(k, shape):
        return 0.01 * jax.random.normal(k, shape, jnp.float32)

    a_c = jax.random.uniform(ks[18], (L, RNN_WIDTH), jnp.float32, 0.9, 0.999)
    base = a_c ** (1.0 / LRU_C)
    lru_lambda = jnp.log(base) - jnp.log1p(-base)
    return {
        'x': jax.random.normal(ks[0], (BATCH, SEQ, D_MODEL), jnp.float32),
        'mem': jax.random.normal(ks[1], (BATCH, N_MEM, D_MODEL), jnp.float32),
        'ffn1_norm': gain(ks[2], (L, D_MODEL)),
        'ffn1_w_in': dense(ks[3], (L, D_MODEL, 2 * FFN_HIDDEN), D_MODEL),
        'ffn1_w_down': dense(ks[4], (L, FFN_HIDDEN, D_MODEL), FFN_HIDDEN),
        'mix_norm': gain(ks[5], (L, D_MODEL)),
        'w_in': dense(ks[6], (L, D_MODEL, IN_WIDTH), D_MODEL),
        'b_gate': small(ks[7], (L, N_BRANCH, D_MODEL)),
        'q_norm': gain(ks[8], (L, Q_LORA)),
        'w_uq': dense(ks[9], (L, Q_LORA, MLA_HEADS * (NOPE_DIM + ROPE_DIM)), Q_LORA),
        'kv_norm': gain(ks[10], (L, KV_LORA)),
        'w_ukv': dense(ks[11], (L, KV_LORA, MLA_HEADS * (NOPE_DIM + V_DIM)), KV_LORA),
        'conv_w': dense(ks[12], (L, CONV_WIDTH, 1, RNN_WIDTH), CONV_WIDTH),
        'conv_b': small(ks[13], (L, RNN_WIDTH)),
        'w_rg_a': dense(ks[14], (L, RNN_BLOCKS, RNN_BLOCK_DIM, RNN_BLOCK_DIM), RNN_BLOCK_DIM),
        'b_rg_a': small(ks[15], (L, RNN_BLOCKS, RNN_BLOCK_DIM)),
        'w_rg_i': dense(ks[16], (L, RNN_BLOCKS, RNN_BLOCK_DIM, RNN_BLOCK_DIM), RNN_BLOCK_DIM),
        'b_rg_i': small(ks[17], (L, RNN_BLOCKS, RNN_BLOCK_DIM)),
        'lru_lambda': lru_lambda,
        'mem_norm': gain(ks[19], (L, D_MODEL)),
        'w_mem_kv': dense(ks[20], (L, D_MODEL, 2 * MEM_WIDTH), D_MODEL),
        'w_branch': dense(ks[21], (L, N_BRANCH, BRANCH_WIDTH, D_MODEL), BRANCH_WIDTH),
        'w_out': dense(ks[22], (L, D_MODEL, D_MODEL), D_MODEL),
        'ffn2_norm': gain(ks[23], (L, D_MODEL)),
        'ffn2_w_in': dense(ks[24], (L, D_MODEL, 2 * FFN_HIDDEN), D_MODEL),
        'ffn2_w_down': dense(ks[25], (L, FFN_HIDDEN, D_MODEL), FFN_HIDDEN),
        'final_norm': gain(ks[26], (D_MODEL,)),
    }


def reference(x, mem, ffn1_norm, ffn1_w_in, ffn1_w_down, mix_norm, w_in, b_gate,
              q_norm, w_uq, kv_norm, w_ukv, conv_w, conv_b, w_rg_a, b_rg_a,
              w_rg_i, b_rg_i, lru_lambda, mem_norm, w_mem_kv, w_branch, w_out,
              ffn2_norm, ffn2_w_in, ffn2_w_down, final_norm):
    bsz, seq = x.shape[:2]
    for l in range(DEPTH):
        x = swiglu_half_step(x, ffn1_norm[l], ffn1_w_in[l], ffn1_w_down[l])
        h = rms_norm(x, mix_norm[l])
        z = h @ w_in[l]
        z_cq, z_ckv, z_kr, z_x, z_g, z_mq, z_gate = jnp.split(z, SPLIT_POINTS, axis=-1)
        y_a = mla_branch(z_cq, z_ckv, z_kr, q_norm[l], w_uq[l], kv_norm[l], w_ukv[l])
        y_b = rglru_branch(z_x, z_g, conv_w[l], conv_b[l], w_rg_a[l], b_rg_a[l],
                           w_rg_i[l], b_rg_i[l], lru_lambda[l])
        y_c = memory_branch(z_mq, mem, mem_norm[l], w_mem_kv[l])
        branches = jnp.stack([y_a, y_b, y_c], axis=2)
        gates = jax.nn.sigmoid(z_gate.reshape(bsz, seq, N_BRANCH, D_MODEL) + b_gate[l])
        proj = jnp.einsum('bsnc,ncd->bsnd', branches, w_branch[l])
        merged = jnp.sum(gates * proj, axis=2)
        x = x + merged @ w_out[l]
        x = swiglu_half_step(x, ffn2_norm[l], ffn2_w_in[l], ffn2_w_down[l])
    return rms_norm(x, final_norm)
```

```python
import numpy as np
import concourse.bass as bass
import concourse.mybir as mybir
from concourse.bass_utils import run_bass_kernel_spmd

F32 = mybir.dt.float32
BF16 = mybir.dt.bfloat16
AF = mybir.ActivationFunctionType
ALU = mybir.AluOpType


class Sched:
    ENGS = ("pe", "act", "dve", "pool", "sp")

    def __init__(self):
        self.ops = {e: [] for e in self.ENGS}
        self.count = {e: 0 for e in self.ENGS}
        self.last_writer = {}
        self.readers = {}
        self.waited = {e: {} for e in self.ENGS}
        self.dma_cnt = {}
        self.dry = False

    def op(self, eng, fn, reads=(), writes=(), dma=None, ndma=1):
        if self.dry:
            return None
        deps = {}
        for r in reads:
            t = self.last_writer.get(r)
            if t is not None:
                deps[t[0]] = max(deps.get(t[0], 0), t[1])
        for w in writes:
            t = self.last_writer.get(w)
            if t is not None:
                deps[t[0]] = max(deps.get(t[0], 0), t[1])
            for t in self.readers.get(w, ()):
                deps[t[0]] = max(deps.get(t[0], 0), t[1])
        if dma is None:
            self.count[eng] += 1
            tok = (("eng", eng), self.count[eng])
        else:
            self.dma_cnt[dma] = self.dma_cnt.get(dma, 0) + 16 * ndma
            tok = (("dma", dma), self.dma_cnt[dma])
        waits = []
        wd = self.waited[eng]
        for sk, v in deps.items():
            if wd.get(sk, 0) >= v:
                continue
            wd[sk] = v
            waits.append((sk, v))
        self.ops[eng].append((fn, waits, dma is None, tok))
        for r in reads:
            self.readers.setdefault(r, []).append(tok)
        for w in writes:
            self.last_writer[w] = tok
            self.readers[w] = []
        return tok

    def check_progress(self):
        pos = {e: 0 for e in self.ENGS}
        sem = {}
        while True:
            progress = False
            for e in self.ENGS:
                q = self.ops[e]
                while pos[e] < len(q):
                    fn, waits, inc, tok = q[pos[e]]
                    if all(sem.get(sk, 0) >= v for sk, v in waits):
                        sem[tok[0]] = max(sem.get(tok[0], 0), tok[1])
                        pos[e] += 1
                        progress = True
                    else:
                        break
            if not progress:
                break
        stuck = {e: (pos[e], len(self.ops[e])) for e in self.ENGS if pos[e] < len(self.ops[e])}
        assert not stuck, "schedule deadlocks: %r" % stuck

    def emit(self, nc, final_waits=()):
        import contextlib
        dma_keys = sorted(self.dma_cnt.keys(), key=str)
        with contextlib.ExitStack() as es:
            sems = {}
            for e in self.ENGS:
                sems[("eng", e)] = es.enter_context(nc.semaphore("s_" + e))
            for k in dma_keys:
                sems[("dma", k)] = es.enter_context(nc.semaphore("d_" + str(k)))
            block = es.enter_context(nc.Block())
            ops = self.ops

            def run(engname, eng):
                mysem = sems[("eng", engname)]
                for fn, waits, inc, _tok in ops[engname]:
                    for sk, v in waits:
                        eng.wait_ge(sems[sk], v)
                    ins = fn(eng, sems)
                    if inc:
                        ins.then_inc(mysem, 1)

            @block.tensor
            def _(eng):
                run("pe", eng)

            @block.scalar
            def _(eng):
                run("act", eng)

            @block.vector
            def _(eng):
                run("dve", eng)

            @block.gpsimd
            def _(eng):
                run("pool", eng)

            @block.sync
            def _(eng):
                run("sp", eng)
                for k in dma_keys:
                    eng.wait_ge(sems[("dma", k)], self.dma_cnt[k])
                for e in ("pe", "act", "dve", "pool"):
                    if self.count[e]:
                        eng.wait_ge(sems[("eng", e)], self.count[e])


D = 1024
SEQ = 2048
NB = 16
NCORES = 8
SPC = NB // NCORES
ST = 1024
TT = 512
NTT = ST // TT
NST = SEQ // ST
HID = 2816
NHC = HID // 128
G0 = 12
NMEM = 256
EPS = 1e-6
ATT_SCALE = 192.0 ** -0.5
MEM_SCALE = 256.0 ** -0.5
NSLOT = 4
SLOT_ELEMS = 4096

C_FFN1, C_MIX, C_FFN2, C_FIN, C_MEM = 0, 8, 16, 24, 32
C_QN, C_KVN = 40, 43
C_CW, C_CB = 45, 77
C_BA, C_BI, C_LAM = 85, 93, 101
C_BG = 109
NCV = 136


PHASES = {"nmm": 0, "marks": []}


class Arena:
    def __init__(self, nc, name, nbytes):
        self.hb = nc.alloc_sbuf_tensor(name, [128, nbytes // 2], BF16)
        self.hf = self.hb.bitcast(F32)
        self.nbytes = nbytes
        self.off = 0

    def alloc(self, shape, dt, at=None):
        n = int(np.prod(shape))
        nb = n * (4 if dt == F32 else 2)
        off = self.off if at is None else at
        assert off % 32 == 0 and off + nb <= self.nbytes, (off, nb, self.nbytes)
        if dt == F32:
            ap = self.hf[:, off // 4: off // 4 + n]
        else:
            ap = self.hb[:, off // 2: off // 2 + n]
        if len(shape) == 2:
            ap = ap.rearrange("p (a b) -> p a b", a=shape[0])
        elif len(shape) == 3:
            ap = ap.rearrange("p (a b c) -> p a b c", a=shape[0], b=shape[1])
        if at is None:
            self.off = off + (nb + 31) // 32 * 32
        return ap


def build_program():
    nc = bass.Bass("TRN2", target_bir_lowering=False)

    def din(name, shape):
        return nc.dram_tensor(name, list(shape), F32, kind="ExternalInput").ap()

    xT = din("xT", [D, SPC * SEQ])
    memT = din("memT", [D, SPC * NMEM])
    cvec_d = din("cvec", [128, NCV])
    rope_d = din("rope", [128, 2, SEQ])
    wf_i = [din("wf1i", [11, 128, 4096]), din("wf2i", [11, 128, 4096])]
    wf_da = [din("wf1da", [8, 128, G0 * 128]), din("wf2da", [8, 128, G0 * 128])]
    wf_db = [din("wf1db", [8, 128, (NHC - G0) * 128]), din("wf2db", [8, 128, (NHC - G0) * 128])]
    wA1_d = din("wA1", [128, 4096])
    wA2_d = din("wA2", [128, 3072])
    wrg_d = din("wrg", [4, 128, 4096])
    wmq_d = din("wmq", [2, 128, 4096])
    wmg_d = din("wmg", [12, 128, 4096])
    wout_d = din("wout", [2, 128, 4096])
    wuqn_d = din("wuqn", [128, 3072])
    wuqr_d = din("wuqr", [128, 3072])
    wukv_d = din("wukv", [128, 4096])
    wmkv_d = din("wmkv", [4, 128, 4096])
    wrgai_d = din("wrgai", [128, 2048])
    outT = nc.dram_tensor("outT", [D, SPC * SEQ], F32, kind="ExternalOutput").ap()

    total = nc.sbuf_bytes_remaining
    main = Arena(nc, "main", (total - 256) // 64 * 64)
    cvec = main.alloc([1, NCV], F32)[:, 0, :]
    dvec = main.alloc([1, 56], F32)[:, 0, :]
    ones = main.alloc([1, 128], BF16)[:, 0, :]
    maskL = main.alloc([1, 128], BF16)[:, 0, :]
    onesr = main.alloc([1, 64], BF16)[:, 0, :]
    wrgai = main.alloc([2, 8, 128], BF16)
    state = main.alloc([1, 8], F32)[:, 0, :]
    hist = main.alloc([8, 4], F32)
    ring = [main.alloc([1, SLOT_ELEMS], BF16)[:, 0, :] for _ in range(NSLOT)]
    x_sb = main.alloc([8, ST], F32)
    h_sb = main.alloc([8, ST], BF16)
    kn_sb = main.alloc([8, SEQ], BF16)
    v_sb = main.alloc([16, 1024], BF16)
    kr_sb = main.alloc([1, SEQ], BF16)[:, 0, :]
    km_sb = main.alloc([8, NMEM], BF16)
    vm_sb = main.alloc([2, 1024], BF16)
    sq_sb = main.alloc([2, TT], BF16)
    lnv = main.alloc([1, TT], F32)[:, 0, :]
    rstd = main.alloc([1, TT], F32)[:, 0, :]
    rope_sb = main.alloc([2, TT], F32)
    A0 = main.off
    asz = main.nbytes - A0
    assert asz >= 36864, asz

    def aa(shape, dt, rel):
        return main.alloc(shape, dt, at=A0 + rel)

    hid = aa([G0, ST], BF16, 0)
    sg = [aa([1, TT], F32, 24576 + 2048 * i)[:, 0, :] for i in range(2)]
    ost = [aa([1, TT], F32, 28672 + 2048 * i)[:, 0, :] for i in range(2)]
    memx = aa([8, NMEM], F32, 0)
    memn = aa([8, NMEM], BF16, 8192)
    y_sb = aa([8, ST], BF16, 0)
    U = 16384
    cqn = aa([3, TT], BF16, U)
    ckvn = aa([2, TT], BF16, U + 3072)
    qn_sb = aa([2, TT], BF16, U + 5120)
    qrz = aa([4, TT], BF16, U + 7168)
    pT = aa([3, TT], BF16, U + 11264)
    rt = [aa([1, TT], F32, U + 14336 + 2048 * i)[:, 0, :] for i in range(2)]
    merged = aa([8, ST], BF16, U)
    sgm = [aa([1, TT], F32, U + 16384 + 2048 * i)[:, 0, :] for i in range(2)]
    zx = aa([1, 1032], F32, U)[:, 0, :]
    xcs = aa([6, TT], F32, U + 4128)
    xcbs = aa([4, TT], BF16, U + 16416)
    qm_sb = aa([4, TT], BF16, U)
    pm_sb = aa([4, TT], BF16, U + 4096)

    P = [nc.alloc_psum_tensor("ps%d" % i, [128, TT], F32) for i in range(8)]

    S = Sched()
    ctx = {"pending_switch": False}

    def A(keys):
        return ["@" + k for k in keys]

    def op(eng, fn, reads=(), writes=()):
        reads = list(reads)
        writes = list(writes)
        if any(k.startswith("@") for k in reads + writes):
            if ctx["pending_switch"]:
                writes.append("GUARD")
                ctx["pending_switch"] = False
            else:
                reads.append("GUARD")
        return S.op(eng, fn, reads, writes)

    def switch():
        ctx["pending_switch"] = True

    def mark(name):
        if not S.dry:
            PHASES["marks"].append((name, PHASES["nmm"]))

    import collections
    bg = collections.deque()

    def drain(n=3):
        for _ in range(n):
            if not bg:
                return
            bg.popleft()[1]()

    def flush(tt=None):
        if tt is None:
            while bg:
                bg.popleft()[1]()
            return
        last = -1
        for i, (t, _) in enumerate(bg):
            if t == tt:
                last = i
        for _ in range(last + 1):
            bg.popleft()[1]()

    def mm(out, pairs, reads, writes):
        n = len(pairs)
        if not S.dry:
            PHASES["nmm"] += n

        def fn(e, s):
            for i, (l, r) in enumerate(pairs):
                ins = e.matmul(out, lhsT=l, rhs=r, start=(i == 0), stop=(i == n - 1))
            return ins
        op("pe", fn, reads, writes)
        drain()

    def mm1(out, l, r, start, stop, reads, writes):
        if not S.dry:
            PHASES["nmm"] += 1
        op("pe", lambda e, s: e.matmul(out, lhsT=l, rhs=r, start=start, stop=stop), reads, writes)

    def act(out, in_, func, reads, writes, **kw):
        op("act", lambda e, s: e.activation(out=out, in_=in_, func=func, **kw), reads, writes)

    def stt(out, in0, scalar, in1, op0, op1, reads, writes):
        op("dve", lambda e, s: e.scalar_tensor_tensor(out=out, in0=in0, scalar=scalar, in1=in1, op0=op0, op1=op1), reads, writes)

    def tt_(out, in0, in1, o, reads, writes):
        op("dve", lambda e, s: e.tensor_tensor(out=out, in0=in0, in1=in1, op=o), reads, writes)

    cp_ctr = [0]

    def evac(out, in_, reads, writes):
        cp_ctr[0] += 1
        if cp_ctr[0] % 2:
            act(out, in_, AF.Copy, reads, writes)
        else:
            op("dve", lambda e, s: e.tensor_copy(out=out, in_=in_), reads, writes)

    wplan = []
    wstate = {"i": 0, "loaded": 0, "released": 0}

    def issue_load(j):
        dram = wplan[j]
        slot = j % NSLOT
        n = dram.shape[-1]

        def fn(e, s):
            return e.dma_start(out=ring[slot][:, 0:n], in_=dram, max_dma_last_dim=4096).then_inc(s[("dma", "w%d" % slot)], 16)
        S.op("pool", fn, writes=["W%d" % slot], dma="w%d" % slot)

    def wget(dram, shape, keep=0):
        i = wstate["i"]
        wstate["i"] += 1
        assert keep < NSLOT
        wstate["released"] = max(wstate["released"], i - keep)
        if S.dry:
            wplan.append(dram)
        else:
            while wstate["loaded"] < min(len(wplan), wstate["released"] + NSLOT):
                issue_load(wstate["loaded"])
                wstate["loaded"] += 1
            assert wstate["loaded"] > i
        slot = i % NSLOT
        n = int(np.prod(shape))
        ap = ring[slot][:, 0:n]
        if len(shape) == 2:
            ap = ap.rearrange("p (a b) -> p a b", a=shape[0])
        return ap, "W%d" % slot

    def tsl(tt):
        return slice(tt * TT, (tt + 1) * TT)

    def rmsnorm(srcs, src_keys, gcol, nfeat, ntok, outs, out_keys):
        flush()
        for st_ in rmsnorm_steps(srcs, src_keys, gcol, nfeat, ntok, outs, out_keys):
            st_()

    def rmsnorm_steps(srcs, src_keys, gcol, nfeat, ntok, outs, out_keys):
        nk = len(srcs)
        steps = []
        for k in range(nk):
            def sq_step(k=k):
                b = k % 2
                act(sq_sb[:, b, 0:ntok], srcs[k], AF.Square, [src_keys[k]], ["sq%d" % b])
                mm1(P[7][:, 0:ntok], ones, sq_sb[:, b, 0:ntok], k == 0, k == nk - 1, ["sq%d" % b, "ones"], ["P7"])
            steps.append(sq_step)
        steps.append(lambda: act(lnv[:, 0:ntok], P[7][:, 0:ntok], AF.Ln, ["P7"], ["lnv"], scale=1.0 / nfeat, bias=EPS))
        steps.append(lambda: act(rstd[:, 0:ntok], lnv[:, 0:ntok], AF.Exp, ["lnv"], ["rstd"], scale=-0.5))
        for k in range(nk):
            def mul_step(k=k):
                stt(outs[k], srcs[k], cvec[:, gcol + k:gcol + k + 1], rstd[:, 0:ntok], ALU.mult, ALU.mult,
                    [src_keys[k], "rstd", "cvec"], [out_keys[k]])
            steps.append(mul_step)
        return steps

    def norm_tt(gcol, tt, front=False):
        steps = rmsnorm_steps([x_sb[:, k, tsl(tt)] for k in range(8)], ["x%d_%d" % (k, tt) for k in range(8)], gcol, D, TT,
                              [h_sb[:, k, tsl(tt)] for k in range(8)], ["h%d_%d" % (k, tt) for k in range(8)])
        if front:
            bg.extendleft(reversed([(tt, st_) for st_ in steps]))
        else:
            bg.extend([(tt, st_) for st_ in steps])

    def hkeys(tt):
        flush(tt)
        return ["h%d_%d" % (k, tt) for k in range(8)]

    def ffn(fi, after_tt):
        switch()
        cnt = 0
        for grp in range(2):
            j0 = 0 if grp == 0 else G0
            ng = G0 if grp == 0 else NHC - G0
            def up(t, jj, tt, wt, wk):
                nonlocal cnt
                jl = 2 * t + jj - j0
                b = cnt % 2
                cnt += 1
                pg, pu = P[2 * b], P[2 * b + 1]
                mm(pg[:], [(wt[:, k, jj * 256:jj * 256 + 128], h_sb[:, k, tsl(tt)]) for k in range(8)],
                   [wk] + hkeys(tt), ["P%d" % (2 * b)])
                mm(pu[:], [(wt[:, k, jj * 256 + 128:jj * 256 + 256], h_sb[:, k, tsl(tt)]) for k in range(8)],
                   [wk] + hkeys(tt), ["P%d" % (2 * b + 1)])
                act(sg[b], pg[:], AF.Silu, ["P%d" % (2 * b)], A(["sg%d" % b]))
                tt_(hid[:, jl, tsl(tt)], sg[b], pu[:], ALU.mult, A(["sg%d" % b]) + ["P%d" % (2 * b + 1)],
                    A(["hid%d_%d" % (jl, tt)]))
            tiles_g = range(j0 // 2, (j0 + ng) // 2)
            tiles_g = list(tiles_g)
            if grp == 0:
                wa = wget(wf_i[fi][tiles_g[0]], [8, 512])
                wb = wget(wf_i[fi][tiles_g[1]], [8, 512], keep=1)
                for tt in range(NTT):
                    for t, (wt, wk) in ((tiles_g[0], wa), (tiles_g[1], wb)):
                        for jj in range(2):
                            up(t, jj, tt, wt, wk)
                tiles_g = tiles_g[2:]
            for t in tiles_g:
                wt, wk = wget(wf_i[fi][t], [8, 512])
                for jj in range(2):
                    for tt in range(NTT):
                        up(t, jj, tt, wt, wk)
            wd = wf_da[fi] if grp == 0 else wf_db[fi]

            def down(m, tt, wt, wk):
                nonlocal cnt
                b = 4 + (cnt % 2)
                cnt += 1
                mm(P[b][:], [(wt[:, j, :], hid[:, j, tsl(tt)]) for j in range(ng)],
                   [wk] + A(["hid%d_%d" % (j, tt) for j in range(ng)]), ["P%d" % b])
                stt(x_sb[:, m, tsl(tt)], P[b][:], 0.5, x_sb[:, m, tsl(tt)], ALU.mult, ALU.add,
                    ["P%d" % b, "x%d_%d" % (m, tt)], ["x%d_%d" % (m, tt)])
            if grp == 0:
                for m in range(8):
                    wt, wk = wget(wd[m], [ng, 128])
                    for tt in range(NTT):
                        down(m, tt, wt, wk)
            else:
                for tt in range(NTT):
                    for m in range(8):
                        wt, wk = wget(wd[m], [ng, 128])
                        down(m, tt, wt, wk)
                    after_tt(tt)

    def mem_kv(s):
        switch()

        def ld(e, sm):
            return e.dma_start(out=memx, in_=memT.rearrange("(k p) t -> p k t", p=128)[:, :, s * NMEM:(s + 1) * NMEM]).then_inc(sm[("dma", "mem")], 16)
        S.op("sp", ld, reads=["GUARD"], writes=["@memx", "GUARD"], dma="mem")
        ctx["pending_switch"] = False
        rmsnorm([memx[:, k, :] for k in range(8)], A(["memx"] * 8), C_MEM, D, NMEM,
                [memn[:, k, :] for k in range(8)], A(["memn%d" % k for k in range(8)]))
        mk = A(["memn%d" % k for k in range(8)])
        c2 = 0
        for t in range(2):
            wt, wk = wget(wmkv_d[t], [8, 512])
            for c in range(4):
                b = c2 % 2
                c2 += 1
                mm(P[b][:, 0:NMEM], [(wt[:, k, c * 128:(c + 1) * 128], memn[:, k, :]) for k in range(8)], [wk] + mk, ["P%d" % b])
                evac(km_sb[:, t * 4 + c, :], P[b][:, 0:NMEM], ["P%d" % b], ["km%d" % (t * 4 + c)])
        for t in range(2):
            wt, wk = wget(wmkv_d[2 + t], [8, 512])
            for mc in range(2):
                b = c2 % 2
                c2 += 1
                mm(P[b][:], [(memn[:, k, mc * 128:(mc + 1) * 128], wt[:, k, :]) for k in range(8)], [wk] + mk, ["P%d" % b])
                evac(vm_sb[:, mc, t * 512:(t + 1) * 512], P[b][:], ["P%d" % b], ["vm%d_%d" % (mc, t)])

    def branch_a(half):
        switch()
        op("dve", lambda e, s: e.memset(qrz.rearrange("p a b -> p (a b)"), 0.0), [], A(["qrz0", "qrz1", "qrz2", "qrz3"]))
        for tt in range(NTT):
            gt = half * NTT + tt
            tok = slice(gt * TT, (gt + 1) * TT)
            hk = hkeys(tt)

            mark("A1")

            def ldr(e, sm, gt=gt):
                return e.dma_start(out=rope_sb, in_=rope_d[:, :, gt * TT:(gt + 1) * TT]).then_inc(sm[("dma", "rope")], 16)
            S.op("sp", ldr, writes=["rope0", "rope1"], dma="rope")
            w1, w1k = wget(wA1_d, [8, 512])
            for c, b in ((0, 0), (1, 1), (2, 2), (3, 5)):
                mm(P[b][:], [(w1[:, k, c * 128:(c + 1) * 128], h_sb[:, k, tsl(tt)]) for k in range(8)], [w1k] + hk, ["P%d" % b])
            w2, w2k = wget(wA2_d, [8, 384])
            for c, b in ((0, 3), (1, 4), (2, 6)):
                mm(P[b][:], [(w2[:, k, c * 128:(c + 1) * 128], h_sb[:, k, tsl(tt)]) for k in range(8)], [w2k] + hk, ["P%d" % b])
            rmsnorm([P[k][:] for k in range(3)], ["P0", "P1", "P2"], C_QN, 384, TT,
                    [cqn[:, k, :] for k in range(3)], A(["cqn%d" % k for k in range(3)]))
            rmsnorm([P[3 + k][:] for k in range(2)], ["P3", "P4"], C_KVN, 256, TT,
                    [ckvn[:, k, :] for k in range(2)], A(["ckvn%d" % k for k in range(2)]))
            tt_(rt[0], P[5][:], rope_sb[:, 0, :], ALU.mult, ["P5", "rope0"], A(["rt0"]))
            tt_(rt[1], P[6][:], rope_sb[:, 1, :], ALU.mult, ["P6", "rope1"], A(["rt1"]))
            tt_(kr_sb[:, tok], rt[0], rt[1], ALU.add, A(["rt0", "rt1"]), ["kr%d" % gt])
            wkv, wkvk = wget(wukv_d, [2, 2048])
            ck = A(["ckvn0", "ckvn1"])
            c2 = 0
            for h in range(8):
                b = c2 % 6
                c2 += 1
                mm(P[b][:], [(wkv[:, k, h * 128:(h + 1) * 128], ckvn[:, k, :]) for k in range(2)], [wkvk] + ck, ["P%d" % b])
                evac(kn_sb[:, h, tok], P[b][:], ["P%d" % b], ["kn%d_%d" % (h, gt)])
            for tk in range(4):
                for hf in range(2):
                    b = c2 % 6
                    c2 += 1
                    mm(P[b][:], [(ckvn[:, k, tk * 128:(tk + 1) * 128], wkv[:, k, 1024 + hf * 512:1024 + (hf + 1) * 512]) for k in range(2)],
                       [wkvk] + ck, ["P%d" % b])
                    evac(v_sb[:, gt * 4 + tk, hf * 512:(hf + 1) * 512], P[b][:], ["P%d" % b], ["v%d_%d" % (gt * 4 + tk, hf)])
            mark("ATT")
            assert not bg
            wqn, wqnk = wget(wuqn_d, [3, 1024])
            wqr, wqrk = wget(wuqr_d, [3, 1024], keep=1)
            cq = A(["cqn0", "cqn1", "cqn2"])
            nkc = 4 * gt + 4

            def qproj(h):
                qb = h % 2
                hp = h // 2
                rb = hp % 2
                mm(P[6][:], [(wqn[:, k, h * 128:(h + 1) * 128], cqn[:, k, :]) for k in range(3)], [wqnk] + cq, ["P6"])
                evac(qn_sb[:, qb, :], P[6][:], ["P6"], A(["qn%d" % qb]))
                if h % 2 == 0:
                    mm(P[7][:], [(wqr[:, k, hp * 128:(hp + 1) * 128], cqn[:, k, :]) for k in range(3)], [wqrk] + cq, ["P7"])
                    mm(P[6][:], [(wqr[:, k, 512 + hp * 128:512 + (hp + 1) * 128], cqn[:, k, :]) for k in range(3)], [wqrk] + cq, ["P6"])
                    tt_(rt[0], P[7][:], rope_sb[:, 0, :], ALU.mult, ["P7", "rope0"], A(["rt0"]))
                    tt_(rt[1], P[6][:], rope_sb[:, 1, :], ALU.mult, ["P6", "rope1"], A(["rt1"]))
                    tt_(qrz[0:64, rb * 2, :], rt[0][0:64, :], rt[1][0:64, :], ALU.add, A(["rt0", "rt1"]), A(["qrz%d" % (rb * 2)]))
                    tt_(qrz[64:128, rb * 2 + 1, :], rt[0][64:128, :], rt[1][64:128, :], ALU.add, A(["rt0", "rt1"]), A(["qrz%d" % (rb * 2 + 1)]))

            qproj(0)
            for h in range(8):
                qb = h % 2
                rz = ((h // 2) % 2) * 2 + (h % 2)
                if h < 7:
                    qproj(h + 1)
                po, pd = P[2 + (h % 2)], P[4 + (h % 2)]
                pok, pdk = "P%d" % (2 + h % 2), "P%d" % (4 + h % 2)

                def qk(kc):
                    q0 = 0 if kc < 4 * gt else 128 * (kc - 4 * gt)
                    sb = kc % 2
                    ksl = slice(kc * 128, (kc + 1) * 128)
                    pairs = [(kn_sb[:, h, ksl], qn_sb[:, qb, q0:TT]), (kr_sb[:, ksl], qrz[:, rz, q0:TT])]
                    rd = ["kn%d_%d" % (h, kc // 4), "kr%d" % (kc // 4)] + A(["qn%d" % qb, "qrz%d" % rz])
                    if kc < 4 * gt:
                        mm(P[sb][:, q0:TT], pairs, rd, ["P%d" % sb])
                    else:
                        def fn(e, s_, pairs=pairs, sb=sb, q0=q0):
                            e.matmul(P[sb][:, q0:TT], lhsT=pairs[0][0], rhs=pairs[0][1], start=True, stop=False)
                            e.matmul(P[sb][:, q0:TT], lhsT=pairs[1][0], rhs=pairs[1][1], start=False, stop=False)
                            return e.matmul(P[sb][:, q0:q0 + 64], lhsT=maskL, rhs=onesr, start=False, stop=True)
                        if not S.dry:
                            PHASES["nmm"] += 3
                        op("pe", fn, rd + ["maskc"], ["P%d" % sb])
                    pb = kc % 3
                    act(pT[:, pb, q0:TT], P[sb][:, q0:TT], AF.Exp, ["P%d" % sb], A(["pT%d" % pb]), scale=ATT_SCALE)

                def pv(kc):
                    q0 = 0 if kc < 4 * gt else 128 * (kc - 4 * gt)
                    pb = kc % 3
                    mm1(po[:, q0:TT], v_sb[:, kc, h * 128:(h + 1) * 128], pT[:, pb, q0:TT], kc == 0, kc == nkc - 1,
                        ["v%d_%d" % (kc, h // 4)] + A(["pT%d" % pb]), [pok])
                    mm1(pd[:, q0:TT], ones, pT[:, pb, q0:TT], kc == 0, kc == nkc - 1, ["ones"] + A(["pT%d" % pb]), [pdk])

                for kc in range(nkc):
                    qk(kc)
                    if kc >= 1:
                        pv(kc - 1)
                pv(nkc - 1)
                act(lnv, pd[:], AF.Ln, [pdk], ["lnv"])
                act(rstd, lnv, AF.Exp, ["lnv"], ["rstd"], scale=-1.0)
                tt_(y_sb[:, h, tsl(tt)], po[:], rstd, ALU.mult, [pok, "rstd"], A(["y%d_%d" % (h, tt)]))

    def branch_b(half):
        switch()
        assert not bg
        RB, IB = (2, 2), (3, 7)
        GROT = (4, 5, 6)
        s_bufs = (rope_sb[:, 0, :], rope_sb[:, 1, :])
        s_keys = ("rope0", "rope1")

        def GBk(n, tt):
            return GROT[(2 * n + tt) % 3]
        a_bufs = (lnv, rstd)
        a_keys = ("lnv", "rstd")
        tiles = {}

        def tile_for(n):
            t = n // 2
            if t not in tiles:
                tiles[t] = wget(wrg_d[t], [8, 512], keep=1)
            return tiles[t]

        def stage1(n):
            wt, wk = tile_for(n)
            nn = n % 2
            par = n % 2
            par3 = n % 3
            xcol = slice(nn * 256, nn * 256 + 128)
            op("dve", lambda e, s: e.tensor_copy(out=zx[:, 0:3], in_=hist[:, n, 0:3]), ["hist%d" % n], A(["zxh"]))
            for tt in range(NTT):
                b = tt
                xk = "xc%d_%d" % (par3, tt)
                mm(P[b][:], [(wt[:, k, xcol], h_sb[:, k, tsl(tt)]) for k in range(8)], [wk] + hkeys(tt), ["P%d" % b])
                act(zx[:, 3 + tt * TT:3 + (tt + 1) * TT], P[b][:], AF.Copy, ["P%d" % b], A(["zx%d" % tt]))
                act(xcs[:, par3 * 2 + tt, :], P[b][:], AF.Identity, ["P%d" % b, "cvec"], A([xk]),
                    scale=cvec[:, C_CW + 24 + n:C_CW + 25 + n], bias=cvec[:, C_CB + n:C_CB + n + 1])
            op("dve", lambda e, s: e.tensor_copy(out=hist[:, n, 0:3], in_=zx[:, ST:ST + 3]), A(["zx1"]), ["hist%d" % n])
            for tt in range(NTT):
                zk = A(["zxh", "zx0"] if tt == 0 else ["zx0", "zx1"])
                xk = "xc%d_%d" % (par3, tt)
                xc_ = xcs[:, par3 * 2 + tt, :]
                o0 = tt * TT
                for w in range(3):
                    stt(xc_, zx[:, o0 + w:o0 + w + TT], cvec[:, C_CW + 8 * w + n:C_CW + 8 * w + n + 1], xc_, ALU.mult, ALU.add,
                        zk + A([xk]) + ["cvec"], A([xk]))
                xb_ = xcbs[:, par * 2 + tt, :]
                op("pool", lambda e, s, xb_=xb_, xc_=xc_: e.tensor_copy(out=xb_, in_=xc_), A([xk]), A(["xcb%d_%d" % (par, tt)]))

        def stage2a(n):
            wt, wk = tile_for(n)
            nn = n % 2
            par = n % 2
            par3 = n % 3
            gcol = slice(nn * 256 + 128, nn * 256 + 256)
            def gate(which, tt):
                xb_ = xcbs[:, par * 2 + tt, :]
                xbk = A(["xcb%d_%d" % (par, tt)])
                bank = RB[tt] if which == 0 else IB[tt]
                mm1(P[bank][:], wrgai[:, which, n, :], xb_, True, True, ["wrgai"] + xbk, ["P%d" % bank])
            def tanh_gate(which, tt):
                if which == 0:
                    act(s_bufs[tt], P[RB[tt]][:], AF.Tanh, ["P%d" % RB[tt], "dvec"], [s_keys[tt]], scale=0.5, bias=dvec[:, 40 + n:41 + n])
                else:
                    ik = "P%d" % IB[tt]
                    act(P[IB[tt]][:], P[IB[tt]][:], AF.Tanh, [ik, "dvec"], [ik], scale=0.5, bias=dvec[:, 48 + n:49 + n])
            gate(0, 0)
            gate(1, 0)
            gate(1, 1)
            tanh_gate(0, 0)
            gate(0, 1)
            GB = (GBk(n, 0), GBk(n, 1))
            for tt in range(NTT):
                mm(P[GB[tt]][:], [(wt[:, k, gcol], h_sb[:, k, tsl(tt)]) for k in range(8)], [wk] + hkeys(tt), ["P%d" % GB[tt]])
            tanh_gate(1, 0)
            tanh_gate(1, 1)
            tanh_gate(0, 1)
            for tt in range(NTT):
                gk = "P%d" % GB[tt]
                act(P[GB[tt]][:], P[GB[tt]][:], AF.Gelu_apprx_tanh, [gk], [gk])
            return GB

        def stage2b(n, GB):
            par3 = n % 3
            for tt in range(NTT):
                act(a_bufs[tt], s_bufs[tt], AF.Exp, [s_keys[tt], "dvec"], [a_keys[tt]],
                    scale=dvec[:, 32 + n:33 + n], bias=dvec[:, 32 + n:33 + n])
            for tt in range(NTT):
                act(s_bufs[tt], a_bufs[tt], AF.Square, [a_keys[tt]], [s_keys[tt]])
            for tt in range(NTT):
                act(s_bufs[tt], s_bufs[tt], AF.Sqrt, [s_keys[tt]], [s_keys[tt]], scale=-1.0, bias=1.0)
            for tt in range(NTT):
                xk = A(["xc%d_%d" % (par3, tt)])
                xc_ = xcs[:, par3 * 2 + tt, :]
                stt(xc_, P[IB[tt]][:], 1.0, xc_, ALU.add, ALU.mult, ["P%d" % IB[tt]] + xk, xk)
            for tt in range(NTT):
                xk = A(["xc%d_%d" % (par3, tt)])
                xc_ = xcs[:, par3 * 2 + tt, :]
                stt(xc_, xc_, 0.5, s_bufs[tt], ALU.mult, ALU.mult, [s_keys[tt]] + xk, xk)
            for tt in range(NTT):
                xk = A(["xc%d_%d" % (par3, tt)])
                xc_ = xcs[:, par3 * 2 + tt, :]
                ab = a_bufs[tt]
                op("dve", lambda e, s, xc_=xc_, ab=ab: e.tensor_tensor_scan(out=xc_, data0=ab, data1=xc_, initial=state[:, n:n + 1],
                                                                            op0=ALU.mult, op1=ALU.add),
                   [a_keys[tt], "state%d" % n] + xk, xk)
                op("dve", lambda e, s, xc_=xc_: e.tensor_copy(out=state[:, n:n + 1], in_=xc_[:, TT - 1:TT]), xk, ["state%d" % n])
            for tt in range(NTT):
                xk = A(["xc%d_%d" % (par3, tt)])
                xc_ = xcs[:, par3 * 2 + tt, :]
                tt_(y_sb[:, n, tsl(tt)], P[GB[tt]][:], xc_, ALU.mult, ["P%d" % GB[tt]] + xk, A(["y%d_%d" % (n, tt)]))

        stage1(0)
        stage1(1)
        for n in range(8):
            GBn = stage2a(n)
            if n + 2 < 8:
                stage1(n + 2)
            stage2b(n, GBn)

    def branch_c(half):
        switch()
        assert not bg
        iters = [(t, hh, tt) for t in range(2) for hh in range(2) for tt in range(NTT)]
        tiles = {}

        def qproj(i):
            t, hh, tt = iters[i]
            if t not in tiles:
                tiles[t] = wget(wmq_d[t], [8, 512])
            wt, wk = tiles[t]
            par = i % 2
            for dc in range(2):
                b = dc
                col = slice(hh * 256 + dc * 128, hh * 256 + (dc + 1) * 128)
                mm(P[b][:], [(wt[:, k, col], h_sb[:, k, tsl(tt)]) for k in range(8)], [wk] + hkeys(tt), ["P%d" % b])
                evac(qm_sb[:, par * 2 + dc, :], P[b][:], ["P%d" % b], A(["qm%d" % (par * 2 + dc)]))

        def rest(i):
            t, hh, tt = iters[i]
            h = 2 * t + hh
            par = i % 2
            qk_ = A(["qm%d" % (par * 2), "qm%d" % (par * 2 + 1)])
            for mc in range(2):
                b = 2 + mc
                mm(P[b][:], [(km_sb[:, h * 2 + dc, mc * 128:(mc + 1) * 128], qm_sb[:, par * 2 + dc, :]) for dc in range(2)],
                   ["km%d" % (h * 2), "km%d" % (h * 2 + 1)] + qk_, ["P%d" % b])
                act(pm_sb[:, par * 2 + mc, :], P[b][:], AF.Exp, ["P%d" % b], A(["pm%d" % (par * 2 + mc)]), scale=MEM_SCALE)
            pmk = A(["pm%d" % (par * 2), "pm%d" % (par * 2 + 1)])
            db = 6 + par
            mm(P[db][:], [(ones, pm_sb[:, par * 2 + mc, :]) for mc in range(2)], ["ones"] + pmk, ["P%d" % db])
            for dc in range(2):
                b = 4 + dc
                col = slice(h * 256 + dc * 128, h * 256 + (dc + 1) * 128)
                mm(P[b][:], [(vm_sb[:, mc, col], pm_sb[:, par * 2 + mc, :]) for mc in range(2)],
                   ["vm%d_%d" % (mc, h // 2) for mc in range(2)] + pmk, ["P%d" % b])
            act(lnv, P[db][:], AF.Ln, ["P%d" % db], ["lnv"])
            act(rstd, lnv, AF.Exp, ["lnv"], ["rstd"], scale=-1.0)
            for dc in range(2):
                b = 4 + dc
                tt_(y_sb[:, h * 2 + dc, tsl(tt)], P[b][:], rstd, ALU.mult, ["P%d" % b, "rstd"], A(["y%d_%d" % (h * 2 + dc, tt)]))

        qproj(0)
        for i in range(len(iters)):
            if i + 1 < len(iters):
                qproj(i + 1)
            rest(i)

    def merge(br, after_tt=None):
        switch()
        cnt = 0
        for mp in range(4):
            wt, wk = wget(wmg_d[br * 4 + mp], [8, 512])
            for mi in range(2):
                m = 2 * mp + mi
                for tt in range(NTT):
                    b = cnt % 2
                    cnt += 1
                    pg, pp = P[2 * b], P[2 * b + 1]
                    mm(pg[:], [(wt[:, k, mi * 256:mi * 256 + 128], h_sb[:, k, tsl(tt)]) for k in range(8)], [wk] + hkeys(tt), ["P%d" % (2 * b)])
                    mm(pp[:], [(wt[:, k, mi * 256 + 128:mi * 256 + 256], y_sb[:, k, tsl(tt)]) for k in range(8)],
                       [wk] + A(["y%d_%d" % (k, tt) for k in range(8)]), ["P%d" % (2 * b + 1)])
                    act(sgm[b], pg[:], AF.Sigmoid, ["P%d" % (2 * b), "cvec"], A(["sgm%d" % b]),
                        bias=cvec[:, C_BG + br * 8 + m:C_BG + br * 8 + m + 1])
                    tt_(merged[:, m, tsl(tt)], sgm[b], pp[:], ALU.mult, A(["sgm%d" % b]) + ["P%d" % (2 * b + 1)], A(["mg%d_%d" % (m, tt)]))
        w0 = wget(wout_d[0], [8, 512])
        w1 = wget(wout_d[1], [8, 512], keep=1)
        for tt in range(NTT):
            for m2 in range(8):
                wt, wk = w0 if m2 < 4 else w1
                mi = m2 % 4
                b = 4 + cnt % 2
                cnt += 1
                mm(P[b][:], [(wt[:, m, mi * 128:(mi + 1) * 128], merged[:, m, tsl(tt)]) for m in range(8)],
                   [wk] + A(["mg%d_%d" % (m, tt) for m in range(8)]), ["P%d" % b])
                tt_(x_sb[:, m2, tsl(tt)], P[b][:], x_sb[:, m2, tsl(tt)], ALU.add, ["P%d" % b, "x%d_%d" % (m2, tt)], ["x%d_%d" % (m2, tt)])
            if after_tt is not None:
                after_tt(tt)

    oc = [0]

    def final_out_tt(s, half, tt, nxt):
        srcs = [x_sb[:, k, tsl(tt)] for k in range(8)]
        keys = ["x%d_%d" % (k, tt) for k in range(8)]
        for k in range(8):
            def sq_step(k=k):
                b = k % 2
                act(sq_sb[:, b, :], srcs[k], AF.Square, [keys[k]], ["sq%d" % b])
                mm1(P[7][:], ones, sq_sb[:, b, :], k == 0, k == 7, ["sq%d" % b, "ones"], ["P7"])
            bg.append((None, sq_step))
        bg.append((None, lambda: act(lnv, P[7][:], AF.Ln, ["P7"], ["lnv"], scale=1.0 / D, bias=EPS)))
        bg.append((None, lambda: act(rstd, lnv, AF.Exp, ["lnv"], ["rstd"], scale=-0.5)))
        c0 = s * SEQ + half * ST + tt * TT
        for k in range(8):
            def out_step(k=k):
                b = oc[0] % 2
                oc[0] += 1
                stt(ost[b], srcs[k], cvec[:, C_FIN + k:C_FIN + k + 1], rstd, ALU.mult, ALU.mult, [keys[k], "rstd", "cvec"], A(["ost%d" % b]))

                def st(e, sm, b=b, k=k, c0=c0):
                    return e.dma_start(out=outT[k * 128:(k + 1) * 128, c0:c0 + TT], in_=ost[b]).then_inc(sm[("dma", "o%d" % b)], 16)
                S.op("sp", st, reads=["@ost%d" % b, "GUARD"], dma="o%d" % b)
                if nxt is not None:
                    load_x_one(nxt[0], nxt[1], tt, k)
            bg.append((None, out_step))

    def load_x_one(s, half, tt, k):
        c0 = s * SEQ + half * ST + tt * TT

        def ld(e, sm):
            return e.dma_start(out=x_sb[:, k, tsl(tt)], in_=xT[k * 128:(k + 1) * 128, c0:c0 + TT]).then_inc(sm[("dma", "x%d_%d" % (k, tt))], 16)
        S.op("sp", ld, writes=["x%d_%d" % (k, tt)], dma="x%d_%d" % (k, tt))

    def load_x_tt(s, half, tt):
        for k in range(8):
            load_x_one(s, half, tt, k)

    def prologue():
        def ldc(e, sm):
            return e.dma_start(out=cvec, in_=cvec_d).then_inc(sm[("dma", "c")], 16)
        S.op("sp", ldc, writes=["cvec"], dma="c")

        def ldw(e, sm):
            return e.dma_start(out=wrgai.rearrange("p a b c -> p (a b c)"), in_=wrgai_d, max_dma_last_dim=4096).then_inc(sm[("dma", "wc")], 16)
        S.op("pool", ldw, writes=["wrgai"], dma="wc")
        op("dve", lambda e, s: e.memset(ones, 1.0), [], ["ones"])
        op("dve", lambda e, s: e.memset(maskL, 0.0), [], ["maskc"])
        op("dve", lambda e, s: e.memset(maskL[0:1, 64:128], -30000.0), ["maskc"], ["maskc"])
        op("dve", lambda e, s: e.memset(onesr, 0.0), ["maskc"], ["maskc"])
        op("dve", lambda e, s: e.memset(onesr[0:1, :], 1.0), ["maskc"], ["maskc"])
        yv, pv_ = dvec[:, 16:24], dvec[:, 24:32]
        act(yv, cvec[:, C_LAM:C_LAM + 8], AF.Exp, ["cvec"], ["dv_y"], scale=-1.0)
        ts = lambda o, i, s1, s2, o0, o1, r, w: op("dve", lambda e, s: e.tensor_scalar(out=o, in0=i, scalar1=s1, scalar2=s2, op0=o0, op1=o1), r, w)
        ts(pv_, yv, -0.25, 1.0 / 3.0, ALU.mult, ALU.add, ["dv_y"], ["dv_p"])
        tt_(pv_, pv_, yv, ALU.mult, ["dv_p", "dv_y"], ["dv_p"])
        ts(pv_, pv_, -1.0, 0.5, ALU.mult, ALU.add, ["dv_p"], ["dv_p"])
        tt_(pv_, pv_, yv, ALU.mult, ["dv_p", "dv_y"], ["dv_p"])
        ts(pv_, pv_, -1.0, 1.0, ALU.mult, ALU.add, ["dv_p"], ["dv_p"])
        tt_(pv_, pv_, yv, ALU.mult, ["dv_p", "dv_y"], ["dv_p"])
        ts(dvec[:, 0:8], pv_, -8.0, 0.0, ALU.mult, ALU.add, ["dv_p"], ["dvec"])
        ts(dvec[:, 8:16], pv_, -16.0, 0.0, ALU.mult, ALU.add, ["dv_p"], ["dvec"])
        ts(dvec[:, 32:40], pv_, -4.0, 0.0, ALU.mult, ALU.add, ["dv_p"], ["dvec"])
        ts(dvec[:, 40:48], cvec[:, C_BA:C_BA + 8], 0.5, 0.0, ALU.mult, ALU.add, ["cvec", "dvec"], ["dvec"])
        ts(dvec[:, 48:56], cvec[:, C_BI:C_BI + 8], 0.5, 0.0, ALU.mult, ALU.add, ["cvec", "dvec"], ["dvec"])

    def body():
        wstate["i"] = 0
        wstate["released"] = 0
        oc[0] = 0
        prologue()
        sts = [(s, half) for s in range(SPC) for half in range(NST)]
        for tt in range(NTT):
            load_x_tt(0, 0, tt)
        for idx, (s, half) in enumerate(sts):
            if half == 0:
                flush()
                mark("memkv")
                mem_kv(s)
                op("dve", lambda e, sm: e.memset(state, 0.0), ["state%d" % n for n in range(8)], ["state%d" % n for n in range(8)])
                op("dve", lambda e, sm: e.memset(hist.rearrange("p a b -> p (a b)"), 0.0), [], ["hist%d" % n for n in range(8)])
            mark("ffn1")
            norm_tt(C_FFN1, 0, front=True)
            norm_tt(C_FFN1, 1)
            ffn(0, lambda tt: norm_tt(C_MIX, tt))
            mark("A")
            branch_a(half)
            mark("mergeA")
            merge(0)
            mark("B")
            branch_b(half)
            mark("mergeB")
            merge(1)
            mark("C")
            branch_c(half)
            mark("mergeC")
            merge(2, lambda tt: norm_tt(C_FFN2, tt))
            mark("ffn2")

            def tail(tt, s=s, half=half, idx=idx):
                flush()
                final_out_tt(s, half, tt, sts[idx + 1] if idx + 1 < len(sts) else None)
            ffn(1, tail)
        flush()
        mark("end")

    S.dry = True
    body()
    S.dry = False
    ctx["pending_switch"] = False
    cp_ctr[0] = 0
    PHASES["nmm"] = 0
    PHASES["marks"] = []
    body()
    assert wstate["i"] == len(wplan)
    S.check_progress()
    S.emit(nc)
    import sys
    print("[mk] ops per engine", S.count, "dma", {k: v // 16 for k, v in S.dma_cnt.items()}, "wtiles", len(wplan), file=sys.stderr)
    return nc


def _tile(W):
    K, N = W.shape
    kc = K // 128
    return np.ascontiguousarray(W.reshape(kc, 128, N).transpose(1, 0, 2).reshape(128, kc * N))


def _pcol(v):
    return np.ascontiguousarray(v.reshape(-1, 128).T)


def _prep_shared(inp):
    f = lambda k: np.asarray(inp[k], dtype=np.float32)
    sh = {}
    r128 = lambda a: np.arange(a * 128, (a + 1) * 128)
    for name, wi, wd in (("wf1", "ffn1_w_in", "ffn1_w_down"), ("wf2", "ffn2_w_in", "ffn2_w_down")):
        Wi = f(wi)[0]
        Wd = f(wd)[0]
        tiles = []
        for t in range(11):
            cols = np.concatenate([r128(2 * t), HID + r128(2 * t), r128(2 * t + 1), HID + r128(2 * t + 1)])
            tiles.append(_tile(Wi[:, cols]))
        sh[name + "i"] = np.stack(tiles)
        sh[name + "da"] = np.stack([_tile(Wd[0:G0 * 128, m * 128:(m + 1) * 128]) for m in range(8)])
        sh[name + "db"] = np.stack([_tile(Wd[G0 * 128:, m * 128:(m + 1) * 128]) for m in range(8)])
    Win = f("w_in")[0]
    o_cq, o_ckv, o_kr, o_x, o_g, o_mq, o_gate = 0, 384, 640, 704, 1728, 2752, 3776
    kr = np.arange(o_kr, o_kr + 64)
    krsw = np.concatenate([kr[32:], kr[:32]])
    sh["wA1"] = _tile(Win[:, np.concatenate([np.arange(o_cq, o_cq + 384), kr, kr])])
    sh["wA2"] = _tile(Win[:, np.concatenate([np.arange(o_ckv, o_ckv + 256), krsw, krsw])])
    sh["wrg"] = np.stack([_tile(Win[:, np.concatenate([o_x + r128(2 * t), o_g + r128(2 * t), o_x + r128(2 * t + 1), o_g + r128(2 * t + 1)])])
                          for t in range(4)])
    sh["wmq"] = np.stack([_tile(Win[:, o_mq + t * 512:o_mq + (t + 1) * 512]) for t in range(2)])
    Wbr = f("w_branch")[0]
    tiles = []
    for b in range(3):
        for mp in range(4):
            parts = []
            for mi in range(2):
                m = 2 * mp + mi
                parts.append(Win[:, o_gate + b * 1024 + m * 128:o_gate + b * 1024 + (m + 1) * 128])
                parts.append(Wbr[b][:, m * 128:(m + 1) * 128])
            tiles.append(_tile(np.concatenate(parts, axis=1)))
    sh["wmg"] = np.stack(tiles)
    Wo = f("w_out")[0]
    sh["wout"] = np.stack([_tile(Wo[:, t * 512:(t + 1) * 512]) for t in range(2)])
    Wuq = f("w_uq")[0]
    sh["wuqn"] = _tile(Wuq[:, np.concatenate([h * 192 + np.arange(128) for h in range(8)])])
    rope_c = np.concatenate([h * 192 + 128 + np.arange(64) for h in range(8)])
    rope_sw = np.concatenate([h * 192 + 128 + np.concatenate([np.arange(32, 64), np.arange(0, 32)]) for h in range(8)])
    sh["wuqr"] = _tile(Wuq[:, np.concatenate([rope_c, rope_sw])])
    Wukv = f("w_ukv")[0]
    sh["wukv"] = _tile(Wukv[:, np.concatenate([h * 256 + np.arange(128) for h in range(8)] + [h * 256 + 128 + np.arange(128) for h in range(8)])])
    Wm = f("w_mem_kv")[0]
    sh["wmkv"] = np.stack([_tile(Wm[:, t * 512:(t + 1) * 512]) for t in range(4)])
    rg = np.stack([f("w_rg_a")[0], f("w_rg_i")[0]])
    sh["wrgai"] = np.ascontiguousarray(rg.transpose(2, 0, 1, 3).reshape(128, 2048))
    cv = np.zeros((128, NCV), np.float32)
    cv[:, C_FFN1:C_FFN1 + 8] = _pcol(f("ffn1_norm")[0])
    cv[:, C_MIX:C_MIX + 8] = _pcol(f("mix_norm")[0])
    cv[:, C_FFN2:C_FFN2 + 8] = _pcol(f("ffn2_norm")[0])
    cv[:, C_FIN:C_FIN + 8] = _pcol(f("final_norm"))
    cv[:, C_MEM:C_MEM + 8] = _pcol(f("mem_norm")[0])
    cv[:, C_QN:C_QN + 3] = _pcol(f("q_norm")[0])
    cv[:, C_KVN:C_KVN + 2] = _pcol(f("kv_norm")[0])
    cw = f("conv_w")[0]
    for w in range(4):
        cv[:, C_CW + 8 * w:C_CW + 8 * w + 8] = _pcol(cw[w, 0])
    cv[:, C_CB:C_CB + 8] = _pcol(f("conv_b")[0])
    cv[:, C_BA:C_BA + 8] = f("b_rg_a")[0].T
    cv[:, C_BI:C_BI + 8] = f("b_rg_i")[0].T
    cv[:, C_LAM:C_LAM + 8] = _pcol(f("lru_lambda")[0])
    bg = f("b_gate")[0]
    for b in range(3):
        cv[:, C_BG + 8 * b:C_BG + 8 * b + 8] = _pcol(bg[b])
    sh["cvec"] = cv
    pos = np.arange(SEQ, dtype=np.float32)
    inv_freq = (1.0 / (np.float32(10000.0) ** (np.arange(0, 64, 2, dtype=np.float32) / np.float32(64.0)))).astype(np.float32)
    ang = (pos[:, None] * inv_freq[None, :]).astype(np.float32)
    cos, sin = np.cos(ang).astype(np.float32), np.sin(ang).astype(np.float32)
    rope = np.zeros((128, 2, SEQ), np.float32)
    for p in range(128):
        j = p % 64
        rope[p, 0] = cos[:, j % 32]
        rope[p, 1] = sin[:, j % 32] * (-1.0 if j < 32 else 1.0)
    sh["rope"] = rope
    return sh


_NC_CACHE = {}


def kernel(**inputs):
    x = np.asarray(inputs["x"], dtype=np.float32)
    mem = np.asarray(inputs["mem"], dtype=np.float32)
    sh = _prep_shared(inputs)
    if "nc" not in _NC_CACHE:
        _NC_CACHE["nc"] = build_program()
    nc = _NC_CACHE["nc"]
    in_maps = []
    for c in range(NCORES):
        d = dict(sh)
        d["xT"] = np.ascontiguousarray(x[c * SPC:(c + 1) * SPC].transpose(2, 0, 1).reshape(D, SPC * SEQ))
        d["memT"] = np.ascontiguousarray(mem[c * SPC:(c + 1) * SPC].transpose(2, 0, 1).reshape(D, SPC * NMEM))
        in_maps.append(d)
    res = run_bass_kernel_spmd(nc, in_maps, core_ids=list(range(NCORES)))
    out = np.empty((NB, SEQ, D), np.float32)
    for c in range(NCORES):
        o = np.asarray(res.results[c]["outT"]).reshape(D, SPC, SEQ)
        out[c * SPC:(c + 1) * SPC] = o.transpose(1, 2, 0)
    return out
```

```python
import numpy as np
import concourse.bass as bass
import concourse.mybir as mybir
from concourse.bass_utils import run_bass_kernel_spmd

F32 = mybir.dt.float32
BF16 = mybir.dt.bfloat16
AF = mybir.ActivationFunctionType
ALU = mybir.AluOpType


class Sched:
    ENGS = ("pe", "act", "dve", "pool", "sp")

    def __init__(self):
        self.ops = {e: [] for e in self.ENGS}
        self.count = {e: 0 for e in self.ENGS}
        self.last_writer = {}
        self.readers = {}
        self.waited = {e: {} for e in self.ENGS}
        self.dma_cnt = {}
        self.dry = False

    def op(self, eng, fn, reads=(), writes=(), dma=None, ndma=1):
        if self.dry:
            return None
        deps = {}
        for r in reads:
            t = self.last_writer.get(r)
            if t is not None:
                deps[t[0]] = max(deps.get(t[0], 0), t[1])
        for w in writes:
            t = self.last_writer.get(w)
            if t is not None:
                deps[t[0]] = max(deps.get(t[0], 0), t[1])
            for t in self.readers.get(w, ()):
                deps[t[0]] = max(deps.get(t[0], 0), t[1])
        if dma is None:
            self.count[eng] += 1
            tok = (("eng", eng), self.count[eng])
        else:
            self.dma_cnt[dma] = self.dma_cnt.get(dma, 0) + 16 * ndma
            tok = (("dma", dma), self.dma_cnt[dma])
        waits = []
        wd = self.waited[eng]
        for sk, v in deps.items():
            if wd.get(sk, 0) >= v:
                continue
            wd[sk] = v
            waits.append((sk, v))
        self.ops[eng].append((fn, waits, dma is None, tok))
        for r in reads:
            self.readers.setdefault(r, []).append(tok)
        for w in writes:
            self.last_writer[w] = tok
            self.readers[w] = []
        return tok

    def check_progress(self):
        pos = {e: 0 for e in self.ENGS}
        sem = {}
        while True:
            progress = False
            for e in self.ENGS:
                q = self.ops[e]
                while pos[e] < len(q):
                    fn, waits, inc, tok = q[pos[e]]
                    if all(sem.get(sk, 0) >= v for sk, v in waits):
                        sem[tok[0]] = max(sem.get(tok[0], 0), tok[1])
                        pos[e] += 1
                        progress = True
                    else:
                        break
            if not progress:
                break
        stuck = {e: (pos[e], len(self.ops[e])) for e in self.ENGS if pos[e] < len(self.ops[e])}
        assert not stuck, "schedule deadlocks: %r" % stuck

    def emit(self, nc, final_waits=()):
        import contextlib
        dma_keys = sorted(self.dma_cnt.keys(), key=str)
        with contextlib.ExitStack() as es:
            sems = {}
            for e in self.ENGS:
                sems[("eng", e)] = es.enter_context(nc.semaphore("s_" + e))
            for k in dma_keys:
                sems[("dma", k)] = es.enter_context(nc.semaphore("d_" + str(k)))
            block = es.enter_context(nc.Block())
            ops = self.ops

            def run(engname, eng):
                mysem = sems[("eng", engname)]
                for fn, waits, inc, _tok in ops[engname]:
                    for sk, v in waits:
                        eng.wait_ge(sems[sk], v)
                    ins = fn(eng, sems)
                    if inc:
                        ins.then_inc(mysem, 1)

            @block.tensor
            def _(eng):
                run("pe", eng)

            @block.scalar
            def _(eng):
                run("act", eng)

            @block.vector
            def _(eng):
                run("dve", eng)

            @block.gpsimd
            def _(eng):
                run("pool", eng)

            @block.sync
            def _(eng):
                run("sp", eng)
                for k in dma_keys:
                    eng.wait_ge(sems[("dma", k)], self.dma_cnt[k])
                for e in ("pe", "act", "dve", "pool"):
                    if self.count[e]:
                        eng.wait_ge(sems[("eng", e)], self.count[e])


D = 1024
SEQ = 2048
NB = 16
NCORES = 8
SPC = NB // NCORES
ST = 1024
TT = 512
NTT = ST // TT
NST = SEQ // ST
HID = 2816
NHC = HID // 128
G0 = 12
NMEM = 256
EPS = 1e-6
ATT_SCALE = 192.0 ** -0.5
MEM_SCALE = 256.0 ** -0.5
NSLOT = 4
SLOT_ELEMS = 4096

C_FFN1, C_MIX, C_FFN2, C_FIN, C_MEM = 0, 8, 16, 24, 32
C_QN, C_KVN = 40, 43
C_CW, C_CB = 45, 77
C_BA, C_BI, C_LAM = 85, 93, 101
C_BG = 109
NCV = 136


PHASES = {"nmm": 0, "marks": []}


class Arena:
    def __init__(self, nc, name, nbytes):
        self.hb = nc.alloc_sbuf_tensor(name, [128, nbytes // 2], BF16)
        self.hf = self.hb.bitcast(F32)
        self.nbytes = nbytes
        self.off = 0

    def alloc(self, shape, dt, at=None):
        n = int(np.prod(shape))
        nb = n * (4 if dt == F32 else 2)
        off = self.off if at is None else at
        assert off % 32 == 0 and off + nb <= self.nbytes, (off, nb, self.nbytes)
        if dt == F32:
            ap = self.hf[:, off // 4: off // 4 + n]
        else:
            ap = self.hb[:, off // 2: off // 2 + n]
        if len(shape) == 2:
            ap = ap.rearrange("p (a b) -> p a b", a=shape[0])
        elif len(shape) == 3:
            ap = ap.rearrange("p (a b c) -> p a b c", a=shape[0], b=shape[1])
        if at is None:
            self.off = off + (nb + 31) // 32 * 32
        return ap


def build_program():
    nc = bass.Bass("TRN2", target_bir_lowering=False)

    def din(name, shape):
        return nc.dram_tensor(name, list(shape), F32, kind="ExternalInput").ap()

    xT = din("xT", [D, SPC * SEQ])
    memT = din("memT", [D, SPC * NMEM])
    cvec_d = din("cvec", [128, NCV])
    rope_d = din("rope", [128, 2, SEQ])
    wf_i = [din("wf1i", [11, 128, 4096]), din("wf2i", [11, 128, 4096])]
    wf_da = [din("wf1da", [8, 128, G0 * 128]), din("wf2da", [8, 128, G0 * 128])]
    wf_db = [din("wf1db", [8, 128, (NHC - G0) * 128]), din("wf2db", [8, 128, (NHC - G0) * 128])]
    wA1_d = din("wA1", [128, 4096])
    wA2_d = din("wA2", [128, 3072])
    wrg_d = din("wrg", [4, 128, 4096])
    wmq_d = din("wmq", [2, 128, 4096])
    wmg_d = din("wmg", [12, 128, 4096])
    wout_d = din("wout", [2, 128, 4096])
    wuqn_d = din("wuqn", [128, 3072])
    wuqr_d = din("wuqr", [128, 3072])
    wukv_d = din("wukv", [128, 4096])
    wmkv_d = din("wmkv", [4, 128, 4096])
    wrgai_d = din("wrgai", [128, 2048])
    outT = nc.dram_tensor("outT", [D, SPC * SEQ], F32, kind="ExternalOutput").ap()

    total = nc.sbuf_bytes_remaining
    main = Arena(nc, "main", (total - 256) // 64 * 64)
    cvec = main.alloc([1, NCV], F32)[:, 0, :]
    dvec = main.alloc([1, 56], F32)[:, 0, :]
    ones = main.alloc([1, 128], BF16)[:, 0, :]
    maskL = main.alloc([1, 128], BF16)[:, 0, :]
    onesr = main.alloc([1, 64], BF16)[:, 0, :]
    wrgai = main.alloc([2, 8, 128], BF16)
    state = main.alloc([1, 8], F32)[:, 0, :]
    hist = main.alloc([8, 4], F32)
    ring = [main.alloc([1, SLOT_ELEMS], BF16)[:, 0, :] for _ in range(NSLOT)]
    x_sb = main.alloc([8, ST], F32)
    h_sb = main.alloc([8, ST], BF16)
    kn_sb = main.alloc([8, SEQ], BF16)
    v_sb = main.alloc([16, 1024], BF16)
    kr_sb = main.alloc([1, SEQ], BF16)[:, 0, :]
    km_sb = main.alloc([8, NMEM], BF16)
    vm_sb = main.alloc([2, 1024], BF16)
    sq_sb = main.alloc([2, TT], BF16)
    lnv = main.alloc([1, TT], F32)[:, 0, :]
    rstd = main.alloc([1, TT], F32)[:, 0, :]
    rope_sb = main.alloc([2, TT], F32)
    A0 = main.off
    asz = main.nbytes - A0
    assert asz >= 36864, asz

    def aa(shape, dt, rel):
        return main.alloc(shape, dt, at=A0 + rel)

    hid = aa([G0, ST], BF16, 0)
    sg = [aa([1, TT], F32, 24576 + 2048 * i)[:, 0, :] for i in range(2)]
    ost = [aa([1, TT], F32, 28672 + 2048 * i)[:, 0, :] for i in range(2)]
    memx = aa([8, NMEM], F32, 0)
    memn = aa([8, NMEM], BF16, 8192)
    y_sb = aa([8, ST], BF16, 0)
    U = 16384
    cqn = aa([3, TT], BF16, U)
    ckvn = aa([2, TT], BF16, U + 3072)
    qn_sb = aa([2, TT], BF16, U + 5120)
    qrz = aa([4, TT], BF16, U + 7168)
    pT = aa([3, TT], BF16, U + 11264)
    rt = [aa([1, TT], F32, U + 14336 + 2048 * i)[:, 0, :] for i in range(2)]
    merged = aa([8, ST], BF16, U)
    sgm = [aa([1, TT], F32, U + 16384 + 2048 * i)[:, 0, :] for i in range(2)]
    zx = aa([1, 1032], F32, U)[:, 0, :]
    xcs = aa([6, TT], F32, U + 4128)
    xcbs = aa([4, TT], BF16, U + 16416)
    qm_sb = aa([4, TT], BF16, U)
    pm_sb = aa([4, TT], BF16, U + 4096)

    P = [nc.alloc_psum_tensor("ps%d" % i, [128, TT], F32) for i in range(8)]

    S = Sched()
    ctx = {"pending_switch": False}

    def A(keys):
        return ["@" + k for k in keys]

    def op(eng, fn, reads=(), writes=()):
        reads = list(reads)
        writes = list(writes)
        if any(k.startswith("@") for k in reads + writes):
            if ctx["pending_switch"]:
                writes.append("GUARD")
                ctx["pending_switch"] = False
            else:
                reads.append("GUARD")
        return S.op(eng, fn, reads, writes)

    def switch():
        ctx["pending_switch"] = True

    def mark(name):
        if not S.dry:
            PHASES["marks"].append((name, PHASES["nmm"]))

    import collections
    bg = collections.deque()

    def drain(n=3):
        for _ in range(n):
            if not bg:
                return
            bg.popleft()[1]()

    def flush(tt=None):
        if tt is None:
            while bg:
                bg.popleft()[1]()
            return
        last = -1
        for i, (t, _) in enumerate(bg):
            if t == tt:
                last = i
        for _ in range(last + 1):
            bg.popleft()[1]()

    def mm(out, pairs, reads, writes):
        n = len(pairs)
        if not S.dry:
            PHASES["nmm"] += n

        def fn(e, s):
            for i, (l, r) in enumerate(pairs):
                ins = e.matmul(out, lhsT=l, rhs=r, start=(i == 0), stop=(i == n - 1))
            return ins
        op("pe", fn, reads, writes)
        drain()

    def mm1(out, l, r, start, stop, reads, writes):
        if not S.dry:
            PHASES["nmm"] += 1
        op("pe", lambda e, s: e.matmul(out, lhsT=l, rhs=r, start=start, stop=stop), reads, writes)

    def act(out, in_, func, reads, writes, **kw):
        op("act", lambda e, s: e.activation(out=out, in_=in_, func=func, **kw), reads, writes)

    def stt(out, in0, scalar, in1, op0, op1, reads, writes):
        op("dve", lambda e, s: e.scalar_tensor_tensor(out=out, in0=in0, scalar=scalar, in1=in1, op0=op0, op1=op1), reads, writes)

    def tt_(out, in0, in1, o, reads, writes):
        op("dve", lambda e, s: e.tensor_tensor(out=out, in0=in0, in1=in1, op=o), reads, writes)

    cp_ctr = [0]

    def evac(out, in_, reads, writes):
        cp_ctr[0] += 1
        if cp_ctr[0] % 2:
            act(out, in_, AF.Copy, reads, writes)
        else:
            op("dve", lambda e, s: e.tensor_copy(out=out, in_=in_), reads, writes)

    wplan = []
    wstate = {"i": 0, "loaded": 0, "released": 0}

    def issue_load(j):
        dram = wplan[j]
        slot = j % NSLOT
        n = dram.shape[-1]

        def fn(e, s):
            return e.dma_start(out=ring[slot][:, 0:n], in_=dram, max_dma_last_dim=4096).then_inc(s[("dma", "w%d" % slot)], 16)
        S.op("pool", fn, writes=["W%d" % slot], dma="w%d" % slot)

    def wget(dram, shape, keep=0):
        i = wstate["i"]
        wstate["i"] += 1
        assert keep < NSLOT
        wstate["released"] = max(wstate["released"], i - keep)
        if S.dry:
            wplan.append(dram)
        else:
            while wstate["loaded"] < min(len(wplan), wstate["released"] + NSLOT):
                issue_load(wstate["loaded"])
                wstate["loaded"] += 1
            assert wstate["loaded"] > i
        slot = i % NSLOT
        n = int(np.prod(shape))
        ap = ring[slot][:, 0:n]
        if len(shape) == 2:
            ap = ap.rearrange("p (a b) -> p a b", a=shape[0])
        return ap, "W%d" % slot

    def tsl(tt):
        return slice(tt * TT, (tt + 1) * TT)

    def rmsnorm(srcs, src_keys, gcol, nfeat, ntok, outs, out_keys):
        flush()
        for st_ in rmsnorm_steps(srcs, src_keys, gcol, nfeat, ntok, outs, out_keys):
            st_()

    def rmsnorm_steps(srcs, src_keys, gcol, nfeat, ntok, outs, out_keys):
        nk = len(srcs)
        steps = []
        for k in range(nk):
            def sq_step(k=k):
                b = k % 2
                act(sq_sb[:, b, 0:ntok], srcs[k], AF.Square, [src_keys[k]], ["sq%d" % b])
                mm1(P[7][:, 0:ntok], ones, sq_sb[:, b, 0:ntok], k == 0, k == nk - 1, ["sq%d" % b, "ones"], ["P7"])
            steps.append(sq_step)
        steps.append(lambda: act(lnv[:, 0:ntok], P[7][:, 0:ntok], AF.Ln, ["P7"], ["lnv"], scale=1.0 / nfeat, bias=EPS))
        steps.append(lambda: act(rstd[:, 0:ntok], lnv[:, 0:ntok], AF.Exp, ["lnv"], ["rstd"], scale=-0.5))
        for k in range(nk):
            def mul_step(k=k):
                stt(outs[k], srcs[k], cvec[:, gcol + k:gcol + k + 1], rstd[:, 0:ntok], ALU.mult, ALU.mult,
                    [src_keys[k], "rstd", "cvec"], [out_keys[k]])
            steps.append(mul_step)
        return steps

    def norm_tt(gcol, tt, front=False):
        steps = rmsnorm_steps([x_sb[:, k, tsl(tt)] for k in range(8)], ["x%d_%d" % (k, tt) for k in range(8)], gcol, D, TT,
                              [h_sb[:, k, tsl(tt)] for k in range(8)], ["h%d_%d" % (k, tt) for k in range(8)])
        if front:
            bg.extendleft(reversed([(tt, st_) for st_ in steps]))
        else:
            bg.extend([(tt, st_) for st_ in steps])

    def hkeys(tt):
        flush(tt)
        return ["h%d_%d" % (k, tt) for k in range(8)]

    def ffn(fi, after_tt):
        switch()
        cnt = 0
        for grp in range(2):
            j0 = 0 if grp == 0 else G0
            ng = G0 if grp == 0 else NHC - G0
            def up(t, jj, tt, wt, wk):
                nonlocal cnt
                jl = 2 * t + jj - j0
                b = cnt % 2
                cnt += 1
                pg, pu = P[2 * b], P[2 * b + 1]
                mm(pg[:], [(wt[:, k, jj * 256:jj * 256 + 128], h_sb[:, k, tsl(tt)]) for k in range(8)],
                   [wk] + hkeys(tt), ["P%d" % (2 * b)])
                mm(pu[:], [(wt[:, k, jj * 256 + 128:jj * 256 + 256], h_sb[:, k, tsl(tt)]) for k in range(8)],
                   [wk] + hkeys(tt), ["P%d" % (2 * b + 1)])
                act(sg[b], pg[:], AF.Silu, ["P%d" % (2 * b)], A(["sg%d" % b]))
                tt_(hid[:, jl, tsl(tt)], sg[b], pu[:], ALU.mult, A(["sg%d" % b]) + ["P%d" % (2 * b + 1)],
                    A(["hid%d_%d" % (jl, tt)]))
            tiles_g = range(j0 // 2, (j0 + ng) // 2)
            tiles_g = list(tiles_g)
            if grp == 0:
                wa = wget(wf_i[fi][tiles_g[0]], [8, 512])
                wb = wget(wf_i[fi][tiles_g[1]], [8, 512], keep=1)
                for tt in range(NTT):
                    for t, (wt, wk) in ((tiles_g[0], wa), (tiles_g[1], wb)):
                        for jj in range(2):
                            up(t, jj, tt, wt, wk)
                tiles_g = tiles_g[2:]
            for t in tiles_g:
                wt, wk = wget(wf_i[fi][t], [8, 512])
                for jj in range(2):
                    for tt in range(NTT):
                        up(t, jj, tt, wt, wk)
            wd = wf_da[fi] if grp == 0 else wf_db[fi]

            def down(m, tt, wt, wk):
                nonlocal cnt
                b = 4 + (cnt % 2)
                cnt += 1
                mm(P[b][:], [(wt[:, j, :], hid[:, j, tsl(tt)]) for j in range(ng)],
                   [wk] + A(["hid%d_%d" % (j, tt) for j in range(ng)]), ["P%d" % b])
                stt(x_sb[:, m, tsl(tt)], P[b][:], 0.5, x_sb[:, m, tsl(tt)], ALU.mult, ALU.add,
                    ["P%d" % b, "x%d_%d" % (m, tt)], ["x%d_%d" % (m, tt)])
            if grp == 0:
                for m in range(8):
                    wt, wk = wget(wd[m], [ng, 128])
                    for tt in range(NTT):
                        down(m, tt, wt, wk)
            else:
                for tt in range(NTT):
                    for m in range(8):
                        wt, wk = wget(wd[m], [ng, 128])
                        down(m, tt, wt, wk)
                    after_tt(tt)

    def mem_kv(s):
        switch()

        def ld(e, sm):
            return e.dma_start(out=memx, in_=memT.rearrange("(k p) t -> p k t", p=128)[:, :, s * NMEM:(s + 1) * NMEM]).then_inc(sm[("dma", "mem")], 16)
        S.op("sp", ld, reads=["GUARD"], writes=["@memx", "GUARD"], dma="mem")
        ctx["pending_switch"] = False
        rmsnorm([memx[:, k, :] for k in range(8)], A(["memx"] * 8), C_MEM, D, NMEM,
                [memn[:, k, :] for k in range(8)], A(["memn%d" % k for k in range(8)]))
        mk = A(["memn%d" % k for k in range(8)])
        c2 = 0
        for t in range(2):
            wt, wk = wget(wmkv_d[t], [8, 512])
            for c in range(4):
                b = c2 % 2
                c2 += 1
                mm(P[b][:, 0:NMEM], [(wt[:, k, c * 128:(c + 1) * 128], memn[:, k, :]) for k in range(8)], [wk] + mk, ["P%d" % b])
                evac(km_sb[:, t * 4 + c, :], P[b][:, 0:NMEM], ["P%d" % b], ["km%d" % (t * 4 + c)])
        for t in range(2):
            wt, wk = wget(wmkv_d[2 + t], [8, 512])
            for mc in range(2):
                b = c2 % 2
                c2 += 1
                mm(P[b][:], [(memn[:, k, mc * 128:(mc + 1) * 128], wt[:, k, :]) for k in range(8)], [wk] + mk, ["P%d" % b])
                evac(vm_sb[:, mc, t * 512:(t + 1) * 512], P[b][:], ["P%d" % b], ["vm%d_%d" % (mc, t)])

    def branch_a(half):
        switch()
        op("dve", lambda e, s: e.memset(qrz.rearrange("p a b -> p (a b)"), 0.0), [], A(["qrz0", "qrz1", "qrz2", "qrz3"]))
        for tt in range(NTT):
            gt = half * NTT + tt
            tok = slice(gt * TT, (gt + 1) * TT)
            hk = hkeys(tt)

            mark("A1")

            def ldr(e, sm, gt=gt):
                return e.dma_start(out=rope_sb, in_=rope_d[:, :, gt * TT:(gt + 1) * TT]).then_inc(sm[("dma", "rope")], 16)
            S.op("sp", ldr, writes=["rope0", "rope1"], dma="rope")
            w1, w1k = wget(wA1_d, [8, 512])
            for c, b in ((0, 0), (1, 1), (2, 2), (3, 5)):
                mm(P[b][:], [(w1[:, k, c * 128:(c + 1) * 128], h_sb[:, k, tsl(tt)]) for k in range(8)], [w1k] + hk, ["P%d" % b])
            w2, w2k = wget(wA2_d, [8, 384])
            for c, b in ((0, 3), (1, 4), (2, 6)):
                mm(P[b][:], [(w2[:, k, c * 128:(c + 1) * 128], h_sb[:, k, tsl(tt)]) for k in range(8)], [w2k] + hk, ["P%d" % b])
            rmsnorm([P[k][:] for k in range(3)], ["P0", "P1", "P2"], C_QN, 384, TT,
                    [cqn[:, k, :] for k in range(3)], A(["cqn%d" % k for k in range(3)]))
            rmsnorm([P[3 + k][:] for k in range(2)], ["P3", "P4"], C_KVN, 256, TT,
                    [ckvn[:, k, :] for k in range(2)], A(["ckvn%d" % k for k in range(2)]))
            tt_(rt[0], P[5][:], rope_sb[:, 0, :], ALU.mult, ["P5", "rope0"], A(["rt0"]))
            tt_(rt[1], P[6][:], rope_sb[:, 1, :], ALU.mult, ["P6", "rope1"], A(["rt1"]))
            tt_(kr_sb[:, tok], rt[0], rt[1], ALU.add, A(["rt0", "rt1"]), ["kr%d" % gt])
            wkv, wkvk = wget(wukv_d, [2, 2048])
            ck = A(["ckvn0", "ckvn1"])
            c2 = 0
            for h in range(8):
                b = c2 % 6
                c2 += 1
                mm(P[b][:], [(wkv[:, k, h * 128:(h + 1) * 128], ckvn[:, k, :]) for k in range(2)], [wkvk] + ck, ["P%d" % b])
                evac(kn_sb[:, h, tok], P[b][:], ["P%d" % b], ["kn%d_%d" % (h, gt)])
            for tk in range(4):
                for hf in range(2):
                    b = c2 % 6
                    c2 += 1
                    mm(P[b][:], [(ckvn[:, k, tk * 128:(tk + 1) * 128], wkv[:, k, 1024 + hf * 512:1024 + (hf + 1) * 512]) for k in range(2)],
                       [wkvk] + ck, ["P%d" % b])
                    evac(v_sb[:, gt * 4 + tk, hf * 512:(hf + 1) * 512], P[b][:], ["P%d" % b], ["v%d_%d" % (gt * 4 + tk, hf)])
            mark("ATT")
            assert not bg
            wqn, wqnk = wget(wuqn_d, [3, 1024])
            wqr, wqrk = wget(wuqr_d, [3, 1024], keep=1)
            cq = A(["cqn0", "cqn1", "cqn2"])
            nkc = 4 * gt + 4

            def qproj(h):
                qb = h % 2
                hp = h // 2
                rb = hp % 2
                mm(P[6][:], [(wqn[:, k, h * 128:(h + 1) * 128], cqn[:, k, :]) for k in range(3)], [wqnk] + cq, ["P6"])
                evac(qn_sb[:, qb, :], P[6][:], ["P6"], A(["qn%d" % qb]))
                if h % 2 == 0:
                    mm(P[7][:], [(wqr[:, k, hp * 128:(hp + 1) * 128], cqn[:, k, :]) for k in range(3)], [wqrk] + cq, ["P7"])
                    mm(P[6][:], [(wqr[:, k, 512 + hp * 128:512 + (hp + 1) * 128], cqn[:, k, :]) for k in range(3)], [wqrk] + cq, ["P6"])
                    tt_(rt[0], P[7][:], rope_sb[:, 0, :], ALU.mult, ["P7", "rope0"], A(["rt0"]))
                    tt_(rt[1], P[6][:], rope_sb[:, 1, :], ALU.mult, ["P6", "rope1"], A(["rt1"]))
                    tt_(qrz[0:64, rb * 2, :], rt[0][0:64, :], rt[1][0:64, :], ALU.add, A(["rt0", "rt1"]), A(["qrz%d" % (rb * 2)]))
                    tt_(qrz[64:128, rb * 2 + 1, :], rt[0][64:128, :], rt[1][64:128, :], ALU.add, A(["rt0", "rt1"]), A(["qrz%d" % (rb * 2 + 1)]))

            qproj(0)
            for h in range(8):
                qb = h % 2
                rz = ((h // 2) % 2) * 2 + (h % 2)
                if h < 7:
                    qproj(h + 1)
                po, pd = P[2 + (h % 2)], P[4 + (h % 2)]
                pok, pdk = "P%d" % (2 + h % 2), "P%d" % (4 + h % 2)

                def qk(kc):
                    q0 = 0 if kc < 4 * gt else 128 * (kc - 4 * gt)
                    sb = kc % 2
                    ksl = slice(kc * 128, (kc + 1) * 128)
                    pairs = [(kn_sb[:, h, ksl], qn_sb[:, qb, q0:TT]), (kr_sb[:, ksl], qrz[:, rz, q0:TT])]
                    rd = ["kn%d_%d" % (h, kc // 4), "kr%d" % (kc // 4)] + A(["qn%d" % qb, "qrz%d" % rz])
                    if kc < 4 * gt:
                        mm(P[sb][:, q0:TT], pairs, rd, ["P%d" % sb])
                    else:
                        def fn(e, s_, pairs=pairs, sb=sb, q0=q0):
                            e.matmul(P[sb][:, q0:TT], lhsT=pairs[0][0], rhs=pairs[0][1], start=True, stop=False)
                            e.matmul(P[sb][:, q0:TT], lhsT=pairs[1][0], rhs=pairs[1][1], start=False, stop=False)
                            return e.matmul(P[sb][:, q0:q0 + 64], lhsT=maskL, rhs=onesr, start=False, stop=True)
                        if not S.dry:
                            PHASES["nmm"] += 3
                        op("pe", fn, rd + ["maskc"], ["P%d" % sb])
                    pb = kc % 3
                    act(pT[:, pb, q0:TT], P[sb][:, q0:TT], AF.Exp, ["P%d" % sb], A(["pT%d" % pb]), scale=ATT_SCALE)

                def pv(kc):
                    q0 = 0 if kc < 4 * gt else 128 * (kc - 4 * gt)
                    pb = kc % 3
                    mm1(po[:, q0:TT], v_sb[:, kc, h * 128:(h + 1) * 128], pT[:, pb, q0:TT], kc == 0, kc == nkc - 1,
                        ["v%d_%d" % (kc, h // 4)] + A(["pT%d" % pb]), [pok])
                    mm1(pd[:, q0:TT], ones, pT[:, pb, q0:TT], kc == 0, kc == nkc - 1, ["ones"] + A(["pT%d" % pb]), [pdk])

                for kc in range(nkc):
                    qk(kc)
                    if kc >= 1:
                        pv(kc - 1)
                pv(nkc - 1)
                act(lnv, pd[:], AF.Ln, [pdk], ["lnv"])
                act(rstd, lnv, AF.Exp, ["lnv"], ["rstd"], scale=-1.0)
                tt_(y_sb[:, h, tsl(tt)], po[:], rstd, ALU.mult, [pok, "rstd"], A(["y%d_%d" % (h, tt)]))

    def branch_b(half):
        switch()
        assert not bg
        RB, IB = (2, 2), (3, 7)
        GROT = (4, 5, 6)
        s_bufs = (rope_sb[:, 0, :], rope_sb[:, 1, :])
        s_keys = ("rope0", "rope1")

        def GBk(n, tt):
            return GROT[(2 * n + tt) % 3]
        a_bufs = (lnv, rstd)
        a_keys = ("lnv", "rstd")
        tiles = {}

        def tile_for(n):
            t = n // 2
            if t not in tiles:
                tiles[t] = wget(wrg_d[t], [8, 512], keep=1)
            return tiles[t]

        def stage1(n):
            wt, wk = tile_for(n)
            nn = n % 2
            par = n % 2
            par3 = n % 3
            xcol = slice(nn * 256, nn * 256 + 128)
            op("dve", lambda e, s: e.tensor_copy(out=zx[:, 0:3], in_=hist[:, n, 0:3]), ["hist%d" % n], A(["zxh"]))
            for tt in range(NTT):
                b = tt
                xk = "xc%d_%d" % (par3, tt)
                mm(P[b][:], [(wt[:, k, xcol], h_sb[:, k, tsl(tt)]) for k in range(8)], [wk] + hkeys(tt), ["P%d" % b])
                act(zx[:, 3 + tt * TT:3 + (tt + 1) * TT], P[b][:], AF.Copy, ["P%d" % b], A(["zx%d" % tt]))
                act(xcs[:, par3 * 2 + tt, :], P[b][:], AF.Identity, ["P%d" % b, "cvec"], A([xk]),
                    scale=cvec[:, C_CW + 24 + n:C_CW + 25 + n], bias=cvec[:, C_CB + n:C_CB + n + 1])
            op("dve", lambda e, s: e.tensor_copy(out=hist[:, n, 0:3], in_=zx[:, ST:ST + 3]), A(["zx1"]), ["hist%d" % n])
            for tt in range(NTT):
                zk = A(["zxh", "zx0"] if tt == 0 else ["zx0", "zx1"])
                xk = "xc%d_%d" % (par3, tt)
                xc_ = xcs[:, par3 * 2 + tt, :]
                o0 = tt * TT
                for w in range(3):
                    stt(xc_, zx[:, o0 + w:o0 + w + TT], cvec[:, C_CW + 8 * w + n:C_CW + 8 * w + n + 1], xc_, ALU.mult, ALU.add,
                        zk + A([xk]) + ["cvec"], A([xk]))
                xb_ = xcbs[:, par * 2 + tt, :]
                op("pool", lambda e, s, xb_=xb_, xc_=xc_: e.tensor_copy(out=xb_, in_=xc_), A([xk]), A(["xcb%d_%d" % (par, tt)]))

        def stage2(n):
            wt, wk = tile_for(n)
            nn = n % 2
            par = n % 2
            par3 = n % 3
            gcol = slice(nn * 256 + 128, nn * 256 + 256)
            def gate(which, tt):
                xb_ = xcbs[:, par * 2 + tt, :]
                xbk = A(["xcb%d_%d" % (par, tt)])
                bank = RB[tt] if which == 0 else IB[tt]
                mm1(P[bank][:], wrgai[:, which, n, :], xb_, True, True, ["wrgai"] + xbk, ["P%d" % bank])
            def tanh_gate(which, tt):
                if which == 0:
                    act(s_bufs[tt], P[RB[tt]][:], AF.Tanh, ["P%d" % RB[tt], "dvec"], [s_keys[tt]], scale=0.5, bias=dvec[:, 40 + n:41 + n])
                else:
                    ik = "P%d" % IB[tt]
                    act(P[IB[tt]][:], P[IB[tt]][:], AF.Tanh, [ik, "dvec"], [ik], scale=0.5, bias=dvec[:, 48 + n:49 + n])
            gate(0, 0)
            gate(1, 0)
            gate(1, 1)
            tanh_gate(0, 0)
            GB = (GBk(n, 0), GBk(n, 1))
            for tt in range(NTT):
                mm(P[GB[tt]][:], [(wt[:, k, gcol], h_sb[:, k, tsl(tt)]) for k in range(8)], [wk] + hkeys(tt), ["P%d" % GB[tt]])
            gate(0, 1)
            tanh_gate(1, 0)
            tanh_gate(1, 1)
            for tt in range(NTT):
                gk = "P%d" % GB[tt]
                act(P[GB[tt]][:], P[GB[tt]][:], AF.Gelu_apprx_tanh, [gk], [gk])
            tanh_gate(0, 1)
            for tt in range(NTT):
                act(a_bufs[tt], s_bufs[tt], AF.Exp, [s_keys[tt], "dvec"], [a_keys[tt]],
                    scale=dvec[:, 32 + n:33 + n], bias=dvec[:, 32 + n:33 + n])
            for tt in range(NTT):
                act(s_bufs[tt], a_bufs[tt], AF.Square, [a_keys[tt]], [s_keys[tt]])
            for tt in range(NTT):
                act(s_bufs[tt], s_bufs[tt], AF.Sqrt, [s_keys[tt]], [s_keys[tt]], scale=-1.0, bias=1.0)
            for tt in range(NTT):
                xk = A(["xc%d_%d" % (par3, tt)])
                xc_ = xcs[:, par3 * 2 + tt, :]
                stt(xc_, P[IB[tt]][:], 1.0, xc_, ALU.add, ALU.mult, ["P%d" % IB[tt]] + xk, xk)
            for tt in range(NTT):
                xk = A(["xc%d_%d" % (par3, tt)])
                xc_ = xcs[:, par3 * 2 + tt, :]
                stt(xc_, xc_, 0.5, s_bufs[tt], ALU.mult, ALU.mult, [s_keys[tt]] + xk, xk)
            for tt in range(NTT):
                xk = A(["xc%d_%d" % (par3, tt)])
                xc_ = xcs[:, par3 * 2 + tt, :]
                ab = a_bufs[tt]
                op("dve", lambda e, s, xc_=xc_, ab=ab: e.tensor_tensor_scan(out=xc_, data0=ab, data1=xc_, initial=state[:, n:n + 1],
                                                                            op0=ALU.mult, op1=ALU.add),
                   [a_keys[tt], "state%d" % n] + xk, xk)
                op("dve", lambda e, s, xc_=xc_: e.tensor_copy(out=state[:, n:n + 1], in_=xc_[:, TT - 1:TT]), xk, ["state%d" % n])
            for tt in range(NTT):
                xk = A(["xc%d_%d" % (par3, tt)])
                xc_ = xcs[:, par3 * 2 + tt, :]
                tt_(y_sb[:, n, tsl(tt)], P[GB[tt]][:], xc_, ALU.mult, ["P%d" % GB[tt]] + xk, A(["y%d_%d" % (n, tt)]))

        stage1(0)
        for n in range(8):
            if n < 7:
                stage1(n + 1)
            stage2(n)

    def branch_c(half):
        switch()
        assert not bg
        iters = [(t, hh, tt) for t in range(2) for hh in range(2) for tt in range(NTT)]
        tiles = {}

        def qproj(i):
            t, hh, tt = iters[i]
            if t not in tiles:
                tiles[t] = wget(wmq_d[t], [8, 512])
            wt, wk = tiles[t]
            par = i % 2
            for dc in range(2):
                b = dc
                col = slice(hh * 256 + dc * 128, hh * 256 + (dc + 1) * 128)
                mm(P[b][:], [(wt[:, k, col], h_sb[:, k, tsl(tt)]) for k in range(8)], [wk] + hkeys(tt), ["P%d" % b])
                evac(qm_sb[:, par * 2 + dc, :], P[b][:], ["P%d" % b], A(["qm%d" % (par * 2 + dc)]))

        def rest(i):
            t, hh, tt = iters[i]
            h = 2 * t + hh
            par = i % 2
            qk_ = A(["qm%d" % (par * 2), "qm%d" % (par * 2 + 1)])
            for mc in range(2):
                b = 2 + mc
                mm(P[b][:], [(km_sb[:, h * 2 + dc, mc * 128:(mc + 1) * 128], qm_sb[:, par * 2 + dc, :]) for dc in range(2)],
                   ["km%d" % (h * 2), "km%d" % (h * 2 + 1)] + qk_, ["P%d" % b])
                act(pm_sb[:, par * 2 + mc, :], P[b][:], AF.Exp, ["P%d" % b], A(["pm%d" % (par * 2 + mc)]), scale=MEM_SCALE)
            pmk = A(["pm%d" % (par * 2), "pm%d" % (par * 2 + 1)])
            db = 6 + par
            mm(P[db][:], [(ones, pm_sb[:, par * 2 + mc, :]) for mc in range(2)], ["ones"] + pmk, ["P%d" % db])
            for dc in range(2):
                b = 4 + dc
                col = slice(h * 256 + dc * 128, h * 256 + (dc + 1) * 128)
                mm(P[b][:], [(vm_sb[:, mc, col], pm_sb[:, par * 2 + mc, :]) for mc in range(2)],
                   ["vm%d_%d" % (mc, h // 2) for mc in range(2)] + pmk, ["P%d" % b])
            act(lnv, P[db][:], AF.Ln, ["P%d" % db], ["lnv"])
            act(rstd, lnv, AF.Exp, ["lnv"], ["rstd"], scale=-1.0)
            for dc in range(2):
                b = 4 + dc
                tt_(y_sb[:, h * 2 + dc, tsl(tt)], P[b][:], rstd, ALU.mult, ["P%d" % b, "rstd"], A(["y%d_%d" % (h * 2 + dc, tt)]))

        qproj(0)
        for i in range(len(iters)):
            if i + 1 < len(iters):
                qproj(i + 1)
            rest(i)

    def merge(br, after_tt=None):
        switch()
        cnt = 0
        for mp in range(4):
            wt, wk = wget(wmg_d[br * 4 + mp], [8, 512])
            for mi in range(2):
                m = 2 * mp + mi
                for tt in range(NTT):
                    b = cnt % 2
                    cnt += 1
                    pg, pp = P[2 * b], P[2 * b + 1]
                    mm(pg[:], [(wt[:, k, mi * 256:mi * 256 + 128], h_sb[:, k, tsl(tt)]) for k in range(8)], [wk] + hkeys(tt), ["P%d" % (2 * b)])
                    mm(pp[:], [(wt[:, k, mi * 256 + 128:mi * 256 + 256], y_sb[:, k, tsl(tt)]) for k in range(8)],
                       [wk] + A(["y%d_%d" % (k, tt) for k in range(8)]), ["P%d" % (2 * b + 1)])
                    act(sgm[b], pg[:], AF.Sigmoid, ["P%d" % (2 * b), "cvec"], A(["sgm%d" % b]),
                        bias=cvec[:, C_BG + br * 8 + m:C_BG + br * 8 + m + 1])
                    tt_(merged[:, m, tsl(tt)], sgm[b], pp[:], ALU.mult, A(["sgm%d" % b]) + ["P%d" % (2 * b + 1)], A(["mg%d_%d" % (m, tt)]))
        w0 = wget(wout_d[0], [8, 512])
        w1 = wget(wout_d[1], [8, 512], keep=1)
        for tt in range(NTT):
            for m2 in range(8):
                wt, wk = w0 if m2 < 4 else w1
                mi = m2 % 4
                b = 4 + cnt % 2
                cnt += 1
                mm(P[b][:], [(wt[:, m, mi * 128:(mi + 1) * 128], merged[:, m, tsl(tt)]) for m in range(8)],
                   [wk] + A(["mg%d_%d" % (m, tt) for m in range(8)]), ["P%d" % b])
                tt_(x_sb[:, m2, tsl(tt)], P[b][:], x_sb[:, m2, tsl(tt)], ALU.add, ["P%d" % b, "x%d_%d" % (m2, tt)], ["x%d_%d" % (m2, tt)])
            if after_tt is not None:
                after_tt(tt)

    oc = [0]

    def final_out_tt(s, half, tt, nxt):
        srcs = [x_sb[:, k, tsl(tt)] for k in range(8)]
        keys = ["x%d_%d" % (k, tt) for k in range(8)]
        for k in range(8):
            def sq_step(k=k):
                b = k % 2
                act(sq_sb[:, b, :], srcs[k], AF.Square, [keys[k]], ["sq%d" % b])
                mm1(P[7][:], ones, sq_sb[:, b, :], k == 0, k == 7, ["sq%d" % b, "ones"], ["P7"])
            bg.append((None, sq_step))
        bg.append((None, lambda: act(lnv, P[7][:], AF.Ln, ["P7"], ["lnv"], scale=1.0 / D, bias=EPS)))
        bg.append((None, lambda: act(rstd, lnv, AF.Exp, ["lnv"], ["rstd"], scale=-0.5)))
        c0 = s * SEQ + half * ST + tt * TT
        for k in range(8):
            def out_step(k=k):
                b = oc[0] % 2
                oc[0] += 1
                stt(ost[b], srcs[k], cvec[:, C_FIN + k:C_FIN + k + 1], rstd, ALU.mult, ALU.mult, [keys[k], "rstd", "cvec"], A(["ost%d" % b]))

                def st(e, sm, b=b, k=k, c0=c0):
                    return e.dma_start(out=outT[k * 128:(k + 1) * 128, c0:c0 + TT], in_=ost[b]).then_inc(sm[("dma", "o%d" % b)], 16)
                S.op("sp", st, reads=["@ost%d" % b, "GUARD"], dma="o%d" % b)
                if nxt is not None:
                    load_x_one(nxt[0], nxt[1], tt, k)
            bg.append((None, out_step))

    def load_x_one(s, half, tt, k):
        c0 = s * SEQ + half * ST + tt * TT

        def ld(e, sm):
            return e.dma_start(out=x_sb[:, k, tsl(tt)], in_=xT[k * 128:(k + 1) * 128, c0:c0 + TT]).then_inc(sm[("dma", "x%d_%d" % (k, tt))], 16)
        S.op("sp", ld, writes=["x%d_%d" % (k, tt)], dma="x%d_%d" % (k, tt))

    def load_x_tt(s, half, tt):
        for k in range(8):
            load_x_one(s, half, tt, k)

    def prologue():
        def ldc(e, sm):
            return e.dma_start(out=cvec, in_=cvec_d).then_inc(sm[("dma", "c")], 16)
        S.op("sp", ldc, writes=["cvec"], dma="c")

        def ldw(e, sm):
            return e.dma_start(out=wrgai.rearrange("p a b c -> p (a b c)"), in_=wrgai_d, max_dma_last_dim=4096).then_inc(sm[("dma", "wc")], 16)
        S.op("pool", ldw, writes=["wrgai"], dma="wc")
        op("dve", lambda e, s: e.memset(ones, 1.0), [], ["ones"])
        op("dve", lambda e, s: e.memset(maskL, 0.0), [], ["maskc"])
        op("dve", lambda e, s: e.memset(maskL[0:1, 64:128], -30000.0), ["maskc"], ["maskc"])
        op("dve", lambda e, s: e.memset(onesr, 0.0), ["maskc"], ["maskc"])
        op("dve", lambda e, s: e.memset(onesr[0:1, :], 1.0), ["maskc"], ["maskc"])
        yv, pv_ = dvec[:, 16:24], dvec[:, 24:32]
        act(yv, cvec[:, C_LAM:C_LAM + 8], AF.Exp, ["cvec"], ["dv_y"], scale=-1.0)
        ts = lambda o, i, s1, s2, o0, o1, r, w: op("dve", lambda e, s: e.tensor_scalar(out=o, in0=i, scalar1=s1, scalar2=s2, op0=o0, op1=o1), r, w)
        ts(pv_, yv, -0.25, 1.0 / 3.0, ALU.mult, ALU.add, ["dv_y"], ["dv_p"])
        tt_(pv_, pv_, yv, ALU.mult, ["dv_p", "dv_y"], ["dv_p"])
        ts(pv_, pv_, -1.0, 0.5, ALU.mult, ALU.add, ["dv_p"], ["dv_p"])
        tt_(pv_, pv_, yv, ALU.mult, ["dv_p", "dv_y"], ["dv_p"])
        ts(pv_, pv_, -1.0, 1.0, ALU.mult, ALU.add, ["dv_p"], ["dv_p"])
        tt_(pv_, pv_, yv, ALU.mult, ["dv_p", "dv_y"], ["dv_p"])
        ts(dvec[:, 0:8], pv_, -8.0, 0.0, ALU.mult, ALU.add, ["dv_p"], ["dvec"])
        ts(dvec[:, 8:16], pv_, -16.0, 0.0, ALU.mult, ALU.add, ["dv_p"], ["dvec"])
        ts(dvec[:, 32:40], pv_, -4.0, 0.0, ALU.mult, ALU.add, ["dv_p"], ["dvec"])
        ts(dvec[:, 40:48], cvec[:, C_BA:C_BA + 8], 0.5, 0.0, ALU.mult, ALU.add, ["cvec", "dvec"], ["dvec"])
        ts(dvec[:, 48:56], cvec[:, C_BI:C_BI + 8], 0.5, 0.0, ALU.mult, ALU.add, ["cvec", "dvec"], ["dvec"])

    def body():
        wstate["i"] = 0
        wstate["released"] = 0
        oc[0] = 0
        prologue()
        sts = [(s, half) for s in range(SPC) for half in range(NST)]
        for tt in range(NTT):
            load_x_tt(0, 0, tt)
        for idx, (s, half) in enumerate(sts):
            if half == 0:
                flush()
                mark("memkv")
                mem_kv(s)
                op("dve", lambda e, sm: e.memset(state, 0.0), ["state%d" % n for n in range(8)], ["state%d" % n for n in range(8)])
                op("dve", lambda e, sm: e.memset(hist.rearrange("p a b -> p (a b)"), 0.0), [], ["hist%d" % n for n in range(8)])
            mark("ffn1")
            norm_tt(C_FFN1, 0, front=True)
            norm_tt(C_FFN1, 1)
            ffn(0, lambda tt: norm_tt(C_MIX, tt))
            mark("A")
            branch_a(half)
            mark("mergeA")
            merge(0)
            mark("B")
            branch_b(half)
            mark("mergeB")
            merge(1)
            mark("C")
            branch_c(half)
            mark("mergeC")
            merge(2, lambda tt: norm_tt(C_FFN2, tt))
            mark("ffn2")

            def tail(tt, s=s, half=half, idx=idx):
                flush()
                final_out_tt(s, half, tt, sts[idx + 1] if idx + 1 < len(sts) else None)
            ffn(1, tail)
        flush()
        mark("end")

    S.dry = True
    body()
    S.dry = False
    ctx["pending_switch"] = False
    cp_ctr[0] = 0
    PHASES["nmm"] = 0
    PHASES["marks"] = []
    body()
    assert wstate["i"] == len(wplan)
    S.check_progress()
    S.emit(nc)
    import sys
    print("[mk] ops per engine", S.count, "dma", {k: v // 16 for k, v in S.dma_cnt.items()}, "wtiles", len(wplan), file=sys.stderr)
    return nc


def _tile(W):
    K, N = W.shape
    kc = K // 128
    return np.ascontiguousarray(W.reshape(kc, 128, N).transpose(1, 0, 2).reshape(128, kc * N))


def _pcol(v):
    return np.ascontiguousarray(v.reshape(-1, 128).T)


def _prep_shared(inp):
    f = lambda k: np.asarray(inp[k], dtype=np.float32)
    sh = {}
    r128 = lambda a: np.arange(a * 128, (a + 1) * 128)
    for name, wi, wd in (("wf1", "ffn1_w_in", "ffn1_w_down"), ("wf2", "ffn2_w_in", "ffn2_w_down")):
        Wi = f(wi)[0]
        Wd = f(wd)[0]
        tiles = []
        for t in range(11):
            cols = np.concatenate([r128(2 * t), HID + r128(2 * t), r128(2 * t + 1), HID + r128(2 * t + 1)])
            tiles.append(_tile(Wi[:, cols]))
        sh[name + "i"] = np.stack(tiles)
        sh[name + "da"] = np.stack([_tile(Wd[0:G0 * 128, m * 128:(m + 1) * 128]) for m in range(8)])
        sh[name + "db"] = np.stack([_tile(Wd[G0 * 128:, m * 128:(m + 1) * 128]) for m in range(8)])
    Win = f("w_in")[0]
    o_cq, o_ckv, o_kr, o_x, o_g, o_mq, o_gate = 0, 384, 640, 704, 1728, 2752, 3776
    kr = np.arange(o_kr, o_kr + 64)
    krsw = np.concatenate([kr[32:], kr[:32]])
    sh["wA1"] = _tile(Win[:, np.concatenate([np.arange(o_cq, o_cq + 384), kr, kr])])
    sh["wA2"] = _tile(Win[:, np.concatenate([np.arange(o_ckv, o_ckv + 256), krsw, krsw])])
    sh["wrg"] = np.stack([_tile(Win[:, np.concatenate([o_x + r128(2 * t), o_g + r128(2 * t), o_x + r128(2 * t + 1), o_g + r128(2 * t + 1)])])
                          for t in range(4)])
    sh["wmq"] = np.stack([_tile(Win[:, o_mq + t * 512:o_mq + (t + 1) * 512]) for t in range(2)])
    Wbr = f("w_branch")[0]
    tiles = []
    for b in range(3):
        for mp in range(4):
            parts = []
            for mi in range(2):
                m = 2 * mp + mi
                parts.append(Win[:, o_gate + b * 1024 + m * 128:o_gate + b * 1024 + (m + 1) * 128])
                parts.append(Wbr[b][:, m * 128:(m + 1) * 128])
            tiles.append(_tile(np.concatenate(parts, axis=1)))
    sh["wmg"] = np.stack(tiles)
    Wo = f("w_out")[0]
    sh["wout"] = np.stack([_tile(Wo[:, t * 512:(t + 1) * 512]) for t in range(2)])
    Wuq = f("w_uq")[0]
    sh["wuqn"] = _tile(Wuq[:, np.concatenate([h * 192 + np.arange(128) for h in range(8)])])
    rope_c = np.concatenate([h * 192 + 128 + np.arange(64) for h in range(8)])
    rope_sw = np.concatenate([h * 192 + 128 + np.concatenate([np.arange(32, 64), np.arange(0, 32)]) for h in range(8)])
    sh["wuqr"] = _tile(Wuq[:, np.concatenate([rope_c, rope_sw])])
    Wukv = f("w_ukv")[0]
    sh["wukv"] = _tile(Wukv[:, np.concatenate([h * 256 + np.arange(128) for h in range(8)] + [h * 256 + 128 + np.arange(128) for h in range(8)])])
    Wm = f("w_mem_kv")[0]
    sh["wmkv"] = np.stack([_tile(Wm[:, t * 512:(t + 1) * 512]) for t in range(4)])
    rg = np.stack([f("w_rg_a")[0], f("w_rg_i")[0]])
    sh["wrgai"] = np.ascontiguousarray(rg.transpose(2, 0, 1, 3).reshape(128, 2048))
    cv = np.zeros((128, NCV), np.float32)
    cv[:, C_FFN1:C_FFN1 + 8] = _pcol(f("ffn1_norm")[0])
    cv[:, C_MIX:C_MIX + 8] = _pcol(f("mix_norm")[0])
    cv[:, C_FFN2:C_FFN2 + 8] = _pcol(f("ffn2_norm")[0])
    cv[:, C_FIN:C_FIN + 8] = _pcol(f("final_norm"))
    cv[:, C_MEM:C_MEM + 8] = _pcol(f("mem_norm")[0])
    cv[:, C_QN:C_QN + 3] = _pcol(f("q_norm")[0])
    cv[:, C_KVN:C_KVN + 2] = _pcol(f("kv_norm")[0])
    cw = f("conv_w")[0]
    for w in range(4):
        cv[:, C_CW + 8 * w:C_CW + 8 * w + 8] = _pcol(cw[w, 0])
    cv[:, C_CB:C_CB + 8] = _pcol(f("conv_b")[0])
    cv[:, C_BA:C_BA + 8] = f("b_rg_a")[0].T
    cv[:, C_BI:C_BI + 8] = f("b_rg_i")[0].T
    cv[:, C_LAM:C_LAM + 8] = _pcol(f("lru_lambda")[0])
    bg = f("b_gate")[0]
    for b in range(3):
        cv[:, C_BG + 8 * b:C_BG + 8 * b + 8] = _pcol(bg[b])
    sh["cvec"] = cv
    pos = np.arange(SEQ, dtype=np.float32)
    inv_freq = (1.0 / (np.float32(10000.0) ** (np.arange(0, 64, 2, dtype=np.float32) / np.float32(64.0)))).astype(np.float32)
    ang = (pos[:, None] * inv_freq[None, :]).astype(np.float32)
    cos, sin = np.cos(ang).astype(np.float32), np.sin(ang).astype(np.float32)
    rope = np.zeros((128, 2, SEQ), np.float32)
    for p in range(128):
        j = p % 64
        rope[p, 0] = cos[:, j % 32]
        rope[p, 1] = sin[:, j % 32] * (-1.0 if j < 32 else 1.0)
    sh["rope"] = rope
    return sh


_NC_CACHE = {}


def kernel(**inputs):
    x = np.asarray(inputs["x"], dtype=np.float32)
    mem = np.asarray(inputs["mem"], dtype=np.float32)
    sh = _prep_shared(inputs)
    if "nc" not in _NC_CACHE:
        _NC_CACHE["nc"] = build_program()
    nc = _NC_CACHE["nc"]
    in_maps = []
    for c in range(NCORES):
        d = dict(sh)
        d["xT"] = np.ascontiguousarray(x[c * SPC:(c + 1) * SPC].transpose(2, 0, 1).reshape(D, SPC * SEQ))
        d["memT"] = np.ascontiguousarray(mem[c * SPC:(c + 1) * SPC].transpose(2, 0, 1).reshape(D, SPC * NMEM))
        in_maps.append(d)
    res = run_bass_kernel_spmd(nc, in_maps, core_ids=list(range(NCORES)))
    out = np.empty((NB, SEQ, D), np.float32)
    for c in range(NCORES):
        o = np.asarray(res.results[c]["outT"]).reshape(D, SPC, SEQ)
        out[c * SPC:(c + 1) * SPC] = o.transpose(1, 2, 0)
    return out
```

```python
import numpy as np
import concourse.bass as bass
import concourse.mybir as mybir
from concourse.bass_utils import run_bass_kernel_spmd

F32 = mybir.dt.float32
BF16 = mybir.dt.bfloat16
AF = mybir.ActivationFunctionType
ALU = mybir.AluOpType


class Sched:
    ENGS = ("pe", "act", "dve", "pool", "sp")

    def __init__(self):
        self.ops = {e: [] for e in self.ENGS}
        self.count = {e: 0 for e in self.ENGS}
        self.last_writer = {}
        self.readers = {}
        self.waited = {e: {} for e in self.ENGS}
        self.dma_cnt = {}
        self.dry = False

    def op(self, eng, fn, reads=(), writes=(), dma=None, ndma=1):
        if self.dry:
            return None
        deps = {}
        for r in reads:
            t = self.last_writer.get(r)
            if t is not None:
                deps[t[0]] = max(deps.get(t[0], 0), t[1])
        for w in writes:
            t = self.last_writer.get(w)
            if t is not None:
                deps[t[0]] = max(deps.get(t[0], 0), t[1])
            for t in self.readers.get(w, ()):
                deps[t[0]] = max(deps.get(t[0], 0), t[1])
        if dma is None:
            self.count[eng] += 1
            tok = (("eng", eng), self.count[eng])
        else:
            self.dma_cnt[dma] = self.dma_cnt.get(dma, 0) + 16 * ndma
            tok = (("dma", dma), self.dma_cnt[dma])
        waits = []
        wd = self.waited[eng]
        for sk, v in deps.items():
            if wd.get(sk, 0) >= v:
                continue
            wd[sk] = v
            waits.append((sk, v))
        self.ops[eng].append((fn, waits, dma is None, tok))
        for r in reads:
            self.readers.setdefault(r, []).append(tok)
        for w in writes:
            self.last_writer[w] = tok
            self.readers[w] = []
        return tok

    def check_progress(self):
        pos = {e: 0 for e in self.ENGS}
        sem = {}
        while True:
            progress = False
            for e in self.ENGS:
                q = self.ops[e]
                while pos[e] < len(q):
                    fn, waits, inc, tok = q[pos[e]]
                    if all(sem.get(sk, 0) >= v for sk, v in waits):
                        sem[tok[0]] = max(sem.get(tok[0], 0), tok[1])
                        pos[e] += 1
                        progress = True
                    else:
                        break
            if not progress:
                break
        stuck = {e: (pos[e], len(self.ops[e])) for e in self.ENGS if pos[e] < len(self.ops[e])}
        assert not stuck, "schedule deadlocks: %r" % stuck

    def emit(self, nc, final_waits=()):
        import contextlib
        dma_keys = sorted(self.dma_cnt.keys(), key=str)
        with contextlib.ExitStack() as es:
            sems = {}
            for e in self.ENGS:
                sems[("eng", e)] = es.enter_context(nc.semaphore("s_" + e))
            for k in dma_keys:
                sems[("dma", k)] = es.enter_context(nc.semaphore("d_" + str(k)))
            block = es.enter_context(nc.Block())
            ops = self.ops

            def run(engname, eng):
                mysem = sems[("eng", engname)]
                for fn, waits, inc, _tok in ops[engname]:
                    for sk, v in waits:
                        eng.wait_ge(sems[sk], v)
                    ins = fn(eng, sems)
                    if inc:
                        ins.then_inc(mysem, 1)

            @block.tensor
            def _(eng):
                run("pe", eng)

            @block.scalar
            def _(eng):
                run("act", eng)

            @block.vector
            def _(eng):
                run("dve", eng)

            @block.gpsimd
            def _(eng):
                run("pool", eng)

            @block.sync
            def _(eng):
                run("sp", eng)
                for k in dma_keys:
                    eng.wait_ge(sems[("dma", k)], self.dma_cnt[k])
                for e in ("pe", "act", "dve", "pool"):
                    if self.count[e]:
                        eng.wait_ge(sems[("eng", e)], self.count[e])


D = 1024
SEQ = 2048
NB = 16
NCORES = 8
SPC = NB // NCORES
ST = 1024
TT = 512
NTT = ST // TT
NST = SEQ // ST
HID = 2816
NHC = HID // 128
G0 = 12
NMEM = 256
EPS = 1e-6
ATT_SCALE = 192.0 ** -0.5
MEM_SCALE = 256.0 ** -0.5
NSLOT = 4
SLOT_ELEMS = 4096

C_FFN1, C_MIX, C_FFN2, C_FIN, C_MEM = 0, 8, 16, 24, 32
C_QN, C_KVN = 40, 43
C_CW, C_CB = 45, 77
C_BA, C_BI, C_LAM = 85, 93, 101
C_BG = 109
NCV = 136


PHASES = {"nmm": 0, "marks": []}


class Arena:
    def __init__(self, nc, name, nbytes):
        self.hb = nc.alloc_sbuf_tensor(name, [128, nbytes // 2], BF16)
        self.hf = self.hb.bitcast(F32)
        self.nbytes = nbytes
        self.off = 0

    def alloc(self, shape, dt, at=None):
        n = int(np.prod(shape))
        nb = n * (4 if dt == F32 else 2)
        off = self.off if at is None else at
        assert off % 32 == 0 and off + nb <= self.nbytes, (off, nb, self.nbytes)
        if dt == F32:
            ap = self.hf[:, off // 4: off // 4 + n]
        else:
            ap = self.hb[:, off // 2: off // 2 + n]
        if len(shape) == 2:
            ap = ap.rearrange("p (a b) -> p a b", a=shape[0])
        elif len(shape) == 3:
            ap = ap.rearrange("p (a b c) -> p a b c", a=shape[0], b=shape[1])
        if at is None:
            self.off = off + (nb + 31) // 32 * 32
        return ap


def build_program():
    nc = bass.Bass("TRN2", target_bir_lowering=False)

    def din(name, shape):
        return nc.dram_tensor(name, list(shape), F32, kind="ExternalInput").ap()

    xT = din("xT", [D, SPC * SEQ])
    memT = din("memT", [D, SPC * NMEM])
    cvec_d = din("cvec", [128, NCV])
    rope_d = din("rope", [128, 2, SEQ])
    wf_i = [din("wf1i", [11, 128, 4096]), din("wf2i", [11, 128, 4096])]
    wf_da = [din("wf1da", [8, 128, G0 * 128]), din("wf2da", [8, 128, G0 * 128])]
    wf_db = [din("wf1db", [8, 128, (NHC - G0) * 128]), din("wf2db", [8, 128, (NHC - G0) * 128])]
    wA1_d = din("wA1", [128, 4096])
    wA2_d = din("wA2", [128, 3072])
    wrg_d = din("wrg", [4, 128, 4096])
    wmq_d = din("wmq", [2, 128, 4096])
    wmg_d = din("wmg", [12, 128, 4096])
    wout_d = din("wout", [2, 128, 4096])
    wuqn_d = din("wuqn", [128, 3072])
    wuqr_d = din("wuqr", [128, 3072])
    wukv_d = din("wukv", [128, 4096])
    wmkv_d = din("wmkv", [4, 128, 4096])
    wrgai_d = din("wrgai", [128, 2048])
    outT = nc.dram_tensor("outT", [D, SPC * SEQ], F32, kind="ExternalOutput").ap()

    total = nc.sbuf_bytes_remaining
    main = Arena(nc, "main", (total - 256) // 64 * 64)
    cvec = main.alloc([1, NCV], F32)[:, 0, :]
    dvec = main.alloc([1, 56], F32)[:, 0, :]
    ones = main.alloc([1, 128], BF16)[:, 0, :]
    maskL = main.alloc([1, 128], BF16)[:, 0, :]
    onesr = main.alloc([1, 64], BF16)[:, 0, :]
    wrgai = main.alloc([2, 8, 128], BF16)
    state = main.alloc([1, 8], F32)[:, 0, :]
    hist = main.alloc([8, 4], F32)
    ring = [main.alloc([1, SLOT_ELEMS], BF16)[:, 0, :] for _ in range(NSLOT)]
    x_sb = main.alloc([8, ST], F32)
    h_sb = main.alloc([8, ST], BF16)
    kn_sb = main.alloc([8, SEQ], BF16)
    v_sb = main.alloc([16, 1024], BF16)
    kr_sb = main.alloc([1, SEQ], BF16)[:, 0, :]
    km_sb = main.alloc([8, NMEM], BF16)
    vm_sb = main.alloc([2, 1024], BF16)
    sq_sb = main.alloc([2, TT], BF16)
    lnv = main.alloc([1, TT], F32)[:, 0, :]
    rstd = main.alloc([1, TT], F32)[:, 0, :]
    rope_sb = main.alloc([2, TT], F32)
    A0 = main.off
    asz = main.nbytes - A0
    assert asz >= 36864, asz

    def aa(shape, dt, rel):
        return main.alloc(shape, dt, at=A0 + rel)

    hid = aa([G0, ST], BF16, 0)
    sg = [aa([1, TT], F32, 24576 + 2048 * i)[:, 0, :] for i in range(2)]
    ost = [aa([1, TT], F32, 28672 + 2048 * i)[:, 0, :] for i in range(2)]
    memx = aa([8, NMEM], F32, 0)
    memn = aa([8, NMEM], BF16, 8192)
    y_sb = aa([8, ST], BF16, 0)
    U = 16384
    cqn = aa([3, TT], BF16, U)
    ckvn = aa([2, TT], BF16, U + 3072)
    qn_sb = aa([2, TT], BF16, U + 5120)
    qrz = aa([4, TT], BF16, U + 7168)
    pT = aa([3, TT], BF16, U + 11264)
    rt = [aa([1, TT], F32, U + 14336 + 2048 * i)[:, 0, :] for i in range(2)]
    merged = aa([8, ST], BF16, U)
    sgm = [aa([1, TT], F32, U + 16384 + 2048 * i)[:, 0, :] for i in range(2)]
    zx = aa([1, 1032], F32, U)[:, 0, :]
    xcs = aa([6, TT], F32, U + 4128)
    xcbs = aa([4, TT], BF16, U + 16416)
    qm_sb = aa([4, TT], BF16, U)
    pm_sb = aa([4, TT], BF16, U + 4096)

    P = [nc.alloc_psum_tensor("ps%d" % i, [128, TT], F32) for i in range(8)]

    S = Sched()
    ctx = {"pending_switch": False}

    def A(keys):
        return ["@" + k for k in keys]

    def op(eng, fn, reads=(), writes=()):
        reads = list(reads)
        writes = list(writes)
        if any(k.startswith("@") for k in reads + writes):
            if ctx["pending_switch"]:
                writes.append("GUARD")
                ctx["pending_switch"] = False
            else:
                reads.append("GUARD")
        return S.op(eng, fn, reads, writes)

    def switch():
        ctx["pending_switch"] = True

    def mark(name):
        if not S.dry:
            PHASES["marks"].append((name, PHASES["nmm"]))

    import collections
    bg = collections.deque()

    def drain(n=3):
        for _ in range(n):
            if not bg:
                return
            bg.popleft()[1]()

    def flush(tt=None):
        if tt is None:
            while bg:
                bg.popleft()[1]()
            return
        last = -1
        for i, (t, _) in enumerate(bg):
            if t == tt:
                last = i
        for _ in range(last + 1):
            bg.popleft()[1]()

    def mm(out, pairs, reads, writes):
        n = len(pairs)
        if not S.dry:
            PHASES["nmm"] += n

        def fn(e, s):
            for i, (l, r) in enumerate(pairs):
                ins = e.matmul(out, lhsT=l, rhs=r, start=(i == 0), stop=(i == n - 1))
            return ins
        op("pe", fn, reads, writes)
        drain()

    def mm1(out, l, r, start, stop, reads, writes):
        if not S.dry:
            PHASES["nmm"] += 1
        op("pe", lambda e, s: e.matmul(out, lhsT=l, rhs=r, start=start, stop=stop), reads, writes)

    def act(out, in_, func, reads, writes, **kw):
        op("act", lambda e, s: e.activation(out=out, in_=in_, func=func, **kw), reads, writes)

    def stt(out, in0, scalar, in1, op0, op1, reads, writes):
        op("dve", lambda e, s: e.scalar_tensor_tensor(out=out, in0=in0, scalar=scalar, in1=in1, op0=op0, op1=op1), reads, writes)

    def tt_(out, in0, in1, o, reads, writes):
        op("dve", lambda e, s: e.tensor_tensor(out=out, in0=in0, in1=in1, op=o), reads, writes)

    cp_ctr = [0]

    def evac(out, in_, reads, writes):
        cp_ctr[0] += 1
        if cp_ctr[0] % 2:
            act(out, in_, AF.Copy, reads, writes)
        else:
            op("dve", lambda e, s: e.tensor_copy(out=out, in_=in_), reads, writes)

    wplan = []
    wstate = {"i": 0, "loaded": 0, "released": 0}

    def issue_load(j):
        dram = wplan[j]
        slot = j % NSLOT
        n = dram.shape[-1]

        def fn(e, s):
            return e.dma_start(out=ring[slot][:, 0:n], in_=dram, max_dma_last_dim=4096).then_inc(s[("dma", "w%d" % slot)], 16)
        S.op("pool", fn, writes=["W%d" % slot], dma="w%d" % slot)

    def wget(dram, shape, keep=0):
        i = wstate["i"]
        wstate["i"] += 1
        assert keep < NSLOT
        wstate["released"] = max(wstate["released"], i - keep)
        if S.dry:
            wplan.append(dram)
        else:
            while wstate["loaded"] < min(len(wplan), wstate["released"] + NSLOT):
                issue_load(wstate["loaded"])
                wstate["loaded"] += 1
            assert wstate["loaded"] > i
        slot = i % NSLOT
        n = int(np.prod(shape))
        ap = ring[slot][:, 0:n]
        if len(shape) == 2:
            ap = ap.rearrange("p (a b) -> p a b", a=shape[0])
        return ap, "W%d" % slot

    def tsl(tt):
        return slice(tt * TT, (tt + 1) * TT)

    def rmsnorm(srcs, src_keys, gcol, nfeat, ntok, outs, out_keys):
        flush()
        for st_ in rmsnorm_steps(srcs, src_keys, gcol, nfeat, ntok, outs, out_keys):
            st_()

    def rmsnorm_steps(srcs, src_keys, gcol, nfeat, ntok, outs, out_keys):
        nk = len(srcs)
        steps = []
        for k in range(nk):
            def sq_step(k=k):
                b = k % 2
                act(sq_sb[:, b, 0:ntok], srcs[k], AF.Square, [src_keys[k]], ["sq%d" % b])
                mm1(P[7][:, 0:ntok], ones, sq_sb[:, b, 0:ntok], k == 0, k == nk - 1, ["sq%d" % b, "ones"], ["P7"])
            steps.append(sq_step)
        steps.append(lambda: act(lnv[:, 0:ntok], P[7][:, 0:ntok], AF.Ln, ["P7"], ["lnv"], scale=1.0 / nfeat, bias=EPS))
        steps.append(lambda: act(rstd[:, 0:ntok], lnv[:, 0:ntok], AF.Exp, ["lnv"], ["rstd"], scale=-0.5))
        for k in range(nk):
            def mul_step(k=k):
                stt(outs[k], srcs[k], cvec[:, gcol + k:gcol + k + 1], rstd[:, 0:ntok], ALU.mult, ALU.mult,
                    [src_keys[k], "rstd", "cvec"], [out_keys[k]])
            steps.append(mul_step)
        return steps

    def norm_tt(gcol, tt, front=False):
        steps = rmsnorm_steps([x_sb[:, k, tsl(tt)] for k in range(8)], ["x%d_%d" % (k, tt) for k in range(8)], gcol, D, TT,
                              [h_sb[:, k, tsl(tt)] for k in range(8)], ["h%d_%d" % (k, tt) for k in range(8)])
        if front:
            bg.extendleft(reversed([(tt, st_) for st_ in steps]))
        else:
            bg.extend([(tt, st_) for st_ in steps])

    def hkeys(tt):
        flush(tt)
        return ["h%d_%d" % (k, tt) for k in range(8)]

    def ffn(fi, after_tt):
        switch()
        cnt = 0
        for grp in range(2):
            j0 = 0 if grp == 0 else G0
            ng = G0 if grp == 0 else NHC - G0
            def up(t, jj, tt, wt, wk):
                nonlocal cnt
                jl = 2 * t + jj - j0
                b = cnt % 2
                cnt += 1
                pg, pu = P[2 * b], P[2 * b + 1]
                mm(pg[:], [(wt[:, k, jj * 256:jj * 256 + 128], h_sb[:, k, tsl(tt)]) for k in range(8)],
                   [wk] + hkeys(tt), ["P%d" % (2 * b)])
                mm(pu[:], [(wt[:, k, jj * 256 + 128:jj * 256 + 256], h_sb[:, k, tsl(tt)]) for k in range(8)],
                   [wk] + hkeys(tt), ["P%d" % (2 * b + 1)])
                act(sg[b], pg[:], AF.Silu, ["P%d" % (2 * b)], A(["sg%d" % b]))
                tt_(hid[:, jl, tsl(tt)], sg[b], pu[:], ALU.mult, A(["sg%d" % b]) + ["P%d" % (2 * b + 1)],
                    A(["hid%d_%d" % (jl, tt)]))
            tiles_g = range(j0 // 2, (j0 + ng) // 2)
            tiles_g = list(tiles_g)
            if grp == 0:
                wa = wget(wf_i[fi][tiles_g[0]], [8, 512])
                wb = wget(wf_i[fi][tiles_g[1]], [8, 512], keep=1)
                for tt in range(NTT):
                    for t, (wt, wk) in ((tiles_g[0], wa), (tiles_g[1], wb)):
                        for jj in range(2):
                            up(t, jj, tt, wt, wk)
                tiles_g = tiles_g[2:]
            for t in tiles_g:
                wt, wk = wget(wf_i[fi][t], [8, 512])
                for jj in range(2):
                    for tt in range(NTT):
                        up(t, jj, tt, wt, wk)
            wd = wf_da[fi] if grp == 0 else wf_db[fi]

            def down(m, tt, wt, wk):
                nonlocal cnt
                b = 4 + (cnt % 2)
                cnt += 1
                mm(P[b][:], [(wt[:, j, :], hid[:, j, tsl(tt)]) for j in range(ng)],
                   [wk] + A(["hid%d_%d" % (j, tt) for j in range(ng)]), ["P%d" % b])
                stt(x_sb[:, m, tsl(tt)], P[b][:], 0.5, x_sb[:, m, tsl(tt)], ALU.mult, ALU.add,
                    ["P%d" % b, "x%d_%d" % (m, tt)], ["x%d_%d" % (m, tt)])
            if grp == 0:
                for m in range(8):
                    wt, wk = wget(wd[m], [ng, 128])
                    for tt in range(NTT):
                        down(m, tt, wt, wk)
            else:
                for tt in range(NTT):
                    for m in range(8):
                        wt, wk = wget(wd[m], [ng, 128])
                        down(m, tt, wt, wk)
                    after_tt(tt)

    def mem_kv(s):
        switch()

        def ld(e, sm):
            return e.dma_start(out=memx, in_=memT.rearrange("(k p) t -> p k t", p=128)[:, :, s * NMEM:(s + 1) * NMEM]).then_inc(sm[("dma", "mem")], 16)
        S.op("sp", ld, reads=["GUARD"], writes=["@memx", "GUARD"], dma="mem")
        ctx["pending_switch"] = False
        rmsnorm([memx[:, k, :] for k in range(8)], A(["memx"] * 8), C_MEM, D, NMEM,
                [memn[:, k, :] for k in range(8)], A(["memn%d" % k for k in range(8)]))
        mk = A(["memn%d" % k for k in range(8)])
        c2 = 0
        for t in range(2):
            wt, wk = wget(wmkv_d[t], [8, 512])
            for c in range(4):
                b = c2 % 2
                c2 += 1
                mm(P[b][:, 0:NMEM], [(wt[:, k, c * 128:(c + 1) * 128], memn[:, k, :]) for k in range(8)], [wk] + mk, ["P%d" % b])
                evac(km_sb[:, t * 4 + c, :], P[b][:, 0:NMEM], ["P%d" % b], ["km%d" % (t * 4 + c)])
        for t in range(2):
            wt, wk = wget(wmkv_d[2 + t], [8, 512])
            for mc in range(2):
                b = c2 % 2
                c2 += 1
                mm(P[b][:], [(memn[:, k, mc * 128:(mc + 1) * 128], wt[:, k, :]) for k in range(8)], [wk] + mk, ["P%d" % b])
                evac(vm_sb[:, mc, t * 512:(t + 1) * 512], P[b][:], ["P%d" % b], ["vm%d_%d" % (mc, t)])

    def branch_a(half):
        switch()
        op("dve", lambda e, s: e.memset(qrz.rearrange("p a b -> p (a b)"), 0.0), [], A(["qrz0", "qrz1", "qrz2", "qrz3"]))
        for tt in range(NTT):
            gt = half * NTT + tt
            tok = slice(gt * TT, (gt + 1) * TT)
            hk = hkeys(tt)

            mark("A1")

            def ldr(e, sm, gt=gt):
                return e.dma_start(out=rope_sb, in_=rope_d[:, :, gt * TT:(gt + 1) * TT]).then_inc(sm[("dma", "rope")], 16)
            S.op("sp", ldr, writes=["rope0", "rope1"], dma="rope")
            w1, w1k = wget(wA1_d, [8, 512])
            for c, b in ((0, 0), (1, 1), (2, 2), (3, 5)):
                mm(P[b][:], [(w1[:, k, c * 128:(c + 1) * 128], h_sb[:, k, tsl(tt)]) for k in range(8)], [w1k] + hk, ["P%d" % b])
            w2, w2k = wget(wA2_d, [8, 384])
            for c, b in ((0, 3), (1, 4), (2, 6)):
                mm(P[b][:], [(w2[:, k, c * 128:(c + 1) * 128], h_sb[:, k, tsl(tt)]) for k in range(8)], [w2k] + hk, ["P%d" % b])
            rmsnorm([P[k][:] for k in range(3)], ["P0", "P1", "P2"], C_QN, 384, TT,
                    [cqn[:, k, :] for k in range(3)], A(["cqn%d" % k for k in range(3)]))
            rmsnorm([P[3 + k][:] for k in range(2)], ["P3", "P4"], C_KVN, 256, TT,
                    [ckvn[:, k, :] for k in range(2)], A(["ckvn%d" % k for k in range(2)]))
            tt_(rt[0], P[5][:], rope_sb[:, 0, :], ALU.mult, ["P5", "rope0"], A(["rt0"]))
            tt_(rt[1], P[6][:], rope_sb[:, 1, :], ALU.mult, ["P6", "rope1"], A(["rt1"]))
            tt_(kr_sb[:, tok], rt[0], rt[1], ALU.add, A(["rt0", "rt1"]), ["kr%d" % gt])
            wkv, wkvk = wget(wukv_d, [2, 2048])
            ck = A(["ckvn0", "ckvn1"])
            c2 = 0
            for h in range(8):
                b = c2 % 6
                c2 += 1
                mm(P[b][:], [(wkv[:, k, h * 128:(h + 1) * 128], ckvn[:, k, :]) for k in range(2)], [wkvk] + ck, ["P%d" % b])
                evac(kn_sb[:, h, tok], P[b][:], ["P%d" % b], ["kn%d_%d" % (h, gt)])
            for tk in range(4):
                for hf in range(2):
                    b = c2 % 6
                    c2 += 1
                    mm(P[b][:], [(ckvn[:, k, tk * 128:(tk + 1) * 128], wkv[:, k, 1024 + hf * 512:1024 + (hf + 1) * 512]) for k in range(2)],
                       [wkvk] + ck, ["P%d" % b])
                    evac(v_sb[:, gt * 4 + tk, hf * 512:(hf + 1) * 512], P[b][:], ["P%d" % b], ["v%d_%d" % (gt * 4 + tk, hf)])
            mark("ATT")
            assert not bg
            wqn, wqnk = wget(wuqn_d, [3, 1024])
            wqr, wqrk = wget(wuqr_d, [3, 1024], keep=1)
            cq = A(["cqn0", "cqn1", "cqn2"])
            nkc = 4 * gt + 4

            def qproj(h):
                qb = h % 2
                hp = h // 2
                rb = hp % 2
                mm(P[6][:], [(wqn[:, k, h * 128:(h + 1) * 128], cqn[:, k, :]) for k in range(3)], [wqnk] + cq, ["P6"])
                evac(qn_sb[:, qb, :], P[6][:], ["P6"], A(["qn%d" % qb]))
                if h % 2 == 0:
                    mm(P[7][:], [(wqr[:, k, hp * 128:(hp + 1) * 128], cqn[:, k, :]) for k in range(3)], [wqrk] + cq, ["P7"])
                    mm(P[6][:], [(wqr[:, k, 512 + hp * 128:512 + (hp + 1) * 128], cqn[:, k, :]) for k in range(3)], [wqrk] + cq, ["P6"])
                    tt_(rt[0], P[7][:], rope_sb[:, 0, :], ALU.mult, ["P7", "rope0"], A(["rt0"]))
                    tt_(rt[1], P[6][:], rope_sb[:, 1, :], ALU.mult, ["P6", "rope1"], A(["rt1"]))
                    tt_(qrz[0:64, rb * 2, :], rt[0][0:64, :], rt[1][0:64, :], ALU.add, A(["rt0", "rt1"]), A(["qrz%d" % (rb * 2)]))
                    tt_(qrz[64:128, rb * 2 + 1, :], rt[0][64:128, :], rt[1][64:128, :], ALU.add, A(["rt0", "rt1"]), A(["qrz%d" % (rb * 2 + 1)]))

            qproj(0)
            for h in range(8):
                qb = h % 2
                rz = ((h // 2) % 2) * 2 + (h % 2)
                if h < 7:
                    qproj(h + 1)
                po, pd = P[2 + (h % 2)], P[4 + (h % 2)]
                pok, pdk = "P%d" % (2 + h % 2), "P%d" % (4 + h % 2)

                def qk(kc):
                    q0 = 0 if kc < 4 * gt else 128 * (kc - 4 * gt)
                    sb = kc % 2
                    ksl = slice(kc * 128, (kc + 1) * 128)
                    pairs = [(kn_sb[:, h, ksl], qn_sb[:, qb, q0:TT]), (kr_sb[:, ksl], qrz[:, rz, q0:TT])]
                    rd = ["kn%d_%d" % (h, kc // 4), "kr%d" % (kc // 4)] + A(["qn%d" % qb, "qrz%d" % rz])
                    if kc < 4 * gt:
                        mm(P[sb][:, q0:TT], pairs, rd, ["P%d" % sb])
                    else:
                        def fn(e, s_, pairs=pairs, sb=sb, q0=q0):
                            e.matmul(P[sb][:, q0:TT], lhsT=pairs[0][0], rhs=pairs[0][1], start=True, stop=False)
                            e.matmul(P[sb][:, q0:TT], lhsT=pairs[1][0], rhs=pairs[1][1], start=False, stop=False)
                            return e.matmul(P[sb][:, q0:q0 + 64], lhsT=maskL, rhs=onesr, start=False, stop=True)
                        if not S.dry:
                            PHASES["nmm"] += 3
                        op("pe", fn, rd + ["maskc"], ["P%d" % sb])
                    pb = kc % 3
                    act(pT[:, pb, q0:TT], P[sb][:, q0:TT], AF.Exp, ["P%d" % sb], A(["pT%d" % pb]), scale=ATT_SCALE)

                def pv(kc):
                    q0 = 0 if kc < 4 * gt else 128 * (kc - 4 * gt)
                    pb = kc % 3
                    mm1(po[:, q0:TT], v_sb[:, kc, h * 128:(h + 1) * 128], pT[:, pb, q0:TT], kc == 0, kc == nkc - 1,
                        ["v%d_%d" % (kc, h // 4)] + A(["pT%d" % pb]), [pok])
                    mm1(pd[:, q0:TT], ones, pT[:, pb, q0:TT], kc == 0, kc == nkc - 1, ["ones"] + A(["pT%d" % pb]), [pdk])

                for kc in range(nkc):
                    qk(kc)
                    if kc >= 1:
                        pv(kc - 1)
                pv(nkc - 1)
                act(lnv, pd[:], AF.Ln, [pdk], ["lnv"])
                act(rstd, lnv, AF.Exp, ["lnv"], ["rstd"], scale=-1.0)
                tt_(y_sb[:, h, tsl(tt)], po[:], rstd, ALU.mult, [pok, "rstd"], A(["y%d_%d" % (h, tt)]))

    def branch_b(half):
        switch()
        assert not bg
        RB, IB = (2, 2), (3, 7)
        GROT = (4, 5, 6)
        s_bufs = (rope_sb[:, 0, :], rope_sb[:, 1, :])
        s_keys = ("rope0", "rope1")

        def GBk(n, tt):
            return GROT[(2 * n + tt) % 3]
        a_bufs = (lnv, rstd)
        a_keys = ("lnv", "rstd")
        tiles = {}

        def tile_for(n):
            t = n // 2
            if t not in tiles:
                tiles[t] = wget(wrg_d[t], [8, 512], keep=1)
            return tiles[t]

        def stage1(n):
            wt, wk = tile_for(n)
            nn = n % 2
            par = n % 2
            par3 = n % 3
            xcol = slice(nn * 256, nn * 256 + 128)
            op("dve", lambda e, s: e.tensor_copy(out=zx[:, 0:3], in_=hist[:, n, 0:3]), ["hist%d" % n], A(["zxh"]))
            for tt in range(NTT):
                b = tt
                xk = "xc%d_%d" % (par3, tt)
                mm(P[b][:], [(wt[:, k, xcol], h_sb[:, k, tsl(tt)]) for k in range(8)], [wk] + hkeys(tt), ["P%d" % b])
                act(zx[:, 3 + tt * TT:3 + (tt + 1) * TT], P[b][:], AF.Copy, ["P%d" % b], A(["zx%d" % tt]))
                act(xcs[:, par3 * 2 + tt, :], P[b][:], AF.Identity, ["P%d" % b, "cvec"], A([xk]),
                    scale=cvec[:, C_CW + 24 + n:C_CW + 25 + n], bias=cvec[:, C_CB + n:C_CB + n + 1])
            op("dve", lambda e, s: e.tensor_copy(out=hist[:, n, 0:3], in_=zx[:, ST:ST + 3]), A(["zx1"]), ["hist%d" % n])
            for tt in range(NTT):
                zk = A(["zxh", "zx0"] if tt == 0 else ["zx0", "zx1"])
                xk = "xc%d_%d" % (par3, tt)
                xc_ = xcs[:, par3 * 2 + tt, :]
                o0 = tt * TT
                for w in range(3):
                    stt(xc_, zx[:, o0 + w:o0 + w + TT], cvec[:, C_CW + 8 * w + n:C_CW + 8 * w + n + 1], xc_, ALU.mult, ALU.add,
                        zk + A([xk]) + ["cvec"], A([xk]))
                xb_ = xcbs[:, par * 2 + tt, :]
                op("pool", lambda e, s, xb_=xb_, xc_=xc_: e.tensor_copy(out=xb_, in_=xc_), A([xk]), A(["xcb%d_%d" % (par, tt)]))

        def stage2(n):
            wt, wk = tile_for(n)
            nn = n % 2
            par = n % 2
            par3 = n % 3
            gcol = slice(nn * 256 + 128, nn * 256 + 256)
            def gate(which, tt):
                xb_ = xcbs[:, par * 2 + tt, :]
                xbk = A(["xcb%d_%d" % (par, tt)])
                bank = RB[tt] if which == 0 else IB[tt]
                mm1(P[bank][:], wrgai[:, which, n, :], xb_, True, True, ["wrgai"] + xbk, ["P%d" % bank])
            def tanh_gate(which, tt):
                if which == 0:
                    act(s_bufs[tt], P[RB[tt]][:], AF.Tanh, ["P%d" % RB[tt], "dvec"], [s_keys[tt]], scale=0.5, bias=dvec[:, 40 + n:41 + n])
                else:
                    ik = "P%d" % IB[tt]
                    act(P[IB[tt]][:], P[IB[tt]][:], AF.Tanh, [ik, "dvec"], [ik], scale=0.5, bias=dvec[:, 48 + n:49 + n])
            gate(0, 0)
            gate(1, 0)
            gate(1, 1)
            tanh_gate(0, 0)
            GB = (GBk(n, 0), GBk(n, 1))
            for tt in range(NTT):
                mm(P[GB[tt]][:], [(wt[:, k, gcol], h_sb[:, k, tsl(tt)]) for k in range(8)], [wk] + hkeys(tt), ["P%d" % GB[tt]])
            gate(0, 1)
            tanh_gate(1, 0)
            tanh_gate(1, 1)
            for tt in range(NTT):
                gk = "P%d" % GB[tt]
                act(P[GB[tt]][:], P[GB[tt]][:], AF.Gelu_apprx_tanh, [gk], [gk])
            tanh_gate(0, 1)
            for tt in range(NTT):
                act(a_bufs[tt], s_bufs[tt], AF.Exp, [s_keys[tt], "dvec"], [a_keys[tt]],
                    scale=dvec[:, 32 + n:33 + n], bias=dvec[:, 32 + n:33 + n])
            for tt in range(NTT):
                act(s_bufs[tt], a_bufs[tt], AF.Square, [a_keys[tt]], [s_keys[tt]])
            for tt in range(NTT):
                act(s_bufs[tt], s_bufs[tt], AF.Sqrt, [s_keys[tt]], [s_keys[tt]], scale=-1.0, bias=1.0)
            for tt in range(NTT):
                xk = A(["xc%d_%d" % (par3, tt)])
                xc_ = xcs[:, par3 * 2 + tt, :]
                stt(xc_, P[IB[tt]][:], 1.0, xc_, ALU.add, ALU.mult, ["P%d" % IB[tt]] + xk, xk)
            for tt in range(NTT):
                xk = A(["xc%d_%d" % (par3, tt)])
                xc_ = xcs[:, par3 * 2 + tt, :]
                ab = a_bufs[tt]
                stt(xc_, xc_, 0.5, s_bufs[tt], ALU.mult, ALU.mult, [s_keys[tt]] + xk, xk)
                op("dve", lambda e, s, xc_=xc_, ab=ab: e.tensor_tensor_scan(out=xc_, data0=ab, data1=xc_, initial=state[:, n:n + 1],
                                                                            op0=ALU.mult, op1=ALU.add),
                   [a_keys[tt], "state%d" % n] + xk, xk)
                op("dve", lambda e, s, xc_=xc_: e.tensor_copy(out=state[:, n:n + 1], in_=xc_[:, TT - 1:TT]), xk, ["state%d" % n])
                tt_(y_sb[:, n, tsl(tt)], P[GB[tt]][:], xc_, ALU.mult, ["P%d" % GB[tt]] + xk, A(["y%d_%d" % (n, tt)]))

        stage1(0)
        for n in range(8):
            if n < 7:
                stage1(n + 1)
            stage2(n)

    def branch_c(half):
        switch()
        assert not bg
        iters = [(t, hh, tt) for t in range(2) for hh in range(2) for tt in range(NTT)]
        tiles = {}

        def qproj(i):
            t, hh, tt = iters[i]
            if t not in tiles:
                tiles[t] = wget(wmq_d[t], [8, 512])
            wt, wk = tiles[t]
            par = i % 2
            for dc in range(2):
                b = dc
                col = slice(hh * 256 + dc * 128, hh * 256 + (dc + 1) * 128)
                mm(P[b][:], [(wt[:, k, col], h_sb[:, k, tsl(tt)]) for k in range(8)], [wk] + hkeys(tt), ["P%d" % b])
                evac(qm_sb[:, par * 2 + dc, :], P[b][:], ["P%d" % b], A(["qm%d" % (par * 2 + dc)]))

        def rest(i):
            t, hh, tt = iters[i]
            h = 2 * t + hh
            par = i % 2
            qk_ = A(["qm%d" % (par * 2), "qm%d" % (par * 2 + 1)])
            for mc in range(2):
                b = 2 + mc
                mm(P[b][:], [(km_sb[:, h * 2 + dc, mc * 128:(mc + 1) * 128], qm_sb[:, par * 2 + dc, :]) for dc in range(2)],
                   ["km%d" % (h * 2), "km%d" % (h * 2 + 1)] + qk_, ["P%d" % b])
                act(pm_sb[:, par * 2 + mc, :], P[b][:], AF.Exp, ["P%d" % b], A(["pm%d" % (par * 2 + mc)]), scale=MEM_SCALE)
            pmk = A(["pm%d" % (par * 2), "pm%d" % (par * 2 + 1)])
            db = 6 + par
            mm(P[db][:], [(ones, pm_sb[:, par * 2 + mc, :]) for mc in range(2)], ["ones"] + pmk, ["P%d" % db])
            for dc in range(2):
                b = 4 + dc
                col = slice(h * 256 + dc * 128, h * 256 + (dc + 1) * 128)
                mm(P[b][:], [(vm_sb[:, mc, col], pm_sb[:, par * 2 + mc, :]) for mc in range(2)],
                   ["vm%d_%d" % (mc, h // 2) for mc in range(2)] + pmk, ["P%d" % b])
            act(lnv, P[db][:], AF.Ln, ["P%d" % db], ["lnv"])
            act(rstd, lnv, AF.Exp, ["lnv"], ["rstd"], scale=-1.0)
            for dc in range(2):
                b = 4 + dc
                tt_(y_sb[:, h * 2 + dc, tsl(tt)], P[b][:], rstd, ALU.mult, ["P%d" % b, "rstd"], A(["y%d_%d" % (h * 2 + dc, tt)]))

        qproj(0)
        for i in range(len(iters)):
            if i + 1 < len(iters):
                qproj(i + 1)
            rest(i)

    def merge(br, after_tt=None):
        switch()
        cnt = 0
        for mp in range(4):
            wt, wk = wget(wmg_d[br * 4 + mp], [8, 512])
            for mi in range(2):
                m = 2 * mp + mi
                for tt in range(NTT):
                    b = cnt % 2
                    cnt += 1
                    pg, pp = P[2 * b], P[2 * b + 1]
                    mm(pg[:], [(wt[:, k, mi * 256:mi * 256 + 128], h_sb[:, k, tsl(tt)]) for k in range(8)], [wk] + hkeys(tt), ["P%d" % (2 * b)])
                    mm(pp[:], [(wt[:, k, mi * 256 + 128:mi * 256 + 256], y_sb[:, k, tsl(tt)]) for k in range(8)],
                       [wk] + A(["y%d_%d" % (k, tt) for k in range(8)]), ["P%d" % (2 * b + 1)])
                    act(sgm[b], pg[:], AF.Sigmoid, ["P%d" % (2 * b), "cvec"], A(["sgm%d" % b]),
                        bias=cvec[:, C_BG + br * 8 + m:C_BG + br * 8 + m + 1])
                    tt_(merged[:, m, tsl(tt)], sgm[b], pp[:], ALU.mult, A(["sgm%d" % b]) + ["P%d" % (2 * b + 1)], A(["mg%d_%d" % (m, tt)]))
        w0 = wget(wout_d[0], [8, 512])
        w1 = wget(wout_d[1], [8, 512], keep=1)
        for tt in range(NTT):
            for m2 in range(8):
                wt, wk = w0 if m2 < 4 else w1
                mi = m2 % 4
                b = 4 + cnt % 2
                cnt += 1
                mm(P[b][:], [(wt[:, m, mi * 128:(mi + 1) * 128], merged[:, m, tsl(tt)]) for m in range(8)],
                   [wk] + A(["mg%d_%d" % (m, tt) for m in range(8)]), ["P%d" % b])
                tt_(x_sb[:, m2, tsl(tt)], P[b][:], x_sb[:, m2, tsl(tt)], ALU.add, ["P%d" % b, "x%d_%d" % (m2, tt)], ["x%d_%d" % (m2, tt)])
            if after_tt is not None:
                after_tt(tt)

    oc = [0]

    def final_out_tt(s, half, tt, nxt):
        srcs = [x_sb[:, k, tsl(tt)] for k in range(8)]
        keys = ["x%d_%d" % (k, tt) for k in range(8)]
        for k in range(8):
            def sq_step(k=k):
                b = k % 2
                act(sq_sb[:, b, :], srcs[k], AF.Square, [keys[k]], ["sq%d" % b])
                mm1(P[7][:], ones, sq_sb[:, b, :], k == 0, k == 7, ["sq%d" % b, "ones"], ["P7"])
            bg.append((None, sq_step))
        bg.append((None, lambda: act(lnv, P[7][:], AF.Ln, ["P7"], ["lnv"], scale=1.0 / D, bias=EPS)))
        bg.append((None, lambda: act(rstd, lnv, AF.Exp, ["lnv"], ["rstd"], scale=-0.5)))
        c0 = s * SEQ + half * ST + tt * TT
        for k in range(8):
            def out_step(k=k):
                b = oc[0] % 2
                oc[0] += 1
                stt(ost[b], srcs[k], cvec[:, C_FIN + k:C_FIN + k + 1], rstd, ALU.mult, ALU.mult, [keys[k], "rstd", "cvec"], A(["ost%d" % b]))

                def st(e, sm, b=b, k=k, c0=c0):
                    return e.dma_start(out=outT[k * 128:(k + 1) * 128, c0:c0 + TT], in_=ost[b]).then_inc(sm[("dma", "o%d" % b)], 16)
                S.op("sp", st, reads=["@ost%d" % b, "GUARD"], dma="o%d" % b)
                if nxt is not None:
                    load_x_one(nxt[0], nxt[1], tt, k)
            bg.append((None, out_step))

    def load_x_one(s, half, tt, k):
        c0 = s * SEQ + half * ST + tt * TT

        def ld(e, sm):
            return e.dma_start(out=x_sb[:, k, tsl(tt)], in_=xT[k * 128:(k + 1) * 128, c0:c0 + TT]).then_inc(sm[("dma", "x%d_%d" % (k, tt))], 16)
        S.op("sp", ld, writes=["x%d_%d" % (k, tt)], dma="x%d_%d" % (k, tt))

    def load_x_tt(s, half, tt):
        for k in range(8):
            load_x_one(s, half, tt, k)

    def prologue():
        def ldc(e, sm):
            return e.dma_start(out=cvec, in_=cvec_d).then_inc(sm[("dma", "c")], 16)
        S.op("sp", ldc, writes=["cvec"], dma="c")

        def ldw(e, sm):
            return e.dma_start(out=wrgai.rearrange("p a b c -> p (a b c)"), in_=wrgai_d, max_dma_last_dim=4096).then_inc(sm[("dma", "wc")], 16)
        S.op("pool", ldw, writes=["wrgai"], dma="wc")
        op("dve", lambda e, s: e.memset(ones, 1.0), [], ["ones"])
        op("dve", lambda e, s: e.memset(maskL, 0.0), [], ["maskc"])
        op("dve", lambda e, s: e.memset(maskL[0:1, 64:128], -30000.0), ["maskc"], ["maskc"])
        op("dve", lambda e, s: e.memset(onesr, 0.0), ["maskc"], ["maskc"])
        op("dve", lambda e, s: e.memset(onesr[0:1, :], 1.0), ["maskc"], ["maskc"])
        yv, pv_ = dvec[:, 16:24], dvec[:, 24:32]
        act(yv, cvec[:, C_LAM:C_LAM + 8], AF.Exp, ["cvec"], ["dv_y"], scale=-1.0)
        ts = lambda o, i, s1, s2, o0, o1, r, w: op("dve", lambda e, s: e.tensor_scalar(out=o, in0=i, scalar1=s1, scalar2=s2, op0=o0, op1=o1), r, w)
        ts(pv_, yv, -0.25, 1.0 / 3.0, ALU.mult, ALU.add, ["dv_y"], ["dv_p"])
        tt_(pv_, pv_, yv, ALU.mult, ["dv_p", "dv_y"], ["dv_p"])
        ts(pv_, pv_, -1.0, 0.5, ALU.mult, ALU.add, ["dv_p"], ["dv_p"])
        tt_(pv_, pv_, yv, ALU.mult, ["dv_p", "dv_y"], ["dv_p"])
        ts(pv_, pv_, -1.0, 1.0, ALU.mult, ALU.add, ["dv_p"], ["dv_p"])
        tt_(pv_, pv_, yv, ALU.mult, ["dv_p", "dv_y"], ["dv_p"])
        ts(dvec[:, 0:8], pv_, -8.0, 0.0, ALU.mult, ALU.add, ["dv_p"], ["dvec"])
        ts(dvec[:, 8:16], pv_, -16.0, 0.0, ALU.mult, ALU.add, ["dv_p"], ["dvec"])
        ts(dvec[:, 32:40], pv_, -4.0, 0.0, ALU.mult, ALU.add, ["dv_p"], ["dvec"])
        ts(dvec[:, 40:48], cvec[:, C_BA:C_BA + 8], 0.5, 0.0, ALU.mult, ALU.add, ["cvec", "dvec"], ["dvec"])
        ts(dvec[:, 48:56], cvec[:, C_BI:C_BI + 8], 0.5, 0.0, ALU.mult, ALU.add, ["cvec", "dvec"], ["dvec"])

    def body():
        wstate["i"] = 0
        wstate["released"] = 0
        oc[0] = 0
        prologue()
        sts = [(s, half) for s in range(SPC) for half in range(NST)]
        for tt in range(NTT):
            load_x_tt(0, 0, tt)
        for idx, (s, half) in enumerate(sts):
            if half == 0:
                flush()
                mark("memkv")
                mem_kv(s)
                op("dve", lambda e, sm: e.memset(state, 0.0), ["state%d" % n for n in range(8)], ["state%d" % n for n in range(8)])
                op("dve", lambda e, sm: e.memset(hist.rearrange("p a b -> p (a b)"), 0.0), [], ["hist%d" % n for n in range(8)])
            mark("ffn1")
            norm_tt(C_FFN1, 0, front=True)
            norm_tt(C_FFN1, 1)
            ffn(0, lambda tt: norm_tt(C_MIX, tt))
            mark("A")
            branch_a(half)
            mark("mergeA")
            merge(0)
            mark("B")
            branch_b(half)
            mark("mergeB")
            merge(1)
            mark("C")
            branch_c(half)
            mark("mergeC")
            merge(2, lambda tt: norm_tt(C_FFN2, tt))
            mark("ffn2")

            def tail(tt, s=s, half=half, idx=idx):
                flush()
                final_out_tt(s, half, tt, sts[idx + 1] if idx + 1 < len(sts) else None)
            ffn(1, tail)
        flush()
        mark("end")

    S.dry = True
    body()
    S.dry = False
    ctx["pending_switch"] = False
    cp_ctr[0] = 0
    PHASES["nmm"] = 0
    PHASES["marks"] = []
    body()
    assert wstate["i"] == len(wplan)
    S.check_progress()
    S.emit(nc)
    import sys
    print("[mk] ops per engine", S.count, "dma", {k: v // 16 for k, v in S.dma_cnt.items()}, "wtiles", len(wplan), file=sys.stderr)
    return nc


def _tile(W):
    K, N = W.shape
    kc = K // 128
    return np.ascontiguousarray(W.reshape(kc, 128, N).transpose(1, 0, 2).reshape(128, kc * N))


def _pcol(v):
    return np.ascontiguousarray(v.reshape(-1, 128).T)


def _prep_shared(inp):
    f = lambda k: np.asarray(inp[k], dtype=np.float32)
    sh = {}
    r128 = lambda a: np.arange(a * 128, (a + 1) * 128)
    for name, wi, wd in (("wf1", "ffn1_w_in", "ffn1_w_down"), ("wf2", "ffn2_w_in", "ffn2_w_down")):
        Wi = f(wi)[0]
        Wd = f(wd)[0]
        tiles = []
        for t in range(11):
            cols = np.concatenate([r128(2 * t), HID + r128(2 * t), r128(2 * t + 1), HID + r128(2 * t + 1)])
            tiles.append(_tile(Wi[:, cols]))
        sh[name + "i"] = np.stack(tiles)
        sh[name + "da"] = np.stack([_tile(Wd[0:G0 * 128, m * 128:(m + 1) * 128]) for m in range(8)])
        sh[name + "db"] = np.stack([_tile(Wd[G0 * 128:, m * 128:(m + 1) * 128]) for m in range(8)])
    Win = f("w_in")[0]
    o_cq, o_ckv, o_kr, o_x, o_g, o_mq, o_gate = 0, 384, 640, 704, 1728, 2752, 3776
    kr = np.arange(o_kr, o_kr + 64)
    krsw = np.concatenate([kr[32:], kr[:32]])
    sh["wA1"] = _tile(Win[:, np.concatenate([np.arange(o_cq, o_cq + 384), kr, kr])])
    sh["wA2"] = _tile(Win[:, np.concatenate([np.arange(o_ckv, o_ckv + 256), krsw, krsw])])
    sh["wrg"] = np.stack([_tile(Win[:, np.concatenate([o_x + r128(2 * t), o_g + r128(2 * t), o_x + r128(2 * t + 1), o_g + r128(2 * t + 1)])])
                          for t in range(4)])
    sh["wmq"] = np.stack([_tile(Win[:, o_mq + t * 512:o_mq + (t + 1) * 512]) for t in range(2)])
    Wbr = f("w_branch")[0]
    tiles = []
    for b in range(3):
        for mp in range(4):
            parts = []
            for mi in range(2):
                m = 2 * mp + mi
                parts.append(Win[:, o_gate + b * 1024 + m * 128:o_gate + b * 1024 + (m + 1) * 128])
                parts.append(Wbr[b][:, m * 128:(m + 1) * 128])
            tiles.append(_tile(np.concatenate(parts, axis=1)))
    sh["wmg"] = np.stack(tiles)
    Wo = f("w_out")[0]
    sh["wout"] = np.stack([_tile(Wo[:, t * 512:(t + 1) * 512]) for t in range(2)])
    Wuq = f("w_uq")[0]
    sh["wuqn"] = _tile(Wuq[:, np.concatenate([h * 192 + np.arange(128) for h in range(8)])])
    rope_c = np.concatenate([h * 192 + 128 + np.arange(64) for h in range(8)])
    rope_sw = np.concatenate([h * 192 + 128 + np.concatenate([np.arange(32, 64), np.arange(0, 32)]) for h in range(8)])
    sh["wuqr"] = _tile(Wuq[:, np.concatenate([rope_c, rope_sw])])
    Wukv = f("w_ukv")[0]
    sh["wukv"] = _tile(Wukv[:, np.concatenate([h * 256 + np.arange(128) for h in range(8)] + [h * 256 + 128 + np.arange(128) for h in range(8)])])
    Wm = f("w_mem_kv")[0]
    sh["wmkv"] = np.stack([_tile(Wm[:, t * 512:(t + 1) * 512]) for t in range(4)])
    rg = np.stack([f("w_rg_a")[0], f("w_rg_i")[0]])
    sh["wrgai"] = np.ascontiguousarray(rg.transpose(2, 0, 1, 3).reshape(128, 2048))
    cv = np.zeros((128, NCV), np.float32)
    cv[:, C_FFN1:C_FFN1 + 8] = _pcol(f("ffn1_norm")[0])
    cv[:, C_MIX:C_MIX + 8] = _pcol(f("mix_norm")[0])
    cv[:, C_FFN2:C_FFN2 + 8] = _pcol(f("ffn2_norm")[0])
    cv[:, C_FIN:C_FIN + 8] = _pcol(f("final_norm"))
    cv[:, C_MEM:C_MEM + 8] = _pcol(f("mem_norm")[0])
    cv[:, C_QN:C_QN + 3] = _pcol(f("q_norm")[0])
    cv[:, C_KVN:C_KVN + 2] = _pcol(f("kv_norm")[0])
    cw = f("conv_w")[0]
    for w in range(4):
        cv[:, C_CW + 8 * w:C_CW + 8 * w + 8] = _pcol(cw[w, 0])
    cv[:, C_CB:C_CB + 8] = _pcol(f("conv_b")[0])
    cv[:, C_BA:C_BA + 8] = f("b_rg_a")[0].T
    cv[:, C_BI:C_BI + 8] = f("b_rg_i")[0].T
    cv[:, C_LAM:C_LAM + 8] = _pcol(f("lru_lambda")[0])
    bg = f("b_gate")[0]
    for b in range(3):
        cv[:, C_BG + 8 * b:C_BG + 8 * b + 8] = _pcol(bg[b])
    sh["cvec"] = cv
    pos = np.arange(SEQ, dtype=np.float32)
    inv_freq = (1.0 / (np.float32(10000.0) ** (np.arange(0, 64, 2, dtype=np.float32) / np.float32(64.0)))).astype(np.float32)
    ang = (pos[:, None] * inv_freq[None, :]).astype(np.float32)
    cos, sin = np.cos(ang).astype(np.float32), np.sin(ang).astype(np.float32)
    rope = np.zeros((128, 2, SEQ), np.float32)
    for p in range(128):
        j = p % 64
        rope[p, 0] = cos[:, j % 32]
        rope[p, 1] = sin[:, j % 32] * (-1.0 if j < 32 else 1.0)
    sh["rope"] = rope
    return sh


_NC_CACHE = {}


def kernel(**inputs):
    x = np.asarray(inputs["x"], dtype=np.float32)
    mem = np.asarray(inputs["mem"], dtype=np.float32)
    sh = _prep_shared(inputs)
    if "nc" not in _NC_CACHE:
        _NC_CACHE["nc"] = build_program()
    nc = _NC_CACHE["nc"]
    in_maps = []
    for c in range(NCORES):
        d = dict(sh)
        d["xT"] = np.ascontiguousarray(x[c * SPC:(c + 1) * SPC].transpose(2, 0, 1).reshape(D, SPC * SEQ))
        d["memT"] = np.ascontiguousarray(mem[c * SPC:(c + 1) * SPC].transpose(2, 0, 1).reshape(D, SPC * NMEM))
        in_maps.append(d)
    res = run_bass_kernel_spmd(nc, in_maps, core_ids=list(range(NCORES)))
    out = np.empty((NB, SEQ, D), np.float32)
    for c in range(NCORES):
        o = np.asarray(res.results[c]["outT"]).reshape(D, SPC, SEQ)
        out[c * SPC:(c + 1) * SPC] = o.transpose(1, 2, 0)
    return out
```

```python
import numpy as np
import concourse.bass as bass
import concourse.mybir as mybir
from concourse.bass_utils import run_bass_kernel_spmd

F32 = mybir.dt.float32
BF16 = mybir.dt.bfloat16
AF = mybir.ActivationFunctionType
ALU = mybir.AluOpType


class Sched:
    ENGS = ("pe", "act", "dve", "pool", "sp")

    def __init__(self):
        self.ops = {e: [] for e in self.ENGS}
        self.count = {e: 0 for e in self.ENGS}
        self.last_writer = {}
        self.readers = {}
        self.waited = {e: {} for e in self.ENGS}
        self.dma_cnt = {}
        self.dry = False

    def op(self, eng, fn, reads=(), writes=(), dma=None, ndma=1):
        if self.dry:
            return None
        deps = {}
        for r in reads:
            t = self.last_writer.get(r)
            if t is not None:
                deps[t[0]] = max(deps.get(t[0], 0), t[1])
        for w in writes:
            t = self.last_writer.get(w)
            if t is not None:
                deps[t[0]] = max(deps.get(t[0], 0), t[1])
            for t in self.readers.get(w, ()):
                deps[t[0]] = max(deps.get(t[0], 0), t[1])
        if dma is None:
            self.count[eng] += 1
            tok = (("eng", eng), self.count[eng])
        else:
            self.dma_cnt[dma] = self.dma_cnt.get(dma, 0) + 16 * ndma
            tok = (("dma", dma), self.dma_cnt[dma])
        waits = []
        wd = self.waited[eng]
        for sk, v in deps.items():
            if wd.get(sk, 0) >= v:
                continue
            wd[sk] = v
            waits.append((sk, v))
        self.ops[eng].append((fn, waits, dma is None, tok))
        for r in reads:
            self.readers.setdefault(r, []).append(tok)
        for w in writes:
            self.last_writer[w] = tok
            self.readers[w] = []
        return tok

    def check_progress(self):
        pos = {e: 0 for e in self.ENGS}
        sem = {}
        while True:
            progress = False
            for e in self.ENGS:
                q = self.ops[e]
                while pos[e] < len(q):
                    fn, waits, inc, tok = q[pos[e]]
                    if all(sem.get(sk, 0) >= v for sk, v in waits):
                        sem[tok[0]] = max(sem.get(tok[0], 0), tok[1])
                        pos[e] += 1
                        progress = True
                    else:
                        break
            if not progress:
                break
        stuck = {e: (pos[e], len(self.ops[e])) for e in self.ENGS if pos[e] < len(self.ops[e])}
        assert not stuck, "schedule deadlocks: %r" % stuck

    def emit(self, nc, final_waits=()):
        import contextlib
        dma_keys = sorted(self.dma_cnt.keys(), key=str)
        with contextlib.ExitStack() as es:
            sems = {}
            for e in self.ENGS:
                sems[("eng", e)] = es.enter_context(nc.semaphore("s_" + e))
            for k in dma_keys:
                sems[("dma", k)] = es.enter_context(nc.semaphore("d_" + str(k)))
            block = es.enter_context(nc.Block())
            ops = self.ops

            def run(engname, eng):
                mysem = sems[("eng", engname)]
                for fn, waits, inc, _tok in ops[engname]:
                    for sk, v in waits:
                        eng.wait_ge(sems[sk], v)
                    ins = fn(eng, sems)
                    if inc:
                        ins.then_inc(mysem, 1)

            @block.tensor
            def _(eng):
                run("pe", eng)

            @block.scalar
            def _(eng):
                run("act", eng)

            @block.vector
            def _(eng):
                run("dve", eng)

            @block.gpsimd
            def _(eng):
                run("pool", eng)

            @block.sync
            def _(eng):
                run("sp", eng)
                for k in dma_keys:
                    eng.wait_ge(sems[("dma", k)], self.dma_cnt[k])
                for e in ("pe", "act", "dve", "pool"):
                    if self.count[e]:
                        eng.wait_ge(sems[("eng", e)], self.count[e])


D = 1024
SEQ = 2048
NB = 16
NCORES = 8
SPC = NB // NCORES
ST = 1024
TT = 512
NTT = ST // TT
NST = SEQ // ST
HID = 2816
NHC = HID // 128
G0 = 12
NMEM = 256
EPS = 1e-6
ATT_SCALE = 192.0 ** -0.5
MEM_SCALE = 256.0 ** -0.5
NSLOT = 4
SLOT_ELEMS = 4096

C_FFN1, C_MIX, C_FFN2, C_FIN, C_MEM = 0, 8, 16, 24, 32
C_QN, C_KVN = 40, 43
C_CW, C_CB = 45, 77
C_BA, C_BI, C_LAM = 85, 93, 101
C_BG = 109
NCV = 136


PHASES = {"nmm": 0, "marks": []}


class Arena:
    def __init__(self, nc, name, nbytes):
        self.hb = nc.alloc_sbuf_tensor(name, [128, nbytes // 2], BF16)
        self.hf = self.hb.bitcast(F32)
        self.nbytes = nbytes
        self.off = 0

    def alloc(self, shape, dt, at=None):
        n = int(np.prod(shape))
        nb = n * (4 if dt == F32 else 2)
        off = self.off if at is None else at
        assert off % 32 == 0 and off + nb <= self.nbytes, (off, nb, self.nbytes)
        if dt == F32:
            ap = self.hf[:, off // 4: off // 4 + n]
        else:
            ap = self.hb[:, off // 2: off // 2 + n]
        if len(shape) == 2:
            ap = ap.rearrange("p (a b) -> p a b", a=shape[0])
        elif len(shape) == 3:
            ap = ap.rearrange("p (a b c) -> p a b c", a=shape[0], b=shape[1])
        if at is None:
            self.off = off + (nb + 31) // 32 * 32
        return ap


def build_program():
    nc = bass.Bass("TRN2", target_bir_lowering=False)

    def din(name, shape):
        return nc.dram_tensor(name, list(shape), F32, kind="ExternalInput").ap()

    xT = din("xT", [D, SPC * SEQ])
    memT = din("memT", [D, SPC * NMEM])
    cvec_d = din("cvec", [128, NCV])
    rope_d = din("rope", [128, 2, SEQ])
    wf_i = [din("wf1i", [11, 128, 4096]), din("wf2i", [11, 128, 4096])]
    wf_da = [din("wf1da", [8, 128, G0 * 128]), din("wf2da", [8, 128, G0 * 128])]
    wf_db = [din("wf1db", [8, 128, (NHC - G0) * 128]), din("wf2db", [8, 128, (NHC - G0) * 128])]
    wA1_d = din("wA1", [128, 4096])
    wA2_d = din("wA2", [128, 3072])
    wrg_d = din("wrg", [4, 128, 4096])
    wmq_d = din("wmq", [2, 128, 4096])
    wmg_d = din("wmg", [12, 128, 4096])
    wout_d = din("wout", [2, 128, 4096])
    wuqn_d = din("wuqn", [128, 3072])
    wuqr_d = din("wuqr", [128, 3072])
    wukv_d = din("wukv", [128, 4096])
    wmkv_d = din("wmkv", [4, 128, 4096])
    wrgai_d = din("wrgai", [128, 2048])
    outT = nc.dram_tensor("outT", [D, SPC * SEQ], F32, kind="ExternalOutput").ap()

    total = nc.sbuf_bytes_remaining
    main = Arena(nc, "main", (total - 256) // 64 * 64)
    cvec = main.alloc([1, NCV], F32)[:, 0, :]
    dvec = main.alloc([1, 56], F32)[:, 0, :]
    ones = main.alloc([1, 128], BF16)[:, 0, :]
    maskL = main.alloc([1, 128], BF16)[:, 0, :]
    onesr = main.alloc([1, 64], BF16)[:, 0, :]
    wrgai = main.alloc([2, 8, 128], BF16)
    state = main.alloc([1, 8], F32)[:, 0, :]
    hist = main.alloc([8, 4], F32)
    ring = [main.alloc([1, SLOT_ELEMS], BF16)[:, 0, :] for _ in range(NSLOT)]
    x_sb = main.alloc([8, ST], F32)
    h_sb = main.alloc([8, ST], BF16)
    kn_sb = main.alloc([8, SEQ], BF16)
    v_sb = main.alloc([16, 1024], BF16)
    kr_sb = main.alloc([1, SEQ], BF16)[:, 0, :]
    km_sb = main.alloc([8, NMEM], BF16)
    vm_sb = main.alloc([2, 1024], BF16)
    sq_sb = main.alloc([2, TT], BF16)
    lnv = main.alloc([1, TT], F32)[:, 0, :]
    rstd = main.alloc([1, TT], F32)[:, 0, :]
    rope_sb = main.alloc([2, TT], F32)
    A0 = main.off
    asz = main.nbytes - A0
    assert asz >= 36864, asz

    def aa(shape, dt, rel):
        return main.alloc(shape, dt, at=A0 + rel)

    hid = aa([G0, ST], BF16, 0)
    sg = [aa([1, TT], F32, 24576 + 2048 * i)[:, 0, :] for i in range(2)]
    ost = [aa([1, TT], F32, 28672 + 2048 * i)[:, 0, :] for i in range(2)]
    memx = aa([8, NMEM], F32, 0)
    memn = aa([8, NMEM], BF16, 8192)
    y_sb = aa([8, ST], BF16, 0)
    U = 16384
    cqn = aa([3, TT], BF16, U)
    ckvn = aa([2, TT], BF16, U + 3072)
    qn_sb = aa([2, TT], BF16, U + 5120)
    qrz = aa([4, TT], BF16, U + 7168)
    pT = aa([3, TT], BF16, U + 11264)
    rt = [aa([1, TT], F32, U + 14336 + 2048 * i)[:, 0, :] for i in range(2)]
    merged = aa([8, ST], BF16, U)
    sgm = [aa([1, TT], F32, U + 16384 + 2048 * i)[:, 0, :] for i in range(2)]
    zx = aa([1, 1032], F32, U)[:, 0, :]
    xcs = aa([6, TT], F32, U + 4128)
    xcbs = aa([4, TT], BF16, U + 16416)
    qm_sb = aa([4, TT], BF16, U)
    pm_sb = aa([4, TT], BF16, U + 4096)

    P = [nc.alloc_psum_tensor("ps%d" % i, [128, TT], F32) for i in range(8)]

    S = Sched()
    ctx = {"pending_switch": False}

    def A(keys):
        return ["@" + k for k in keys]

    def op(eng, fn, reads=(), writes=()):
        reads = list(reads)
        writes = list(writes)
        if any(k.startswith("@") for k in reads + writes):
            if ctx["pending_switch"]:
                writes.append("GUARD")
                ctx["pending_switch"] = False
            else:
                reads.append("GUARD")
        return S.op(eng, fn, reads, writes)

    def switch():
        ctx["pending_switch"] = True

    def mark(name):
        if not S.dry:
            PHASES["marks"].append((name, PHASES["nmm"]))

    import collections
    bg = collections.deque()

    def drain(n=3):
        for _ in range(n):
            if not bg:
                return
            bg.popleft()[1]()

    def flush(tt=None):
        if tt is None:
            while bg:
                bg.popleft()[1]()
            return
        last = -1
        for i, (t, _) in enumerate(bg):
            if t == tt:
                last = i
        for _ in range(last + 1):
            bg.popleft()[1]()

    def mm(out, pairs, reads, writes):
        n = len(pairs)
        if not S.dry:
            PHASES["nmm"] += n

        def fn(e, s):
            for i, (l, r) in enumerate(pairs):
                ins = e.matmul(out, lhsT=l, rhs=r, start=(i == 0), stop=(i == n - 1))
            return ins
        op("pe", fn, reads, writes)
        drain()

    def mm1(out, l, r, start, stop, reads, writes):
        if not S.dry:
            PHASES["nmm"] += 1
        op("pe", lambda e, s: e.matmul(out, lhsT=l, rhs=r, start=start, stop=stop), reads, writes)

    def act(out, in_, func, reads, writes, **kw):
        op("act", lambda e, s: e.activation(out=out, in_=in_, func=func, **kw), reads, writes)

    def stt(out, in0, scalar, in1, op0, op1, reads, writes):
        op("dve", lambda e, s: e.scalar_tensor_tensor(out=out, in0=in0, scalar=scalar, in1=in1, op0=op0, op1=op1), reads, writes)

    def tt_(out, in0, in1, o, reads, writes):
        op("dve", lambda e, s: e.tensor_tensor(out=out, in0=in0, in1=in1, op=o), reads, writes)

    cp_ctr = [0]

    def evac(out, in_, reads, writes):
        cp_ctr[0] += 1
        if cp_ctr[0] % 2:
            act(out, in_, AF.Copy, reads, writes)
        else:
            op("dve", lambda e, s: e.tensor_copy(out=out, in_=in_), reads, writes)

    wplan = []
    wstate = {"i": 0, "loaded": 0, "released": 0}

    def issue_load(j):
        dram = wplan[j]
        slot = j % NSLOT
        n = dram.shape[-1]

        def fn(e, s):
            return e.dma_start(out=ring[slot][:, 0:n], in_=dram, max_dma_last_dim=4096).then_inc(s[("dma", "w%d" % slot)], 16)
        S.op("pool", fn, writes=["W%d" % slot], dma="w%d" % slot)

    def wget(dram, shape, keep=0):
        i = wstate["i"]
        wstate["i"] += 1
        assert keep < NSLOT
        wstate["released"] = max(wstate["released"], i - keep)
        if S.dry:
            wplan.append(dram)
        else:
            while wstate["loaded"] < min(len(wplan), wstate["released"] + NSLOT):
                issue_load(wstate["loaded"])
                wstate["loaded"] += 1
            assert wstate["loaded"] > i
        slot = i % NSLOT
        n = int(np.prod(shape))
        ap = ring[slot][:, 0:n]
        if len(shape) == 2:
            ap = ap.rearrange("p (a b) -> p a b", a=shape[0])
        return ap, "W%d" % slot

    def tsl(tt):
        return slice(tt * TT, (tt + 1) * TT)

    def rmsnorm(srcs, src_keys, gcol, nfeat, ntok, outs, out_keys):
        flush()
        for st_ in rmsnorm_steps(srcs, src_keys, gcol, nfeat, ntok, outs, out_keys):
            st_()

    def rmsnorm_steps(srcs, src_keys, gcol, nfeat, ntok, outs, out_keys):
        nk = len(srcs)
        steps = []
        for k in range(nk):
            def sq_step(k=k):
                b = k % 2
                act(sq_sb[:, b, 0:ntok], srcs[k], AF.Square, [src_keys[k]], ["sq%d" % b])
                mm1(P[7][:, 0:ntok], ones, sq_sb[:, b, 0:ntok], k == 0, k == nk - 1, ["sq%d" % b, "ones"], ["P7"])
            steps.append(sq_step)
        steps.append(lambda: act(lnv[:, 0:ntok], P[7][:, 0:ntok], AF.Ln, ["P7"], ["lnv"], scale=1.0 / nfeat, bias=EPS))
        steps.append(lambda: act(rstd[:, 0:ntok], lnv[:, 0:ntok], AF.Exp, ["lnv"], ["rstd"], scale=-0.5))
        for k in range(nk):
            def mul_step(k=k):
                stt(outs[k], srcs[k], cvec[:, gcol + k:gcol + k + 1], rstd[:, 0:ntok], ALU.mult, ALU.mult,
                    [src_keys[k], "rstd", "cvec"], [out_keys[k]])
            steps.append(mul_step)
        return steps

    def norm_tt(gcol, tt, front=False):
        steps = rmsnorm_steps([x_sb[:, k, tsl(tt)] for k in range(8)], ["x%d_%d" % (k, tt) for k in range(8)], gcol, D, TT,
                              [h_sb[:, k, tsl(tt)] for k in range(8)], ["h%d_%d" % (k, tt) for k in range(8)])
        if front:
            bg.extendleft(reversed([(tt, st_) for st_ in steps]))
        else:
            bg.extend([(tt, st_) for st_ in steps])

    def hkeys(tt):
        flush(tt)
        return ["h%d_%d" % (k, tt) for k in range(8)]

    def ffn(fi, after_tt):
        switch()
        cnt = 0
        for grp in range(2):
            j0 = 0 if grp == 0 else G0
            ng = G0 if grp == 0 else NHC - G0
            def up(t, jj, tt, wt, wk):
                nonlocal cnt
                jl = 2 * t + jj - j0
                b = cnt % 2
                cnt += 1
                pg, pu = P[2 * b], P[2 * b + 1]
                mm(pg[:], [(wt[:, k, jj * 256:jj * 256 + 128], h_sb[:, k, tsl(tt)]) for k in range(8)],
                   [wk] + hkeys(tt), ["P%d" % (2 * b)])
                mm(pu[:], [(wt[:, k, jj * 256 + 128:jj * 256 + 256], h_sb[:, k, tsl(tt)]) for k in range(8)],
                   [wk] + hkeys(tt), ["P%d" % (2 * b + 1)])
                act(sg[b], pg[:], AF.Silu, ["P%d" % (2 * b)], A(["sg%d" % b]))
                tt_(hid[:, jl, tsl(tt)], sg[b], pu[:], ALU.mult, A(["sg%d" % b]) + ["P%d" % (2 * b + 1)],
                    A(["hid%d_%d" % (jl, tt)]))
            tiles_g = range(j0 // 2, (j0 + ng) // 2)
            tiles_g = list(tiles_g)
            if grp == 0:
                wa = wget(wf_i[fi][tiles_g[0]], [8, 512])
                wb = wget(wf_i[fi][tiles_g[1]], [8, 512], keep=1)
                for tt in range(NTT):
                    for t, (wt, wk) in ((tiles_g[0], wa), (tiles_g[1], wb)):
                        for jj in range(2):
                            up(t, jj, tt, wt, wk)
                tiles_g = tiles_g[2:]
            for t in tiles_g:
                wt, wk = wget(wf_i[fi][t], [8, 512])
                for jj in range(2):
                    for tt in range(NTT):
                        up(t, jj, tt, wt, wk)
            wd = wf_da[fi] if grp == 0 else wf_db[fi]

            def down(m, tt, wt, wk):
                nonlocal cnt
                b = 4 + (cnt % 2)
                cnt += 1
                mm(P[b][:], [(wt[:, j, :], hid[:, j, tsl(tt)]) for j in range(ng)],
                   [wk] + A(["hid%d_%d" % (j, tt) for j in range(ng)]), ["P%d" % b])
                stt(x_sb[:, m, tsl(tt)], P[b][:], 0.5, x_sb[:, m, tsl(tt)], ALU.mult, ALU.add,
                    ["P%d" % b, "x%d_%d" % (m, tt)], ["x%d_%d" % (m, tt)])
            if grp == 0:
                for m in range(8):
                    wt, wk = wget(wd[m], [ng, 128])
                    for tt in range(NTT):
                        down(m, tt, wt, wk)
            else:
                for tt in range(NTT):
                    for m in range(8):
                        wt, wk = wget(wd[m], [ng, 128])
                        down(m, tt, wt, wk)
                    after_tt(tt)

    def mem_kv(s):
        switch()

        def ld(e, sm):
            return e.dma_start(out=memx, in_=memT.rearrange("(k p) t -> p k t", p=128)[:, :, s * NMEM:(s + 1) * NMEM]).then_inc(sm[("dma", "mem")], 16)
        S.op("sp", ld, reads=["GUARD"], writes=["@memx", "GUARD"], dma="mem")
        ctx["pending_switch"] = False
        rmsnorm([memx[:, k, :] for k in range(8)], A(["memx"] * 8), C_MEM, D, NMEM,
                [memn[:, k, :] for k in range(8)], A(["memn%d" % k for k in range(8)]))
        mk = A(["memn%d" % k for k in range(8)])
        c2 = 0
        for t in range(2):
            wt, wk = wget(wmkv_d[t], [8, 512])
            for c in range(4):
                b = c2 % 2
                c2 += 1
                mm(P[b][:, 0:NMEM], [(wt[:, k, c * 128:(c + 1) * 128], memn[:, k, :]) for k in range(8)], [wk] + mk, ["P%d" % b])
                evac(km_sb[:, t * 4 + c, :], P[b][:, 0:NMEM], ["P%d" % b], ["km%d" % (t * 4 + c)])
        for t in range(2):
            wt, wk = wget(wmkv_d[2 + t], [8, 512])
            for mc in range(2):
                b = c2 % 2
                c2 += 1
                mm(P[b][:], [(memn[:, k, mc * 128:(mc + 1) * 128], wt[:, k, :]) for k in range(8)], [wk] + mk, ["P%d" % b])
                evac(vm_sb[:, mc, t * 512:(t + 1) * 512], P[b][:], ["P%d" % b], ["vm%d_%d" % (mc, t)])

    def branch_a(half):
        switch()
        op("dve", lambda e, s: e.memset(qrz.rearrange("p a b -> p (a b)"), 0.0), [], A(["qrz0", "qrz1", "qrz2", "qrz3"]))
        for tt in range(NTT):
            gt = half * NTT + tt
            tok = slice(gt * TT, (gt + 1) * TT)
            hk = hkeys(tt)

            mark("A1")

            def ldr(e, sm, gt=gt):
                return e.dma_start(out=rope_sb, in_=rope_d[:, :, gt * TT:(gt + 1) * TT]).then_inc(sm[("dma", "rope")], 16)
            S.op("sp", ldr, writes=["rope0", "rope1"], dma="rope")
            w1, w1k = wget(wA1_d, [8, 512])
            for c, b in ((0, 0), (1, 1), (2, 2), (3, 5)):
                mm(P[b][:], [(w1[:, k, c * 128:(c + 1) * 128], h_sb[:, k, tsl(tt)]) for k in range(8)], [w1k] + hk, ["P%d" % b])
            w2, w2k = wget(wA2_d, [8, 384])
            for c, b in ((0, 3), (1, 4), (2, 6)):
                mm(P[b][:], [(w2[:, k, c * 128:(c + 1) * 128], h_sb[:, k, tsl(tt)]) for k in range(8)], [w2k] + hk, ["P%d" % b])
            rmsnorm([P[k][:] for k in range(3)], ["P0", "P1", "P2"], C_QN, 384, TT,
                    [cqn[:, k, :] for k in range(3)], A(["cqn%d" % k for k in range(3)]))
            rmsnorm([P[3 + k][:] for k in range(2)], ["P3", "P4"], C_KVN, 256, TT,
                    [ckvn[:, k, :] for k in range(2)], A(["ckvn%d" % k for k in range(2)]))
            tt_(rt[0], P[5][:], rope_sb[:, 0, :], ALU.mult, ["P5", "rope0"], A(["rt0"]))
            tt_(rt[1], P[6][:], rope_sb[:, 1, :], ALU.mult, ["P6", "rope1"], A(["rt1"]))
            tt_(kr_sb[:, tok], rt[0], rt[1], ALU.add, A(["rt0", "rt1"]), ["kr%d" % gt])
            wkv, wkvk = wget(wukv_d, [2, 2048])
            ck = A(["ckvn0", "ckvn1"])
            c2 = 0
            for h in range(8):
                b = c2 % 6
                c2 += 1
                mm(P[b][:], [(wkv[:, k, h * 128:(h + 1) * 128], ckvn[:, k, :]) for k in range(2)], [wkvk] + ck, ["P%d" % b])
                evac(kn_sb[:, h, tok], P[b][:], ["P%d" % b], ["kn%d_%d" % (h, gt)])
            for tk in range(4):
                for hf in range(2):
                    b = c2 % 6
                    c2 += 1
                    mm(P[b][:], [(ckvn[:, k, tk * 128:(tk + 1) * 128], wkv[:, k, 1024 + hf * 512:1024 + (hf + 1) * 512]) for k in range(2)],
                       [wkvk] + ck, ["P%d" % b])
                    evac(v_sb[:, gt * 4 + tk, hf * 512:(hf + 1) * 512], P[b][:], ["P%d" % b], ["v%d_%d" % (gt * 4 + tk, hf)])
            mark("ATT")
            assert not bg
            wqn, wqnk = wget(wuqn_d, [3, 1024])
            wqr, wqrk = wget(wuqr_d, [3, 1024], keep=1)
            cq = A(["cqn0", "cqn1", "cqn2"])
            nkc = 4 * gt + 4

            def qproj(h):
                qb = h % 2
                hp = h // 2
                rb = hp % 2
                mm(P[6][:], [(wqn[:, k, h * 128:(h + 1) * 128], cqn[:, k, :]) for k in range(3)], [wqnk] + cq, ["P6"])
                evac(qn_sb[:, qb, :], P[6][:], ["P6"], A(["qn%d" % qb]))
                if h % 2 == 0:
                    mm(P[7][:], [(wqr[:, k, hp * 128:(hp + 1) * 128], cqn[:, k, :]) for k in range(3)], [wqrk] + cq, ["P7"])
                    mm(P[6][:], [(wqr[:, k, 512 + hp * 128:512 + (hp + 1) * 128], cqn[:, k, :]) for k in range(3)], [wqrk] + cq, ["P6"])
                    tt_(rt[0], P[7][:], rope_sb[:, 0, :], ALU.mult, ["P7", "rope0"], A(["rt0"]))
                    tt_(rt[1], P[6][:], rope_sb[:, 1, :], ALU.mult, ["P6", "rope1"], A(["rt1"]))
                    tt_(qrz[0:64, rb * 2, :], rt[0][0:64, :], rt[1][0:64, :], ALU.add, A(["rt0", "rt1"]), A(["qrz%d" % (rb * 2)]))
                    tt_(qrz[64:128, rb * 2 + 1, :], rt[0][64:128, :], rt[1][64:128, :], ALU.add, A(["rt0", "rt1"]), A(["qrz%d" % (rb * 2 + 1)]))

            qproj(0)
            for h in range(8):
                qb = h % 2
                rz = ((h // 2) % 2) * 2 + (h % 2)
                if h < 7:
                    qproj(h + 1)
                po, pd = P[2 + (h % 2)], P[4 + (h % 2)]
                pok, pdk = "P%d" % (2 + h % 2), "P%d" % (4 + h % 2)

                def qk(kc):
                    q0 = 0 if kc < 4 * gt else 128 * (kc - 4 * gt)
                    sb = kc % 2
                    ksl = slice(kc * 128, (kc + 1) * 128)
                    pairs = [(kn_sb[:, h, ksl], qn_sb[:, qb, q0:TT]), (kr_sb[:, ksl], qrz[:, rz, q0:TT])]
                    rd = ["kn%d_%d" % (h, kc // 4), "kr%d" % (kc // 4)] + A(["qn%d" % qb, "qrz%d" % rz])
                    if kc < 4 * gt:
                        mm(P[sb][:, q0:TT], pairs, rd, ["P%d" % sb])
                    else:
                        def fn(e, s_, pairs=pairs, sb=sb, q0=q0):
                            e.matmul(P[sb][:, q0:TT], lhsT=pairs[0][0], rhs=pairs[0][1], start=True, stop=False)
                            e.matmul(P[sb][:, q0:TT], lhsT=pairs[1][0], rhs=pairs[1][1], start=False, stop=False)
                            return e.matmul(P[sb][:, q0:q0 + 64], lhsT=maskL, rhs=onesr, start=False, stop=True)
                        if not S.dry:
                            PHASES["nmm"] += 3
                        op("pe", fn, rd + ["maskc"], ["P%d" % sb])
                    pb = kc % 3
                    act(pT[:, pb, q0:TT], P[sb][:, q0:TT], AF.Exp, ["P%d" % sb], A(["pT%d" % pb]), scale=ATT_SCALE)

                def pv(kc):
                    q0 = 0 if kc < 4 * gt else 128 * (kc - 4 * gt)
                    pb = kc % 3
                    mm1(po[:, q0:TT], v_sb[:, kc, h * 128:(h + 1) * 128], pT[:, pb, q0:TT], kc == 0, kc == nkc - 1,
                        ["v%d_%d" % (kc, h // 4)] + A(["pT%d" % pb]), [pok])
                    mm1(pd[:, q0:TT], ones, pT[:, pb, q0:TT], kc == 0, kc == nkc - 1, ["ones"] + A(["pT%d" % pb]), [pdk])

                for kc in range(nkc):
                    qk(kc)
                    if kc >= 1:
                        pv(kc - 1)
                pv(nkc - 1)
                act(lnv, pd[:], AF.Ln, [pdk], ["lnv"])
                act(rstd, lnv, AF.Exp, ["lnv"], ["rstd"], scale=-1.0)
                tt_(y_sb[:, h, tsl(tt)], po[:], rstd, ALU.mult, [pok, "rstd"], A(["y%d_%d" % (h, tt)]))

    def branch_b(half):
        switch()
        assert not bg
        RB, IB = (2, 2), (3, 7)
        GROT = (4, 5, 6)
        s_bufs = (rope_sb[:, 0, :], rope_sb[:, 1, :])
        s_keys = ("rope0", "rope1")

        def GBk(n, tt):
            return GROT[(2 * n + tt) % 3]
        a_bufs = (lnv, rstd)
        a_keys = ("lnv", "rstd")
        tiles = {}

        def tile_for(n):
            t = n // 2
            if t not in tiles:
                tiles[t] = wget(wrg_d[t], [8, 512], keep=1)
            return tiles[t]

        def stage1(n):
            wt, wk = tile_for(n)
            nn = n % 2
            par = n % 2
            par3 = n % 3
            xcol = slice(nn * 256, nn * 256 + 128)
            op("dve", lambda e, s: e.tensor_copy(out=zx[:, 0:3], in_=hist[:, n, 0:3]), ["hist%d" % n], A(["zxh"]))
            for tt in range(NTT):
                b = tt
                xk = "xc%d_%d" % (par3, tt)
                mm(P[b][:], [(wt[:, k, xcol], h_sb[:, k, tsl(tt)]) for k in range(8)], [wk] + hkeys(tt), ["P%d" % b])
                act(zx[:, 3 + tt * TT:3 + (tt + 1) * TT], P[b][:], AF.Copy, ["P%d" % b], A(["zx%d" % tt]))
                act(xcs[:, par3 * 2 + tt, :], P[b][:], AF.Identity, ["P%d" % b, "cvec"], A([xk]),
                    scale=cvec[:, C_CW + 24 + n:C_CW + 25 + n], bias=cvec[:, C_CB + n:C_CB + n + 1])
            op("dve", lambda e, s: e.tensor_copy(out=hist[:, n, 0:3], in_=zx[:, ST:ST + 3]), A(["zx1"]), ["hist%d" % n])
            for tt in range(NTT):
                zk = A(["zxh", "zx0"] if tt == 0 else ["zx0", "zx1"])
                xk = "xc%d_%d" % (par3, tt)
                xc_ = xcs[:, par3 * 2 + tt, :]
                o0 = tt * TT
                for w in range(3):
                    stt(xc_, zx[:, o0 + w:o0 + w + TT], cvec[:, C_CW + 8 * w + n:C_CW + 8 * w + n + 1], xc_, ALU.mult, ALU.add,
                        zk + A([xk]) + ["cvec"], A([xk]))
                xb_ = xcbs[:, par * 2 + tt, :]
                op("pool", lambda e, s, xb_=xb_, xc_=xc_: e.tensor_copy(out=xb_, in_=xc_), A([xk]), A(["xcb%d_%d" % (par, tt)]))

        def stage2(n):
            wt, wk = tile_for(n)
            nn = n % 2
            par = n % 2
            par3 = n % 3
            gcol = slice(nn * 256 + 128, nn * 256 + 256)
            def gate(which, tt):
                xb_ = xcbs[:, par * 2 + tt, :]
                xbk = A(["xcb%d_%d" % (par, tt)])
                bank = RB[tt] if which == 0 else IB[tt]
                mm1(P[bank][:], wrgai[:, which, n, :], xb_, True, True, ["wrgai"] + xbk, ["P%d" % bank])
            def tanh_gate(which, tt):
                if which == 0:
                    act(s_bufs[tt], P[RB[tt]][:], AF.Tanh, ["P%d" % RB[tt], "dvec"], [s_keys[tt]], scale=0.5, bias=dvec[:, 40 + n:41 + n])
                else:
                    ik = "P%d" % IB[tt]
                    act(P[IB[tt]][:], P[IB[tt]][:], AF.Tanh, [ik, "dvec"], [ik], scale=0.5, bias=dvec[:, 48 + n:49 + n])
            gate(0, 0)
            gate(1, 0)
            gate(1, 1)
            tanh_gate(0, 0)
            GB = (GBk(n, 0), GBk(n, 1))
            for tt in range(NTT):
                mm(P[GB[tt]][:], [(wt[:, k, gcol], h_sb[:, k, tsl(tt)]) for k in range(8)], [wk] + hkeys(tt), ["P%d" % GB[tt]])
            gate(0, 1)
            tanh_gate(1, 0)
            tanh_gate(1, 1)
            for tt in range(NTT):
                gk = "P%d" % GB[tt]
                act(P[GB[tt]][:], P[GB[tt]][:], AF.Gelu_apprx_tanh, [gk], [gk])
            tanh_gate(0, 1)
            for tt in range(NTT):
                act(a_bufs[tt], s_bufs[tt], AF.Exp, [s_keys[tt], "dvec"], [a_keys[tt]],
                    scale=dvec[:, 32 + n:33 + n], bias=dvec[:, 32 + n:33 + n])
            for tt in range(NTT):
                act(s_bufs[tt], a_bufs[tt], AF.Square, [a_keys[tt]], [s_keys[tt]])
            for tt in range(NTT):
                act(s_bufs[tt], s_bufs[tt], AF.Sqrt, [s_keys[tt]], [s_keys[tt]], scale=-1.0, bias=1.0)
            for tt in range(NTT):
                xk = A(["xc%d_%d" % (par3, tt)])
                xc_ = xcs[:, par3 * 2 + tt, :]
                stt(xc_, P[IB[tt]][:], 1.0, xc_, ALU.add, ALU.mult, ["P%d" % IB[tt]] + xk, xk)
            for tt in range(NTT):
                xk = A(["xc%d_%d" % (par3, tt)])
                xc_ = xcs[:, par3 * 2 + tt, :]
                ab = a_bufs[tt]
                stt(xc_, xc_, 0.5, s_bufs[tt], ALU.mult, ALU.mult, [s_keys[tt]] + xk, xk)
                op("dve", lambda e, s, xc_=xc_, ab=ab: e.tensor_tensor_scan(out=xc_, data0=ab, data1=xc_, initial=state[:, n:n + 1],
                                                                            op0=ALU.mult, op1=ALU.add),
                   [a_keys[tt], "state%d" % n] + xk, xk)
                op("dve", lambda e, s, xc_=xc_: e.tensor_copy(out=state[:, n:n + 1], in_=xc_[:, TT - 1:TT]), xk, ["state%d" % n])
                tt_(y_sb[:, n, tsl(tt)], P[GB[tt]][:], xc_, ALU.mult, ["P%d" % GB[tt]] + xk, A(["y%d_%d" % (n, tt)]))

        stage1(0)
        for n in range(8):
            if n < 7:
                stage1(n + 1)
            stage2(n)

    def branch_c(half):
        switch()
        assert not bg
        iters = [(t, hh, tt) for t in range(2) for hh in range(2) for tt in range(NTT)]
        tiles = {}

        def qproj(i):
            t, hh, tt = iters[i]
            if t not in tiles:
                tiles[t] = wget(wmq_d[t], [8, 512])
            wt, wk = tiles[t]
            par = i % 2
            for dc in range(2):
                b = dc
                col = slice(hh * 256 + dc * 128, hh * 256 + (dc + 1) * 128)
                mm(P[b][:], [(wt[:, k, col], h_sb[:, k, tsl(tt)]) for k in range(8)], [wk] + hkeys(tt), ["P%d" % b])
                evac(qm_sb[:, par * 2 + dc, :], P[b][:], ["P%d" % b], A(["qm%d" % (par * 2 + dc)]))

        def rest(i):
            t, hh, tt = iters[i]
            h = 2 * t + hh
            par = i % 2
            qk_ = A(["qm%d" % (par * 2), "qm%d" % (par * 2 + 1)])
            for mc in range(2):
                b = 2 + mc
                mm(P[b][:], [(km_sb[:, h * 2 + dc, mc * 128:(mc + 1) * 128], qm_sb[:, par * 2 + dc, :]) for dc in range(2)],
                   ["km%d" % (h * 2), "km%d" % (h * 2 + 1)] + qk_, ["P%d" % b])
                act(pm_sb[:, par * 2 + mc, :], P[b][:], AF.Exp, ["P%d" % b], A(["pm%d" % (par * 2 + mc)]), scale=MEM_SCALE)
            pmk = A(["pm%d" % (par * 2), "pm%d" % (par * 2 + 1)])
            db = 6 + par
            mm(P[db][:], [(ones, pm_sb[:, par * 2 + mc, :]) for mc in range(2)], ["ones"] + pmk, ["P%d" % db])
            for dc in range(2):
                b = 4 + dc
                col = slice(h * 256 + dc * 128, h * 256 + (dc + 1) * 128)
                mm(P[b][:], [(vm_sb[:, mc, col], pm_sb[:, par * 2 + mc, :]) for mc in range(2)],
                   ["vm%d_%d" % (mc, h // 2) for mc in range(2)] + pmk, ["P%d" % b])
            act(lnv, P[db][:], AF.Ln, ["P%d" % db], ["lnv"])
            act(rstd, lnv, AF.Exp, ["lnv"], ["rstd"], scale=-1.0)
            for dc in range(2):
                b = 4 + dc
                tt_(y_sb[:, h * 2 + dc, tsl(tt)], P[b][:], rstd, ALU.mult, ["P%d" % b, "rstd"], A(["y%d_%d" % (h * 2 + dc, tt)]))

        qproj(0)
        for i in range(len(iters)):
            if i + 1 < len(iters):
                qproj(i + 1)
            rest(i)

    def merge(br, after_tt=None):
        switch()
        cnt = 0
        for mp in range(4):
            wt, wk = wget(wmg_d[br * 4 + mp], [8, 512])
            if mp == 0:
                its = [(mi, tt) for mi in range(2) for tt in range(NTT)]
                for i, (mi, tt) in enumerate(its):
                    mm(P[i][:], [(wt[:, k, mi * 256:mi * 256 + 128], h_sb[:, k, tsl(tt)]) for k in range(8)], [wk] + hkeys(tt), ["P%d" % i])
                for i, (mi, tt) in enumerate(its):
                    mm(P[4 + i][:], [(wt[:, k, mi * 256 + 128:mi * 256 + 256], y_sb[:, k, tsl(tt)]) for k in range(8)],
                       [wk] + A(["y%d_%d" % (k, tt) for k in range(8)]), ["P%d" % (4 + i)])
                for i, (mi, tt) in enumerate(its):
                    m = mi
                    b = i % 2
                    act(sgm[b], P[i][:], AF.Sigmoid, ["P%d" % i, "cvec"], A(["sgm%d" % b]),
                        bias=cvec[:, C_BG + br * 8 + m:C_BG + br * 8 + m + 1])
                    tt_(merged[:, m, tsl(tt)], sgm[b], P[4 + i][:], ALU.mult, A(["sgm%d" % b]) + ["P%d" % (4 + i)], A(["mg%d_%d" % (m, tt)]))
                continue
            for mi in range(2):
                m = 2 * mp + mi
                for tt in range(NTT):
                    b = cnt % 2
                    cnt += 1
                    pg, pp = P[2 * b], P[2 * b + 1]
                    mm(pg[:], [(wt[:, k, mi * 256:mi * 256 + 128], h_sb[:, k, tsl(tt)]) for k in range(8)], [wk] + hkeys(tt), ["P%d" % (2 * b)])
                    mm(pp[:], [(wt[:, k, mi * 256 + 128:mi * 256 + 256], y_sb[:, k, tsl(tt)]) for k in range(8)],
                       [wk] + A(["y%d_%d" % (k, tt) for k in range(8)]), ["P%d" % (2 * b + 1)])
                    act(sgm[b], pg[:], AF.Sigmoid, ["P%d" % (2 * b), "cvec"], A(["sgm%d" % b]),
                        bias=cvec[:, C_BG + br * 8 + m:C_BG + br * 8 + m + 1])
                    tt_(merged[:, m, tsl(tt)], sgm[b], pp[:], ALU.mult, A(["sgm%d" % b]) + ["P%d" % (2 * b + 1)], A(["mg%d_%d" % (m, tt)]))
        w0 = wget(wout_d[0], [8, 512])
        w1 = wget(wout_d[1], [8, 512], keep=1)
        for tt in range(NTT):
            for m2 in range(8):
                wt, wk = w0 if m2 < 4 else w1
                mi = m2 % 4
                b = 4 + cnt % 2
                cnt += 1
                mm(P[b][:], [(wt[:, m, mi * 128:(mi + 1) * 128], merged[:, m, tsl(tt)]) for m in range(8)],
                   [wk] + A(["mg%d_%d" % (m, tt) for m in range(8)]), ["P%d" % b])
                tt_(x_sb[:, m2, tsl(tt)], P[b][:], x_sb[:, m2, tsl(tt)], ALU.add, ["P%d" % b, "x%d_%d" % (m2, tt)], ["x%d_%d" % (m2, tt)])
            if after_tt is not None:
                after_tt(tt)

    oc = [0]

    def final_out_tt(s, half, tt, nxt):
        srcs = [x_sb[:, k, tsl(tt)] for k in range(8)]
        keys = ["x%d_%d" % (k, tt) for k in range(8)]
        for k in range(8):
            def sq_step(k=k):
                b = k % 2
                act(sq_sb[:, b, :], srcs[k], AF.Square, [keys[k]], ["sq%d" % b])
                mm1(P[7][:], ones, sq_sb[:, b, :], k == 0, k == 7, ["sq%d" % b, "ones"], ["P7"])
            bg.append((None, sq_step))
        bg.append((None, lambda: act(lnv, P[7][:], AF.Ln, ["P7"], ["lnv"], scale=1.0 / D, bias=EPS)))
        bg.append((None, lambda: act(rstd, lnv, AF.Exp, ["lnv"], ["rstd"], scale=-0.5)))
        c0 = s * SEQ + half * ST + tt * TT
        for k in range(8):
            def out_step(k=k):
                b = oc[0] % 2
                oc[0] += 1
                stt(ost[b], srcs[k], cvec[:, C_FIN + k:C_FIN + k + 1], rstd, ALU.mult, ALU.mult, [keys[k], "rstd", "cvec"], A(["ost%d" % b]))

                def st(e, sm, b=b, k=k, c0=c0):
                    return e.dma_start(out=outT[k * 128:(k + 1) * 128, c0:c0 + TT], in_=ost[b]).then_inc(sm[("dma", "o%d" % b)], 16)
                S.op("sp", st, reads=["@ost%d" % b, "GUARD"], dma="o%d" % b)
                if nxt is not None:
                    load_x_one(nxt[0], nxt[1], tt, k)
            bg.append((None, out_step))

    def load_x_one(s, half, tt, k):
        c0 = s * SEQ + half * ST + tt * TT

        def ld(e, sm):
            return e.dma_start(out=x_sb[:, k, tsl(tt)], in_=xT[k * 128:(k + 1) * 128, c0:c0 + TT]).then_inc(sm[("dma", "x%d_%d" % (k, tt))], 16)
        S.op("sp", ld, writes=["x%d_%d" % (k, tt)], dma="x%d_%d" % (k, tt))

    def load_x_tt(s, half, tt):
        for k in range(8):
            load_x_one(s, half, tt, k)

    def prologue():
        def ldc(e, sm):
            return e.dma_start(out=cvec, in_=cvec_d).then_inc(sm[("dma", "c")], 16)
        S.op("sp", ldc, writes=["cvec"], dma="c")

        def ldw(e, sm):
            return e.dma_start(out=wrgai.rearrange("p a b c -> p (a b c)"), in_=wrgai_d, max_dma_last_dim=4096).then_inc(sm[("dma", "wc")], 16)
        S.op("pool", ldw, writes=["wrgai"], dma="wc")
        op("dve", lambda e, s: e.memset(ones, 1.0), [], ["ones"])
        op("dve", lambda e, s: e.memset(maskL, 0.0), [], ["maskc"])
        op("dve", lambda e, s: e.memset(maskL[0:1, 64:128], -30000.0), ["maskc"], ["maskc"])
        op("dve", lambda e, s: e.memset(onesr, 0.0), ["maskc"], ["maskc"])
        op("dve", lambda e, s: e.memset(onesr[0:1, :], 1.0), ["maskc"], ["maskc"])
        yv, pv_ = dvec[:, 16:24], dvec[:, 24:32]
        act(yv, cvec[:, C_LAM:C_LAM + 8], AF.Exp, ["cvec"], ["dv_y"], scale=-1.0)
        ts = lambda o, i, s1, s2, o0, o1, r, w: op("dve", lambda e, s: e.tensor_scalar(out=o, in0=i, scalar1=s1, scalar2=s2, op0=o0, op1=o1), r, w)
        ts(pv_, yv, -0.25, 1.0 / 3.0, ALU.mult, ALU.add, ["dv_y"], ["dv_p"])
        tt_(pv_, pv_, yv, ALU.mult, ["dv_p", "dv_y"], ["dv_p"])
        ts(pv_, pv_, -1.0, 0.5, ALU.mult, ALU.add, ["dv_p"], ["dv_p"])
        tt_(pv_, pv_, yv, ALU.mult, ["dv_p", "dv_y"], ["dv_p"])
        ts(pv_, pv_, -1.0, 1.0, ALU.mult, ALU.add, ["dv_p"], ["dv_p"])
        tt_(pv_, pv_, yv, ALU.mult, ["dv_p", "dv_y"], ["dv_p"])
        ts(dvec[:, 0:8], pv_, -8.0, 0.0, ALU.mult, ALU.add, ["dv_p"], ["dvec"])
        ts(dvec[:, 8:16], pv_, -16.0, 0.0, ALU.mult, ALU.add, ["dv_p"], ["dvec"])
        ts(dvec[:, 32:40], pv_, -4.0, 0.0, ALU.mult, ALU.add, ["dv_p"], ["dvec"])
        ts(dvec[:, 40:48], cvec[:, C_BA:C_BA + 8], 0.5, 0.0, ALU.mult, ALU.add, ["cvec", "dvec"], ["dvec"])
        ts(dvec[:, 48:56], cvec[:, C_BI:C_BI + 8], 0.5, 0.0, ALU.mult, ALU.add, ["cvec", "dvec"], ["dvec"])

    def body():
        wstate["i"] = 0
        wstate["released"] = 0
        oc[0] = 0
        prologue()
        sts = [(s, half) for s in range(SPC) for half in range(NST)]
        for tt in range(NTT):
            load_x_tt(0, 0, tt)
        for idx, (s, half) in enumerate(sts):
            if half == 0:
                flush()
                mark("memkv")
                mem_kv(s)
                op("dve", lambda e, sm: e.memset(state, 0.0), ["state%d" % n for n in range(8)], ["state%d" % n for n in range(8)])
                op("dve", lambda e, sm: e.memset(hist.rearrange("p a b -> p (a b)"), 0.0), [], ["hist%d" % n for n in range(8)])
            mark("ffn1")
            norm_tt(C_FFN1, 0, front=True)
            norm_tt(C_FFN1, 1)
            ffn(0, lambda tt: norm_tt(C_MIX, tt))
            mark("A")
            branch_a(half)
            mark("mergeA")
            merge(0)
            mark("B")
            branch_b(half)
            mark("mergeB")
            merge(1)
            mark("C")
            branch_c(half)
            mark("mergeC")
            merge(2, lambda tt: norm_tt(C_FFN2, tt))
            mark("ffn2")

            def tail(tt, s=s, half=half, idx=idx):
                flush()
                final_out_tt(s, half, tt, sts[idx + 1] if idx + 1 < len(sts) else None)
            ffn(1, tail)
        flush()
        mark("end")

    S.dry = True
    body()
    S.dry = False
    ctx["pending_switch"] = False
    cp_ctr[0] = 0
    PHASES["nmm"] = 0
    PHASES["marks"] = []
    body()
    assert wstate["i"] == len(wplan)
    S.check_progress()
    S.emit(nc)
    import sys
    print("[mk] ops per engine", S.count, "dma", {k: v // 16 for k, v in S.dma_cnt.items()}, "wtiles", len(wplan), file=sys.stderr)
    return nc


def _tile(W):
    K, N = W.shape
    kc = K // 128
    return np.ascontiguousarray(W.reshape(kc, 128, N).transpose(1, 0, 2).reshape(128, kc * N))


def _pcol(v):
    return np.ascontiguousarray(v.reshape(-1, 128).T)


def _prep_shared(inp):
    f = lambda k: np.asarray(inp[k], dtype=np.float32)
    sh = {}
    r128 = lambda a: np.arange(a * 128, (a + 1) * 128)
    for name, wi, wd in (("wf1", "ffn1_w_in", "ffn1_w_down"), ("wf2", "ffn2_w_in", "ffn2_w_down")):
        Wi = f(wi)[0]
        Wd = f(wd)[0]
        tiles = []
        for t in range(11):
            cols = np.concatenate([r128(2 * t), HID + r128(2 * t), r128(2 * t + 1), HID + r128(2 * t + 1)])
            tiles.append(_tile(Wi[:, cols]))
        sh[name + "i"] = np.stack(tiles)
        sh[name + "da"] = np.stack([_tile(Wd[0:G0 * 128, m * 128:(m + 1) * 128]) for m in range(8)])
        sh[name + "db"] = np.stack([_tile(Wd[G0 * 128:, m * 128:(m + 1) * 128]) for m in range(8)])
    Win = f("w_in")[0]
    o_cq, o_ckv, o_kr, o_x, o_g, o_mq, o_gate = 0, 384, 640, 704, 1728, 2752, 3776
    kr = np.arange(o_kr, o_kr + 64)
    krsw = np.concatenate([kr[32:], kr[:32]])
    sh["wA1"] = _tile(Win[:, np.concatenate([np.arange(o_cq, o_cq + 384), kr, kr])])
    sh["wA2"] = _tile(Win[:, np.concatenate([np.arange(o_ckv, o_ckv + 256), krsw, krsw])])
    sh["wrg"] = np.stack([_tile(Win[:, np.concatenate([o_x + r128(2 * t), o_g + r128(2 * t), o_x + r128(2 * t + 1), o_g + r128(2 * t + 1)])])
                          for t in range(4)])
    sh["wmq"] = np.stack([_tile(Win[:, o_mq + t * 512:o_mq + (t + 1) * 512]) for t in range(2)])
    Wbr = f("w_branch")[0]
    tiles = []
    for b in range(3):
        for mp in range(4):
            parts = []
            for mi in range(2):
                m = 2 * mp + mi
                parts.append(Win[:, o_gate + b * 1024 + m * 128:o_gate + b * 1024 + (m + 1) * 128])
                parts.append(Wbr[b][:, m * 128:(m + 1) * 128])
            tiles.append(_tile(np.concatenate(parts, axis=1)))
    sh["wmg"] = np.stack(tiles)
    Wo = f("w_out")[0]
    sh["wout"] = np.stack([_tile(Wo[:, t * 512:(t + 1) * 512]) for t in range(2)])
    Wuq = f("w_uq")[0]
    sh["wuqn"] = _tile(Wuq[:, np.concatenate([h * 192 + np.arange(128) for h in range(8)])])
    rope_c = np.concatenate([h * 192 + 128 + np.arange(64) for h in range(8)])
    rope_sw = np.concatenate([h * 192 + 128 + np.concatenate([np.arange(32, 64), np.arange(0, 32)]) for h in range(8)])
    sh["wuqr"] = _tile(Wuq[:, np.concatenate([rope_c, rope_sw])])
    Wukv = f("w_ukv")[0]
    sh["wukv"] = _tile(Wukv[:, np.concatenate([h * 256 + np.arange(128) for h in range(8)] + [h * 256 + 128 + np.arange(128) for h in range(8)])])
    Wm = f("w_mem_kv")[0]
    sh["wmkv"] = np.stack([_tile(Wm[:, t * 512:(t + 1) * 512]) for t in range(4)])
    rg = np.stack([f("w_rg_a")[0], f("w_rg_i")[0]])
    sh["wrgai"] = np.ascontiguousarray(rg.transpose(2, 0, 1, 3).reshape(128, 2048))
    cv = np.zeros((128, NCV), np.float32)
    cv[:, C_FFN1:C_FFN1 + 8] = _pcol(f("ffn1_norm")[0])
    cv[:, C_MIX:C_MIX + 8] = _pcol(f("mix_norm")[0])
    cv[:, C_FFN2:C_FFN2 + 8] = _pcol(f("ffn2_norm")[0])
    cv[:, C_FIN:C_FIN + 8] = _pcol(f("final_norm"))
    cv[:, C_MEM:C_MEM + 8] = _pcol(f("mem_norm")[0])
    cv[:, C_QN:C_QN + 3] = _pcol(f("q_norm")[0])
    cv[:, C_KVN:C_KVN + 2] = _pcol(f("kv_norm")[0])
    cw = f("conv_w")[0]
    for w in range(4):
        cv[:, C_CW + 8 * w:C_CW + 8 * w + 8] = _pcol(cw[w, 0])
    cv[:, C_CB:C_CB + 8] = _pcol(f("conv_b")[0])
    cv[:, C_BA:C_BA + 8] = f("b_rg_a")[0].T
    cv[:, C_BI:C_BI + 8] = f("b_rg_i")[0].T
    cv[:, C_LAM:C_LAM + 8] = _pcol(f("lru_lambda")[0])
    bg = f("b_gate")[0]
    for b in range(3):
        cv[:, C_BG + 8 * b:C_BG + 8 * b + 8] = _pcol(bg[b])
    sh["cvec"] = cv
    pos = np.arange(SEQ, dtype=np.float32)
    inv_freq = (1.0 / (np.float32(10000.0) ** (np.arange(0, 64, 2, dtype=np.float32) / np.float32(64.0)))).astype(np.float32)
    ang = (pos[:, None] * inv_freq[None, :]).astype(np.float32)
    cos, sin = np.cos(ang).astype(np.float32), np.sin(ang).astype(np.float32)
    rope = np.zeros((128, 2, SEQ), np.float32)
    for p in range(128):
        j = p % 64
        rope[p, 0] = cos[:, j % 32]
        rope[p, 1] = sin[:, j % 32] * (-1.0 if j < 32 else 1.0)
    sh["rope"] = rope
    return sh


_NC_CACHE = {}


def kernel(**inputs):
    x = np.asarray(inputs["x"], dtype=np.float32)
    mem = np.asarray(inputs["mem"], dtype=np.float32)
    sh = _prep_shared(inputs)
    if "nc" not in _NC_CACHE:
        _NC_CACHE["nc"] = build_program()
    nc = _NC_CACHE["nc"]
    in_maps = []
    for c in range(NCORES):
        d = dict(sh)
        d["xT"] = np.ascontiguousarray(x[c * SPC:(c + 1) * SPC].transpose(2, 0, 1).reshape(D, SPC * SEQ))
        d["memT"] = np.ascontiguousarray(mem[c * SPC:(c + 1) * SPC].transpose(2, 0, 1).reshape(D, SPC * NMEM))
        in_maps.append(d)
    res = run_bass_kernel_spmd(nc, in_maps, core_ids=list(range(NCORES)))
    out = np.empty((NB, SEQ, D), np.float32)
    for c in range(NCORES):
        o = np.asarray(res.results[c]["outT"]).reshape(D, SPC, SEQ)
        out[c * SPC:(c + 1) * SPC] = o.transpose(1, 2, 0)
    return out
```

```python
import numpy as np
import concourse.bass as bass
import concourse.mybir as mybir
from concourse.bass_utils import run_bass_kernel_spmd

F32 = mybir.dt.float32
BF16 = mybir.dt.bfloat16
AF = mybir.ActivationFunctionType
ALU = mybir.AluOpType


class Sched:
    ENGS = ("pe", "act", "dve", "pool", "sp")

    def __init__(self):
        self.ops = {e: [] for e in self.ENGS}
        self.count = {e: 0 for e in self.ENGS}
        self.last_writer = {}
        self.readers = {}
        self.waited = {e: {} for e in self.ENGS}
        self.dma_cnt = {}
        self.dry = False
        self.vc = {}
        self.tok_idx = {}
        self.n_ops = 0

    def op(self, eng, fn, reads=(), writes=(), dma=None, ndma=1):
        if self.dry:
            return None
        deps = {}
        for r in reads:
            t = self.last_writer.get(r)
            if t is not None:
                deps[t[0]] = max(deps.get(t[0], 0), t[1])
        for w in writes:
            t = self.last_writer.get(w)
            if t is not None:
                deps[t[0]] = max(deps.get(t[0], 0), t[1])
            for t in self.readers.get(w, ()):
                deps[t[0]] = max(deps.get(t[0], 0), t[1])
        if dma is None:
            self.count[eng] += 1
            tok = (("eng", eng), self.count[eng])
        else:
            self.dma_cnt[dma] = self.dma_cnt.get(dma, 0) + 16 * ndma
            tok = (("dma", dma), self.dma_cnt[dma])
        waits = []
        K = self.waited[eng]
        for sk, v in sorted(deps.items(), key=lambda kv: -self.tok_idx.get(kv, 0)):
            if K.get(sk, 0) >= v:
                continue
            waits.append((sk, v))
            for k2, v2 in self.vc[(sk, v)].items():
                if K.get(k2, 0) < v2:
                    K[k2] = v2
        self.n_ops += 1
        self.tok_idx[tok] = self.n_ops
        vc = dict(K)
        vc[tok[0]] = tok[1]
        self.vc[tok] = vc
        self.ops[eng].append((fn, waits, dma is None, tok))
        for r in reads:
            self.readers.setdefault(r, []).append(tok)
        for w in writes:
            self.last_writer[w] = tok
            self.readers[w] = []
        return tok

    def check_progress(self):
        pos = {e: 0 for e in self.ENGS}
        sem = {}
        while True:
            progress = False
            for e in self.ENGS:
                q = self.ops[e]
                while pos[e] < len(q):
                    fn, waits, inc, tok = q[pos[e]]
                    if all(sem.get(sk, 0) >= v for sk, v in waits):
                        sem[tok[0]] = max(sem.get(tok[0], 0), tok[1])
                        pos[e] += 1
                        progress = True
                    else:
                        break
            if not progress:
                break
        stuck = {e: (pos[e], len(self.ops[e])) for e in self.ENGS if pos[e] < len(self.ops[e])}
        assert not stuck, "schedule deadlocks: %r" % stuck

    def emit(self, nc, final_waits=()):
        import contextlib
        dma_keys = sorted(self.dma_cnt.keys(), key=str)
        with contextlib.ExitStack() as es:
            sems = {}
            for e in self.ENGS:
                sems[("eng", e)] = es.enter_context(nc.semaphore("s_" + e))
            for k in dma_keys:
                sems[("dma", k)] = es.enter_context(nc.semaphore("d_" + str(k)))
            block = es.enter_context(nc.Block())
            ops = self.ops

            def run(engname, eng):
                mysem = sems[("eng", engname)]
                for fn, waits, inc, _tok in ops[engname]:
                    for sk, v in waits:
                        eng.wait_ge(sems[sk], v)
                    ins = fn(eng, sems)
                    if inc:
                        ins.then_inc(mysem, 1)

            @block.tensor
            def _(eng):
                run("pe", eng)

            @block.scalar
            def _(eng):
                run("act", eng)

            @block.vector
            def _(eng):
                run("dve", eng)

            @block.gpsimd
            def _(eng):
                run("pool", eng)

            @block.sync
            def _(eng):
                run("sp", eng)
                for k in dma_keys:
                    eng.wait_ge(sems[("dma", k)], self.dma_cnt[k])
                for e in ("pe", "act", "dve", "pool"):
                    if self.count[e]:
                        eng.wait_ge(sems[("eng", e)], self.count[e])


D = 1024
SEQ = 2048
NB = 16
NCORES = 8
SPC = NB // NCORES
ST = 1024
TT = 512
NTT = ST // TT
NST = SEQ // ST
HID = 2816
NHC = HID // 128
G0 = 12
NMEM = 256
EPS = 1e-6
ATT_SCALE = 192.0 ** -0.5
MEM_SCALE = 256.0 ** -0.5
NSLOT = 4
SLOT_ELEMS = 4096

C_FFN1, C_MIX, C_FFN2, C_FIN, C_MEM = 0, 8, 16, 24, 32
C_QN, C_KVN = 40, 43
C_CW, C_CB = 45, 77
C_BA, C_BI, C_LAM = 85, 93, 101
C_BG = 109
NCV = 136


PHASES = {"nmm": 0, "marks": []}


class Arena:
    def __init__(self, nc, name, nbytes):
        self.hb = nc.alloc_sbuf_tensor(name, [128, nbytes // 2], BF16)
        self.hf = self.hb.bitcast(F32)
        self.nbytes = nbytes
        self.off = 0

    def alloc(self, shape, dt, at=None):
        n = int(np.prod(shape))
        nb = n * (4 if dt == F32 else 2)
        off = self.off if at is None else at
        assert off % 32 == 0 and off + nb <= self.nbytes, (off, nb, self.nbytes)
        if dt == F32:
            ap = self.hf[:, off // 4: off // 4 + n]
        else:
            ap = self.hb[:, off // 2: off // 2 + n]
        if len(shape) == 2:
            ap = ap.rearrange("p (a b) -> p a b", a=shape[0])
        elif len(shape) == 3:
            ap = ap.rearrange("p (a b c) -> p a b c", a=shape[0], b=shape[1])
        if at is None:
            self.off = off + (nb + 31) // 32 * 32
        return ap


def build_program():
    nc = bass.Bass("TRN2", target_bir_lowering=False)

    def din(name, shape):
        return nc.dram_tensor(name, list(shape), F32, kind="ExternalInput").ap()

    xT = din("xT", [D, SPC * SEQ])
    memT = din("memT", [D, SPC * NMEM])
    cvec_d = din("cvec", [128, NCV])
    rope_d = din("rope", [128, 2, SEQ])
    wf_i = [din("wf1i", [11, 128, 4096]), din("wf2i", [11, 128, 4096])]
    wf_da = [din("wf1da", [8, 128, G0 * 128]), din("wf2da", [8, 128, G0 * 128])]
    wf_db = [din("wf1db", [8, 128, (NHC - G0) * 128]), din("wf2db", [8, 128, (NHC - G0) * 128])]
    wA1_d = din("wA1", [128, 4096])
    wA2_d = din("wA2", [128, 3072])
    wrg_d = din("wrg", [4, 128, 4096])
    wmq_d = din("wmq", [2, 128, 4096])
    wmg_d = din("wmg", [12, 128, 4096])
    wout_d = din("wout", [2, 128, 4096])
    wuqn_d = din("wuqn", [128, 3072])
    wuqr_d = din("wuqr", [128, 3072])
    wukv_d = din("wukv", [128, 4096])
    wmkv_d = din("wmkv", [4, 128, 4096])
    wrgai_d = din("wrgai", [128, 2048])
    outT = nc.dram_tensor("outT", [D, SPC * SEQ], F32, kind="ExternalOutput").ap()

    total = nc.sbuf_bytes_remaining
    main = Arena(nc, "main", (total - 256) // 64 * 64)
    cvec = main.alloc([1, NCV], F32)[:, 0, :]
    dvec = main.alloc([1, 56], F32)[:, 0, :]
    ones = main.alloc([1, 128], BF16)[:, 0, :]
    maskL = main.alloc([1, 128], BF16)[:, 0, :]
    onesr = main.alloc([1, 64], BF16)[:, 0, :]
    wrgai = main.alloc([2, 8, 128], BF16)
    state = main.alloc([1, 8], F32)[:, 0, :]
    hist = main.alloc([8, 4], F32)
    ring = [main.alloc([1, SLOT_ELEMS], BF16)[:, 0, :] for _ in range(NSLOT)]
    x_sb = main.alloc([8, ST], F32)
    h_sb = main.alloc([8, ST], BF16)
    kn_sb = main.alloc([8, SEQ], BF16)
    v_sb = main.alloc([16, 1024], BF16)
    kr_sb = main.alloc([1, SEQ], BF16)[:, 0, :]
    km_sb = main.alloc([8, NMEM], BF16)
    vm_sb = main.alloc([2, 1024], BF16)
    sq_sb = main.alloc([2, TT], BF16)
    lnv = main.alloc([1, TT], F32)[:, 0, :]
    rstd = main.alloc([1, TT], F32)[:, 0, :]
    rope_sb = main.alloc([2, TT], F32)
    A0 = main.off
    asz = main.nbytes - A0
    assert asz >= 36864, asz

    def aa(shape, dt, rel):
        return main.alloc(shape, dt, at=A0 + rel)

    hid = aa([G0, ST], BF16, 0)
    sg = [aa([1, TT], F32, 24576 + 2048 * i)[:, 0, :] for i in range(2)]
    ost = [aa([1, TT], F32, 28672 + 2048 * i)[:, 0, :] for i in range(2)]
    memx = aa([8, NMEM], F32, 0)
    memn = aa([8, NMEM], BF16, 8192)
    y_sb = aa([8, ST], BF16, 0)
    U = 16384
    cqn = aa([3, TT], BF16, U)
    ckvn = aa([2, TT], BF16, U + 3072)
    qn_sb = aa([2, TT], BF16, U + 5120)
    qrz = aa([4, TT], BF16, U + 7168)
    pT = aa([3, TT], BF16, U + 11264)
    rt = [aa([1, TT], F32, U + 14336 + 2048 * i)[:, 0, :] for i in range(2)]
    merged = aa([8, ST], BF16, U)
    sgm = [aa([1, TT], F32, U + 16384 + 2048 * i)[:, 0, :] for i in range(2)]
    zx = aa([1, 1032], F32, U)[:, 0, :]
    xcs = aa([6, TT], F32, U + 4128)
    xcbs = aa([4, TT], BF16, U + 16416)
    qm_sb = aa([4, TT], BF16, U)
    pm_sb = aa([4, TT], BF16, U + 4096)

    P = [nc.alloc_psum_tensor("ps%d" % i, [128, TT], F32) for i in range(8)]

    S = Sched()
    ctx = {"pending_switch": False}

    def A(keys):
        return ["@" + k for k in keys]

    def op(eng, fn, reads=(), writes=()):
        reads = list(reads)
        writes = list(writes)
        if any(k.startswith("@") for k in reads + writes):
            if ctx["pending_switch"]:
                writes.append("GUARD")
                ctx["pending_switch"] = False
            else:
                reads.append("GUARD")
        return S.op(eng, fn, reads, writes)

    def switch():
        ctx["pending_switch"] = True

    def mark(name):
        if not S.dry:
            PHASES["marks"].append((name, PHASES["nmm"]))

    import collections
    bg = collections.deque()

    def drain(n=3):
        for _ in range(n):
            if not bg:
                return
            bg.popleft()[1]()

    def flush(tt=None):
        if tt is None:
            while bg:
                bg.popleft()[1]()
            return
        last = -1
        for i, (t, _) in enumerate(bg):
            if t == tt:
                last = i
        for _ in range(last + 1):
            bg.popleft()[1]()

    def mm(out, pairs, reads, writes):
        n = len(pairs)
        if not S.dry:
            PHASES["nmm"] += n

        def fn(e, s):
            for i, (l, r) in enumerate(pairs):
                ins = e.matmul(out, lhsT=l, rhs=r, start=(i == 0), stop=(i == n - 1))
            return ins
        op("pe", fn, reads, writes)
        drain()

    def mm1(out, l, r, start, stop, reads, writes):
        if not S.dry:
            PHASES["nmm"] += 1
        op("pe", lambda e, s: e.matmul(out, lhsT=l, rhs=r, start=start, stop=stop), reads, writes)

    def act(out, in_, func, reads, writes, **kw):
        op("act", lambda e, s: e.activation(out=out, in_=in_, func=func, **kw), reads, writes)

    def stt(out, in0, scalar, in1, op0, op1, reads, writes):
        op("dve", lambda e, s: e.scalar_tensor_tensor(out=out, in0=in0, scalar=scalar, in1=in1, op0=op0, op1=op1), reads, writes)

    def tt_(out, in0, in1, o, reads, writes):
        op("dve", lambda e, s: e.tensor_tensor(out=out, in0=in0, in1=in1, op=o), reads, writes)

    cp_ctr = [0]

    def evac(out, in_, reads, writes):
        cp_ctr[0] += 1
        if cp_ctr[0] % 2:
            act(out, in_, AF.Copy, reads, writes)
        else:
            op("dve", lambda e, s: e.tensor_copy(out=out, in_=in_), reads, writes)

    wplan = []
    wstate = {"i": 0, "loaded": 0, "released": 0}

    def issue_load(j):
        dram = wplan[j]
        slot = j % NSLOT
        n = dram.shape[-1]

        def fn(e, s):
            return e.dma_start(out=ring[slot][:, 0:n], in_=dram, max_dma_last_dim=4096).then_inc(s[("dma", "w%d" % slot)], 16)
        S.op("pool", fn, writes=["W%d" % slot], dma="w%d" % slot)

    def wget(dram, shape, keep=0):
        i = wstate["i"]
        wstate["i"] += 1
        assert keep < NSLOT
        wstate["released"] = max(wstate["released"], i - keep)
        if S.dry:
            wplan.append(dram)
        else:
            while wstate["loaded"] < min(len(wplan), wstate["released"] + NSLOT):
                issue_load(wstate["loaded"])
                wstate["loaded"] += 1
            assert wstate["loaded"] > i
        slot = i % NSLOT
        n = int(np.prod(shape))
        ap = ring[slot][:, 0:n]
        if len(shape) == 2:
            ap = ap.rearrange("p (a b) -> p a b", a=shape[0])
        return ap, "W%d" % slot

    def tsl(tt):
        return slice(tt * TT, (tt + 1) * TT)

    def rmsnorm(srcs, src_keys, gcol, nfeat, ntok, outs, out_keys):
        flush()
        for st_ in rmsnorm_steps(srcs, src_keys, gcol, nfeat, ntok, outs, out_keys):
            st_()

    def rmsnorm_steps(srcs, src_keys, gcol, nfeat, ntok, outs, out_keys):
        nk = len(srcs)
        steps = []
        for k in range(nk):
            def sq_step(k=k):
                b = k % 2
                act(sq_sb[:, b, 0:ntok], srcs[k], AF.Square, [src_keys[k]], ["sq%d" % b])
                mm1(P[7][:, 0:ntok], ones, sq_sb[:, b, 0:ntok], k == 0, k == nk - 1, ["sq%d" % b, "ones"], ["P7"])
            steps.append(sq_step)
        steps.append(lambda: act(lnv[:, 0:ntok], P[7][:, 0:ntok], AF.Ln, ["P7"], ["lnv"], scale=1.0 / nfeat, bias=EPS))
        steps.append(lambda: act(rstd[:, 0:ntok], lnv[:, 0:ntok], AF.Exp, ["lnv"], ["rstd"], scale=-0.5))
        for k in range(nk):
            def mul_step(k=k):
                stt(outs[k], srcs[k], cvec[:, gcol + k:gcol + k + 1], rstd[:, 0:ntok], ALU.mult, ALU.mult,
                    [src_keys[k], "rstd", "cvec"], [out_keys[k]])
            steps.append(mul_step)
        return steps

    def norm_tt(gcol, tt, front=False):
        steps = rmsnorm_steps([x_sb[:, k, tsl(tt)] for k in range(8)], ["x%d_%d" % (k, tt) for k in range(8)], gcol, D, TT,
                              [h_sb[:, k, tsl(tt)] for k in range(8)], ["h%d_%d" % (k, tt) for k in range(8)])
        if front:
            bg.extendleft(reversed([(tt, st_) for st_ in steps]))
        else:
            bg.extend([(tt, st_) for st_ in steps])

    def hkeys(tt):
        flush(tt)
        return ["h%d_%d" % (k, tt) for k in range(8)]

    def ffn(fi, after_tt):
        switch()
        cnt = 0
        for grp in range(2):
            j0 = 0 if grp == 0 else G0
            ng = G0 if grp == 0 else NHC - G0
            def up(t, jj, tt, wt, wk):
                nonlocal cnt
                jl = 2 * t + jj - j0
                b = cnt % 2
                cnt += 1
                pg, pu = P[2 * b], P[2 * b + 1]
                mm(pu[:], [(wt[:, k, jj * 256 + 128:jj * 256 + 256], h_sb[:, k, tsl(tt)]) for k in range(8)],
                   [wk] + hkeys(tt), ["P%d" % (2 * b + 1)])
                mm(pg[:], [(wt[:, k, jj * 256:jj * 256 + 128], h_sb[:, k, tsl(tt)]) for k in range(8)],
                   [wk] + hkeys(tt), ["P%d" % (2 * b)])
                act(sg[b], pg[:], AF.Silu, ["P%d" % (2 * b)], A(["sg%d" % b]))
                tt_(hid[:, jl, tsl(tt)], sg[b], pu[:], ALU.mult, A(["sg%d" % b]) + ["P%d" % (2 * b + 1)],
                    A(["hid%d_%d" % (jl, tt)]))
            tiles_g = range(j0 // 2, (j0 + ng) // 2)
            tiles_g = list(tiles_g)
            if grp == 0:
                wa = wget(wf_i[fi][tiles_g[0]], [8, 512])
                wb = wget(wf_i[fi][tiles_g[1]], [8, 512], keep=1)
                for tt in range(NTT):
                    for t, (wt, wk) in ((tiles_g[0], wa), (tiles_g[1], wb)):
                        for jj in range(2):
                            up(t, jj, tt, wt, wk)
                tiles_g = tiles_g[2:]
            for t in tiles_g:
                wt, wk = wget(wf_i[fi][t], [8, 512])
                for jj in range(2):
                    for tt in range(NTT):
                        up(t, jj, tt, wt, wk)
            wd = wf_da[fi] if grp == 0 else wf_db[fi]

            def down(m, tt, wt, wk):
                nonlocal cnt
                b = 4 + (cnt % 2)
                cnt += 1
                mm(P[b][:], [(wt[:, j, :], hid[:, j, tsl(tt)]) for j in range(ng)],
                   [wk] + A(["hid%d_%d" % (j, tt) for j in range(ng)]), ["P%d" % b])
                stt(x_sb[:, m, tsl(tt)], P[b][:], 0.5, x_sb[:, m, tsl(tt)], ALU.mult, ALU.add,
                    ["P%d" % b, "x%d_%d" % (m, tt)], ["x%d_%d" % (m, tt)])
            if grp == 0:
                for m in range(8):
                    wt, wk = wget(wd[m], [ng, 128])
                    for tt in range(NTT):
                        down(m, tt, wt, wk)
            else:
                for tt in range(NTT):
                    for m in range(8):
                        wt, wk = wget(wd[m], [ng, 128])
                        down(m, tt, wt, wk)
                    after_tt(tt)

    def mem_kv(s):
        switch()

        def ld(e, sm):
            return e.dma_start(out=memx, in_=memT.rearrange("(k p) t -> p k t", p=128)[:, :, s * NMEM:(s + 1) * NMEM]).then_inc(sm[("dma", "mem")], 16)
        S.op("sp", ld, reads=["GUARD"], writes=["@memx", "GUARD"], dma="mem")
        ctx["pending_switch"] = False
        rmsnorm([memx[:, k, :] for k in range(8)], A(["memx"] * 8), C_MEM, D, NMEM,
                [memn[:, k, :] for k in range(8)], A(["memn%d" % k for k in range(8)]))
        mk = A(["memn%d" % k for k in range(8)])
        c2 = 0
        for t in range(2):
            wt, wk = wget(wmkv_d[t], [8, 512])
            for c in range(4):
                b = c2 % 2
                c2 += 1
                mm(P[b][:, 0:NMEM], [(wt[:, k, c * 128:(c + 1) * 128], memn[:, k, :]) for k in range(8)], [wk] + mk, ["P%d" % b])
                evac(km_sb[:, t * 4 + c, :], P[b][:, 0:NMEM], ["P%d" % b], ["km%d" % (t * 4 + c)])
        for t in range(2):
            wt, wk = wget(wmkv_d[2 + t], [8, 512])
            for mc in range(2):
                b = c2 % 2
                c2 += 1
                mm(P[b][:], [(memn[:, k, mc * 128:(mc + 1) * 128], wt[:, k, :]) for k in range(8)], [wk] + mk, ["P%d" % b])
                evac(vm_sb[:, mc, t * 512:(t + 1) * 512], P[b][:], ["P%d" % b], ["vm%d_%d" % (mc, t)])

    def branch_a(half):
        switch()
        op("dve", lambda e, s: e.memset(qrz.rearrange("p a b -> p (a b)"), 0.0), [], A(["qrz0", "qrz1", "qrz2", "qrz3"]))
        for tt in range(NTT):
            gt = half * NTT + tt
            tok = slice(gt * TT, (gt + 1) * TT)
            hk = hkeys(tt)

            mark("A1")

            def ldr(e, sm, gt=gt):
                return e.dma_start(out=rope_sb, in_=rope_d[:, :, gt * TT:(gt + 1) * TT]).then_inc(sm[("dma", "rope")], 16)
            S.op("sp", ldr, writes=["rope0", "rope1"], dma="rope")
            w1, w1k = wget(wA1_d, [8, 512])
            for c, b in ((0, 0), (1, 1), (2, 2), (3, 5)):
                mm(P[b][:], [(w1[:, k, c * 128:(c + 1) * 128], h_sb[:, k, tsl(tt)]) for k in range(8)], [w1k] + hk, ["P%d" % b])
            w2, w2k = wget(wA2_d, [8, 384])
            for c, b in ((0, 3), (1, 4), (2, 6)):
                mm(P[b][:], [(w2[:, k, c * 128:(c + 1) * 128], h_sb[:, k, tsl(tt)]) for k in range(8)], [w2k] + hk, ["P%d" % b])
            rmsnorm([P[k][:] for k in range(3)], ["P0", "P1", "P2"], C_QN, 384, TT,
                    [cqn[:, k, :] for k in range(3)], A(["cqn%d" % k for k in range(3)]))
            rmsnorm([P[3 + k][:] for k in range(2)], ["P3", "P4"], C_KVN, 256, TT,
                    [ckvn[:, k, :] for k in range(2)], A(["ckvn%d" % k for k in range(2)]))
            tt_(rt[0], P[5][:], rope_sb[:, 0, :], ALU.mult, ["P5", "rope0"], A(["rt0"]))
            tt_(rt[1], P[6][:], rope_sb[:, 1, :], ALU.mult, ["P6", "rope1"], A(["rt1"]))
            tt_(kr_sb[:, tok], rt[0], rt[1], ALU.add, A(["rt0", "rt1"]), ["kr%d" % gt])
            wkv, wkvk = wget(wukv_d, [2, 2048])
            ck = A(["ckvn0", "ckvn1"])
            c2 = 0
            for h in range(8):
                b = c2 % 6
                c2 += 1
                mm(P[b][:], [(wkv[:, k, h * 128:(h + 1) * 128], ckvn[:, k, :]) for k in range(2)], [wkvk] + ck, ["P%d" % b])
                evac(kn_sb[:, h, tok], P[b][:], ["P%d" % b], ["kn%d_%d" % (h, gt)])
            for tk in range(4):
                for hf in range(2):
                    b = c2 % 6
                    c2 += 1
                    mm(P[b][:], [(ckvn[:, k, tk * 128:(tk + 1) * 128], wkv[:, k, 1024 + hf * 512:1024 + (hf + 1) * 512]) for k in range(2)],
                       [wkvk] + ck, ["P%d" % b])
                    evac(v_sb[:, gt * 4 + tk, hf * 512:(hf + 1) * 512], P[b][:], ["P%d" % b], ["v%d_%d" % (gt * 4 + tk, hf)])
            mark("ATT")
            assert not bg
            wqn, wqnk = wget(wuqn_d, [3, 1024])
            wqr, wqrk = wget(wuqr_d, [3, 1024], keep=1)
            cq = A(["cqn0", "cqn1", "cqn2"])
            nkc = 4 * gt + 4

            def qproj(h):
                qb = h % 2
                hp = h // 2
                rb = hp % 2
                mm(P[6][:], [(wqn[:, k, h * 128:(h + 1) * 128], cqn[:, k, :]) for k in range(3)], [wqnk] + cq, ["P6"])
                evac(qn_sb[:, qb, :], P[6][:], ["P6"], A(["qn%d" % qb]))
                if h % 2 == 0:
                    mm(P[7][:], [(wqr[:, k, hp * 128:(hp + 1) * 128], cqn[:, k, :]) for k in range(3)], [wqrk] + cq, ["P7"])
                    mm(P[6][:], [(wqr[:, k, 512 + hp * 128:512 + (hp + 1) * 128], cqn[:, k, :]) for k in range(3)], [wqrk] + cq, ["P6"])
                    tt_(rt[0], P[7][:], rope_sb[:, 0, :], ALU.mult, ["P7", "rope0"], A(["rt0"]))
                    tt_(rt[1], P[6][:], rope_sb[:, 1, :], ALU.mult, ["P6", "rope1"], A(["rt1"]))
                    tt_(qrz[0:64, rb * 2, :], rt[0][0:64, :], rt[1][0:64, :], ALU.add, A(["rt0", "rt1"]), A(["qrz%d" % (rb * 2)]))
                    tt_(qrz[64:128, rb * 2 + 1, :], rt[0][64:128, :], rt[1][64:128, :], ALU.add, A(["rt0", "rt1"]), A(["qrz%d" % (rb * 2 + 1)]))

            qproj(0)
            for h in range(8):
                qb = h % 2
                rz = ((h // 2) % 2) * 2 + (h % 2)
                if h < 7:
                    qproj(h + 1)
                po, pd = P[2 + (h % 2)], P[4 + (h % 2)]
                pok, pdk = "P%d" % (2 + h % 2), "P%d" % (4 + h % 2)

                def qk(kc):
                    q0 = 0 if kc < 4 * gt else 128 * (kc - 4 * gt)
                    sb = kc % 2
                    ksl = slice(kc * 128, (kc + 1) * 128)
                    pairs = [(kn_sb[:, h, ksl], qn_sb[:, qb, q0:TT]), (kr_sb[:, ksl], qrz[:, rz, q0:TT])]
                    rd = ["kn%d_%d" % (h, kc // 4), "kr%d" % (kc // 4)] + A(["qn%d" % qb, "qrz%d" % rz])
                    if kc < 4 * gt:
                        mm(P[sb][:, q0:TT], pairs, rd, ["P%d" % sb])
                    else:
                        def fn(e, s_, pairs=pairs, sb=sb, q0=q0):
                            e.matmul(P[sb][:, q0:TT], lhsT=pairs[0][0], rhs=pairs[0][1], start=True, stop=False)
                            e.matmul(P[sb][:, q0:TT], lhsT=pairs[1][0], rhs=pairs[1][1], start=False, stop=False)
                            return e.matmul(P[sb][:, q0:q0 + 64], lhsT=maskL, rhs=onesr, start=False, stop=True)
                        if not S.dry:
                            PHASES["nmm"] += 3
                        op("pe", fn, rd + ["maskc"], ["P%d" % sb])
                    pb = kc % 3
                    act(pT[:, pb, q0:TT], P[sb][:, q0:TT], AF.Exp, ["P%d" % sb], A(["pT%d" % pb]), scale=ATT_SCALE)

                def pv(kc):
                    q0 = 0 if kc < 4 * gt else 128 * (kc - 4 * gt)
                    pb = kc % 3
                    mm1(po[:, q0:TT], v_sb[:, kc, h * 128:(h + 1) * 128], pT[:, pb, q0:TT], kc == 0, kc == nkc - 1,
                        ["v%d_%d" % (kc, h // 4)] + A(["pT%d" % pb]), [pok])
                    mm1(pd[:, q0:TT], ones, pT[:, pb, q0:TT], kc == 0, kc == nkc - 1, ["ones"] + A(["pT%d" % pb]), [pdk])

                for kc in range(nkc):
                    qk(kc)
                    if kc >= 1:
                        pv(kc - 1)
                pv(nkc - 1)
                act(lnv, pd[:], AF.Ln, [pdk], ["lnv"])
                act(rstd, lnv, AF.Exp, ["lnv"], ["rstd"], scale=-1.0)
                tt_(y_sb[:, h, tsl(tt)], po[:], rstd, ALU.mult, [pok, "rstd"], A(["y%d_%d" % (h, tt)]))

    def branch_b(half):
        switch()
        assert not bg
        RB, IB = (2, 2), (3, 7)
        GROT = (4, 5, 6)
        s_bufs = (rope_sb[:, 0, :], rope_sb[:, 1, :])
        s_keys = ("rope0", "rope1")

        def GBk(n, tt):
            return GROT[(2 * n + tt) % 3]
        a_bufs = (lnv, rstd)
        a_keys = ("lnv", "rstd")
        tiles = {}

        def tile_for(n):
            t = n // 2
            if t not in tiles:
                tiles[t] = wget(wrg_d[t], [8, 512], keep=1)
            return tiles[t]

        def stage1(n):
            wt, wk = tile_for(n)
            nn = n % 2
            par = n % 2
            par3 = n % 3
            xcol = slice(nn * 256, nn * 256 + 128)
            op("dve", lambda e, s: e.tensor_copy(out=zx[:, 0:3], in_=hist[:, n, 0:3]), ["hist%d" % n], A(["zxh"]))
            for tt in range(NTT):
                b = tt
                xk = "xc%d_%d" % (par3, tt)
                mm(P[b][:], [(wt[:, k, xcol], h_sb[:, k, tsl(tt)]) for k in range(8)], [wk] + hkeys(tt), ["P%d" % b])
                act(zx[:, 3 + tt * TT:3 + (tt + 1) * TT], P[b][:], AF.Copy, ["P%d" % b], A(["zx%d" % tt]))
                act(xcs[:, par3 * 2 + tt, :], P[b][:], AF.Identity, ["P%d" % b, "cvec"], A([xk]),
                    scale=cvec[:, C_CW + 24 + n:C_CW + 25 + n], bias=cvec[:, C_CB + n:C_CB + n + 1])
            op("dve", lambda e, s: e.tensor_copy(out=hist[:, n, 0:3], in_=zx[:, ST:ST + 3]), A(["zx1"]), ["hist%d" % n])
            for tt in range(NTT):
                zk = A(["zxh", "zx0"] if tt == 0 else ["zx0", "zx1"])
                xk = "xc%d_%d" % (par3, tt)
                xc_ = xcs[:, par3 * 2 + tt, :]
                o0 = tt * TT
                for w in range(3):
                    stt(xc_, zx[:, o0 + w:o0 + w + TT], cvec[:, C_CW + 8 * w + n:C_CW + 8 * w + n + 1], xc_, ALU.mult, ALU.add,
                        zk + A([xk]) + ["cvec"], A([xk]))
                xb_ = xcbs[:, par * 2 + tt, :]
                op("pool", lambda e, s, xb_=xb_, xc_=xc_: e.tensor_copy(out=xb_, in_=xc_), A([xk]), A(["xcb%d_%d" % (par, tt)]))

        def stage2(n):
            wt, wk = tile_for(n)
            nn = n % 2
            par = n % 2
            par3 = n % 3
            gcol = slice(nn * 256 + 128, nn * 256 + 256)
            def gate(which, tt):
                xb_ = xcbs[:, par * 2 + tt, :]
                xbk = A(["xcb%d_%d" % (par, tt)])
                bank = RB[tt] if which == 0 else IB[tt]
                mm1(P[bank][:], wrgai[:, which, n, :], xb_, True, True, ["wrgai"] + xbk, ["P%d" % bank])
            def tanh_gate(which, tt):
                if which == 0:
                    act(s_bufs[tt], P[RB[tt]][:], AF.Tanh, ["P%d" % RB[tt], "dvec"], [s_keys[tt]], scale=0.5, bias=dvec[:, 40 + n:41 + n])
                else:
                    ik = "P%d" % IB[tt]
                    act(P[IB[tt]][:], P[IB[tt]][:], AF.Tanh, [ik, "dvec"], [ik], scale=0.5, bias=dvec[:, 48 + n:49 + n])
            gate(0, 0)
            gate(1, 0)
            gate(1, 1)
            tanh_gate(0, 0)
            GB = (GBk(n, 0), GBk(n, 1))
            for tt in range(NTT):
                mm(P[GB[tt]][:], [(wt[:, k, gcol], h_sb[:, k, tsl(tt)]) for k in range(8)], [wk] + hkeys(tt), ["P%d" % GB[tt]])
            gate(0, 1)
            tanh_gate(1, 0)
            tanh_gate(1, 1)
            for tt in range(NTT):
                gk = "P%d" % GB[tt]
                act(P[GB[tt]][:], P[GB[tt]][:], AF.Gelu_apprx_tanh, [gk], [gk])
            tanh_gate(0, 1)
            for tt in range(NTT):
                act(a_bufs[tt], s_bufs[tt], AF.Exp, [s_keys[tt], "dvec"], [a_keys[tt]],
                    scale=dvec[:, 32 + n:33 + n], bias=dvec[:, 32 + n:33 + n])
            for tt in range(NTT):
                act(s_bufs[tt], a_bufs[tt], AF.Square, [a_keys[tt]], [s_keys[tt]])
            for tt in range(NTT):
                act(s_bufs[tt], s_bufs[tt], AF.Sqrt, [s_keys[tt]], [s_keys[tt]], scale=-1.0, bias=1.0)
            for tt in range(NTT):
                xk = A(["xc%d_%d" % (par3, tt)])
                xc_ = xcs[:, par3 * 2 + tt, :]
                stt(xc_, P[IB[tt]][:], 1.0, xc_, ALU.add, ALU.mult, ["P%d" % IB[tt]] + xk, xk)
            for tt in range(NTT):
                xk = A(["xc%d_%d" % (par3, tt)])
                xc_ = xcs[:, par3 * 2 + tt, :]
                ab = a_bufs[tt]
                stt(xc_, xc_, 0.5, s_bufs[tt], ALU.mult, ALU.mult, [s_keys[tt]] + xk, xk)
                op("dve", lambda e, s, xc_=xc_, ab=ab: e.tensor_tensor_scan(out=xc_, data0=ab, data1=xc_, initial=state[:, n:n + 1],
                                                                            op0=ALU.mult, op1=ALU.add),
                   [a_keys[tt], "state%d" % n] + xk, xk)
                op("dve", lambda e, s, xc_=xc_: e.tensor_copy(out=state[:, n:n + 1], in_=xc_[:, TT - 1:TT]), xk, ["state%d" % n])
                tt_(y_sb[:, n, tsl(tt)], P[GB[tt]][:], xc_, ALU.mult, ["P%d" % GB[tt]] + xk, A(["y%d_%d" % (n, tt)]))

        stage1(0)
        for n in range(8):
            if n < 7:
                stage1(n + 1)
            stage2(n)

    def branch_c(half):
        switch()
        assert not bg
        iters = [(t, hh, tt) for t in range(2) for hh in range(2) for tt in range(NTT)]
        tiles = {}

        def qproj(i):
            t, hh, tt = iters[i]
            if t not in tiles:
                tiles[t] = wget(wmq_d[t], [8, 512])
            wt, wk = tiles[t]
            par = i % 2
            for dc in range(2):
                b = dc
                col = slice(hh * 256 + dc * 128, hh * 256 + (dc + 1) * 128)
                mm(P[b][:], [(wt[:, k, col], h_sb[:, k, tsl(tt)]) for k in range(8)], [wk] + hkeys(tt), ["P%d" % b])
                evac(qm_sb[:, par * 2 + dc, :], P[b][:], ["P%d" % b], A(["qm%d" % (par * 2 + dc)]))

        def rest(i):
            t, hh, tt = iters[i]
            h = 2 * t + hh
            par = i % 2
            qk_ = A(["qm%d" % (par * 2), "qm%d" % (par * 2 + 1)])
            for mc in range(2):
                b = 2 + mc
                mm(P[b][:], [(km_sb[:, h * 2 + dc, mc * 128:(mc + 1) * 128], qm_sb[:, par * 2 + dc, :]) for dc in range(2)],
                   ["km%d" % (h * 2), "km%d" % (h * 2 + 1)] + qk_, ["P%d" % b])
                act(pm_sb[:, par * 2 + mc, :], P[b][:], AF.Exp, ["P%d" % b], A(["pm%d" % (par * 2 + mc)]), scale=MEM_SCALE)
            pmk = A(["pm%d" % (par * 2), "pm%d" % (par * 2 + 1)])
            db = 6 + par
            mm(P[db][:], [(ones, pm_sb[:, par * 2 + mc, :]) for mc in range(2)], ["ones"] + pmk, ["P%d" % db])
            for dc in range(2):
                b = 4 + dc
                col = slice(h * 256 + dc * 128, h * 256 + (dc + 1) * 128)
                mm(P[b][:], [(vm_sb[:, mc, col], pm_sb[:, par * 2 + mc, :]) for mc in range(2)],
                   ["vm%d_%d" % (mc, h // 2) for mc in range(2)] + pmk, ["P%d" % b])
            act(lnv, P[db][:], AF.Ln, ["P%d" % db], ["lnv"])
            act(rstd, lnv, AF.Exp, ["lnv"], ["rstd"], scale=-1.0)
            for dc in range(2):
                b = 4 + dc
                tt_(y_sb[:, h * 2 + dc, tsl(tt)], P[b][:], rstd, ALU.mult, ["P%d" % b, "rstd"], A(["y%d_%d" % (h * 2 + dc, tt)]))

        qproj(0)
        for i in range(len(iters)):
            if i + 1 < len(iters):
                qproj(i + 1)
            rest(i)

    def merge(br, after_tt=None):
        switch()
        cnt = 0
        for mp in range(4):
            wt, wk = wget(wmg_d[br * 4 + mp], [8, 512])
            if mp == 0:
                its = [(mi, tt) for mi in range(2) for tt in range(NTT)]
                for i, (mi, tt) in enumerate(its):
                    mm(P[i][:], [(wt[:, k, mi * 256:mi * 256 + 128], h_sb[:, k, tsl(tt)]) for k in range(8)], [wk] + hkeys(tt), ["P%d" % i])
                for i, (mi, tt) in enumerate(its):
                    mm(P[4 + i][:], [(wt[:, k, mi * 256 + 128:mi * 256 + 256], y_sb[:, k, tsl(tt)]) for k in range(8)],
                       [wk] + A(["y%d_%d" % (k, tt) for k in range(8)]), ["P%d" % (4 + i)])
                for i, (mi, tt) in enumerate(its):
                    m = mi
                    b = i % 2
                    act(sgm[b], P[i][:], AF.Sigmoid, ["P%d" % i, "cvec"], A(["sgm%d" % b]),
                        bias=cvec[:, C_BG + br * 8 + m:C_BG + br * 8 + m + 1])
                    tt_(merged[:, m, tsl(tt)], sgm[b], P[4 + i][:], ALU.mult, A(["sgm%d" % b]) + ["P%d" % (4 + i)], A(["mg%d_%d" % (m, tt)]))
                continue
            for mi in range(2):
                m = 2 * mp + mi
                for tt in range(NTT):
                    b = cnt % 2
                    cnt += 1
                    pg, pp = P[2 * b], P[2 * b + 1]
                    mm(pp[:], [(wt[:, k, mi * 256 + 128:mi * 256 + 256], y_sb[:, k, tsl(tt)]) for k in range(8)],
                       [wk] + A(["y%d_%d" % (k, tt) for k in range(8)]), ["P%d" % (2 * b + 1)])
                    mm(pg[:], [(wt[:, k, mi * 256:mi * 256 + 128], h_sb[:, k, tsl(tt)]) for k in range(8)], [wk] + hkeys(tt), ["P%d" % (2 * b)])
                    act(sgm[b], pg[:], AF.Sigmoid, ["P%d" % (2 * b), "cvec"], A(["sgm%d" % b]),
                        bias=cvec[:, C_BG + br * 8 + m:C_BG + br * 8 + m + 1])
                    tt_(merged[:, m, tsl(tt)], sgm[b], pp[:], ALU.mult, A(["sgm%d" % b]) + ["P%d" % (2 * b + 1)], A(["mg%d_%d" % (m, tt)]))
        w0 = wget(wout_d[0], [8, 512])
        w1 = wget(wout_d[1], [8, 512], keep=1)
        for tt in range(NTT):
            for m2 in range(8):
                wt, wk = w0 if m2 < 4 else w1
                mi = m2 % 4
                b = 4 + cnt % 2
                cnt += 1
                mm(P[b][:], [(wt[:, m, mi * 128:(mi + 1) * 128], merged[:, m, tsl(tt)]) for m in range(8)],
                   [wk] + A(["mg%d_%d" % (m, tt) for m in range(8)]), ["P%d" % b])
                tt_(x_sb[:, m2, tsl(tt)], P[b][:], x_sb[:, m2, tsl(tt)], ALU.add, ["P%d" % b, "x%d_%d" % (m2, tt)], ["x%d_%d" % (m2, tt)])
            if after_tt is not None:
                after_tt(tt)

    oc = [0]

    def final_out_tt(s, half, tt, nxt):
        srcs = [x_sb[:, k, tsl(tt)] for k in range(8)]
        keys = ["x%d_%d" % (k, tt) for k in range(8)]
        for k in range(8):
            def sq_step(k=k):
                b = k % 2
                act(sq_sb[:, b, :], srcs[k], AF.Square, [keys[k]], ["sq%d" % b])
                mm1(P[7][:], ones, sq_sb[:, b, :], k == 0, k == 7, ["sq%d" % b, "ones"], ["P7"])
            bg.append((None, sq_step))
        bg.append((None, lambda: act(lnv, P[7][:], AF.Ln, ["P7"], ["lnv"], scale=1.0 / D, bias=EPS)))
        bg.append((None, lambda: act(rstd, lnv, AF.Exp, ["lnv"], ["rstd"], scale=-0.5)))
        c0 = s * SEQ + half * ST + tt * TT
        for k in range(8):
            def out_step(k=k):
                b = oc[0] % 2
                oc[0] += 1
                stt(ost[b], srcs[k], cvec[:, C_FIN + k:C_FIN + k + 1], rstd, ALU.mult, ALU.mult, [keys[k], "rstd", "cvec"], A(["ost%d" % b]))

                def st(e, sm, b=b, k=k, c0=c0):
                    return e.dma_start(out=outT[k * 128:(k + 1) * 128, c0:c0 + TT], in_=ost[b]).then_inc(sm[("dma", "o%d" % b)], 16)
                S.op("sp", st, reads=["@ost%d" % b, "GUARD"], dma="o%d" % b)
                if nxt is not None:
                    load_x_one(nxt[0], nxt[1], tt, k)
            bg.append((None, out_step))

    def load_x_one(s, half, tt, k):
        c0 = s * SEQ + half * ST + tt * TT

        def ld(e, sm):
            return e.dma_start(out=x_sb[:, k, tsl(tt)], in_=xT[k * 128:(k + 1) * 128, c0:c0 + TT]).then_inc(sm[("dma", "x%d_%d" % (k, tt))], 16)
        S.op("sp", ld, writes=["x%d_%d" % (k, tt)], dma="x%d_%d" % (k, tt))

    def load_x_tt(s, half, tt):
        for k in range(8):
            load_x_one(s, half, tt, k)

    def prologue():
        def ldc(e, sm):
            return e.dma_start(out=cvec, in_=cvec_d).then_inc(sm[("dma", "c")], 16)
        S.op("sp", ldc, writes=["cvec"], dma="c")

        def ldw(e, sm):
            return e.dma_start(out=wrgai.rearrange("p a b c -> p (a b c)"), in_=wrgai_d, max_dma_last_dim=4096).then_inc(sm[("dma", "wc")], 16)
        S.op("pool", ldw, writes=["wrgai"], dma="wc")
        op("dve", lambda e, s: e.memset(ones, 1.0), [], ["ones"])
        op("dve", lambda e, s: e.memset(maskL, 0.0), [], ["maskc"])
        op("dve", lambda e, s: e.memset(maskL[0:1, 64:128], -30000.0), ["maskc"], ["maskc"])
        op("dve", lambda e, s: e.memset(onesr, 0.0), ["maskc"], ["maskc"])
        op("dve", lambda e, s: e.memset(onesr[0:1, :], 1.0), ["maskc"], ["maskc"])
        yv, pv_ = dvec[:, 16:24], dvec[:, 24:32]
        act(yv, cvec[:, C_LAM:C_LAM + 8], AF.Exp, ["cvec"], ["dv_y"], scale=-1.0)
        ts = lambda o, i, s1, s2, o0, o1, r, w: op("dve", lambda e, s: e.tensor_scalar(out=o, in0=i, scalar1=s1, scalar2=s2, op0=o0, op1=o1), r, w)
        ts(pv_, yv, -0.25, 1.0 / 3.0, ALU.mult, ALU.add, ["dv_y"], ["dv_p"])
        tt_(pv_, pv_, yv, ALU.mult, ["dv_p", "dv_y"], ["dv_p"])
        ts(pv_, pv_, -1.0, 0.5, ALU.mult, ALU.add, ["dv_p"], ["dv_p"])
        tt_(pv_, pv_, yv, ALU.mult, ["dv_p", "dv_y"], ["dv_p"])
        ts(pv_, pv_, -1.0, 1.0, ALU.mult, ALU.add, ["dv_p"], ["dv_p"])
        tt_(pv_, pv_, yv, ALU.mult, ["dv_p", "dv_y"], ["dv_p"])
        ts(dvec[:, 0:8], pv_, -8.0, 0.0, ALU.mult, ALU.add, ["dv_p"], ["dvec"])
        ts(dvec[:, 8:16], pv_, -16.0, 0.0, ALU.mult, ALU.add, ["dv_p"], ["dvec"])
        ts(dvec[:, 32:40], pv_, -4.0, 0.0, ALU.mult, ALU.add, ["dv_p"], ["dvec"])
        ts(dvec[:, 40:48], cvec[:, C_BA:C_BA + 8], 0.5, 0.0, ALU.mult, ALU.add, ["cvec", "dvec"], ["dvec"])
        ts(dvec[:, 48:56], cvec[:, C_BI:C_BI + 8], 0.5, 0.0, ALU.mult, ALU.add, ["cvec", "dvec"], ["dvec"])

    def body():
        wstate["i"] = 0
        wstate["released"] = 0
        oc[0] = 0
        prologue()
        sts = [(s, half) for s in range(SPC) for half in range(NST)]
        for tt in range(NTT):
            load_x_tt(0, 0, tt)
        for idx, (s, half) in enumerate(sts):
            if half == 0:
                flush()
                mark("memkv")
                mem_kv(s)
                op("dve", lambda e, sm: e.memset(state, 0.0), ["state%d" % n for n in range(8)], ["state%d" % n for n in range(8)])
                op("dve", lambda e, sm: e.memset(hist.rearrange("p a b -> p (a b)"), 0.0), [], ["hist%d" % n for n in range(8)])
            mark("ffn1")
            norm_tt(C_FFN1, 0, front=True)
            norm_tt(C_FFN1, 1)
            ffn(0, lambda tt: norm_tt(C_MIX, tt))
            mark("A")
            branch_a(half)
            mark("mergeA")
            merge(0)
            mark("B")
            branch_b(half)
            mark("mergeB")
            merge(1)
            mark("C")
            branch_c(half)
            mark("mergeC")
            merge(2, lambda tt: norm_tt(C_FFN2, tt))
            mark("ffn2")

            def tail(tt, s=s, half=half, idx=idx):
                flush()
                final_out_tt(s, half, tt, sts[idx + 1] if idx + 1 < len(sts) else None)
            ffn(1, tail)
        flush()
        mark("end")

    S.dry = True
    body()
    S.dry = False
    ctx["pending_switch"] = False
    cp_ctr[0] = 0
    PHASES["nmm"] = 0
    PHASES["marks"] = []
    body()
    assert wstate["i"] == len(wplan)
    S.check_progress()
    S.emit(nc)
    import sys
    print("[mk] waits", {e: sum(len(o[1]) for o in S.ops[e]) for e in S.ENGS}, "ops per engine", S.count, "dma", {k: v // 16 for k, v in S.dma_cnt.items()}, "wtiles", len(wplan), file=sys.stderr)
    return nc


def _tile(W):
    K, N = W.shape
    kc = K // 128
    return np.ascontiguousarray(W.reshape(kc, 128, N).transpose(1, 0, 2).reshape(128, kc * N))


def _pcol(v):
    return np.ascontiguousarray(v.reshape(-1, 128).T)


def _prep_shared(inp):
    f = lambda k: np.asarray(inp[k], dtype=np.float32)
    sh = {}
    r128 = lambda a: np.arange(a * 128, (a + 1) * 128)
    for name, wi, wd in (("wf1", "ffn1_w_in", "ffn1_w_down"), ("wf2", "ffn2_w_in", "ffn2_w_down")):
        Wi = f(wi)[0]
        Wd = f(wd)[0]
        tiles = []
        for t in range(11):
            cols = np.concatenate([r128(2 * t), HID + r128(2 * t), r128(2 * t + 1), HID + r128(2 * t + 1)])
            tiles.append(_tile(Wi[:, cols]))
        sh[name + "i"] = np.stack(tiles)
        sh[name + "da"] = np.stack([_tile(Wd[0:G0 * 128, m * 128:(m + 1) * 128]) for m in range(8)])
        sh[name + "db"] = np.stack([_tile(Wd[G0 * 128:, m * 128:(m + 1) * 128]) for m in range(8)])
    Win = f("w_in")[0]
    o_cq, o_ckv, o_kr, o_x, o_g, o_mq, o_gate = 0, 384, 640, 704, 1728, 2752, 3776
    kr = np.arange(o_kr, o_kr + 64)
    krsw = np.concatenate([kr[32:], kr[:32]])
    sh["wA1"] = _tile(Win[:, np.concatenate([np.arange(o_cq, o_cq + 384), kr, kr])])
    sh["wA2"] = _tile(Win[:, np.concatenate([np.arange(o_ckv, o_ckv + 256), krsw, krsw])])
    sh["wrg"] = np.stack([_tile(Win[:, np.concatenate([o_x + r128(2 * t), o_g + r128(2 * t), o_x + r128(2 * t + 1), o_g + r128(2 * t + 1)])])
                          for t in range(4)])
    sh["wmq"] = np.stack([_tile(Win[:, o_mq + t * 512:o_mq + (t + 1) * 512]) for t in range(2)])
    Wbr = f("w_branch")[0]
    tiles = []
    for b in range(3):
        for mp in range(4):
            parts = []
            for mi in range(2):
                m = 2 * mp + mi
                parts.append(Win[:, o_gate + b * 1024 + m * 128:o_gate + b * 1024 + (m + 1) * 128])
                parts.append(Wbr[b][:, m * 128:(m + 1) * 128])
            tiles.append(_tile(np.concatenate(parts, axis=1)))
    sh["wmg"] = np.stack(tiles)
    Wo = f("w_out")[0]
    sh["wout"] = np.stack([_tile(Wo[:, t * 512:(t + 1) * 512]) for t in range(2)])
    Wuq = f("w_uq")[0]
    sh["wuqn"] = _tile(Wuq[:, np.concatenate([h * 192 + np.arange(128) for h in range(8)])])
    rope_c = np.concatenate([h * 192 + 128 + np.arange(64) for h in range(8)])
    rope_sw = np.concatenate([h * 192 + 128 + np.concatenate([np.arange(32, 64), np.arange(0, 32)]) for h in range(8)])
    sh["wuqr"] = _tile(Wuq[:, np.concatenate([rope_c, rope_sw])])
    Wukv = f("w_ukv")[0]
    sh["wukv"] = _tile(Wukv[:, np.concatenate([h * 256 + np.arange(128) for h in range(8)] + [h * 256 + 128 + np.arange(128) for h in range(8)])])
    Wm = f("w_mem_kv")[0]
    sh["wmkv"] = np.stack([_tile(Wm[:, t * 512:(t + 1) * 512]) for t in range(4)])
    rg = np.stack([f("w_rg_a")[0], f("w_rg_i")[0]])
    sh["wrgai"] = np.ascontiguousarray(rg.transpose(2, 0, 1, 3).reshape(128, 2048))
    cv = np.zeros((128, NCV), np.float32)
    cv[:, C_FFN1:C_FFN1 + 8] = _pcol(f("ffn1_norm")[0])
    cv[:, C_MIX:C_MIX + 8] = _pcol(f("mix_norm")[0])
    cv[:, C_FFN2:C_FFN2 + 8] = _pcol(f("ffn2_norm")[0])
    cv[:, C_FIN:C_FIN + 8] = _pcol(f("final_norm"))
    cv[:, C_MEM:C_MEM + 8] = _pcol(f("mem_norm")[0])
    cv[:, C_QN:C_QN + 3] = _pcol(f("q_norm")[0])
    cv[:, C_KVN:C_KVN + 2] = _pcol(f("kv_norm")[0])
    cw = f("conv_w")[0]
    for w in range(4):
        cv[:, C_CW + 8 * w:C_CW + 8 * w + 8] = _pcol(cw[w, 0])
    cv[:, C_CB:C_CB + 8] = _pcol(f("conv_b")[0])
    cv[:, C_BA:C_BA + 8] = f("b_rg_a")[0].T
    cv[:, C_BI:C_BI + 8] = f("b_rg_i")[0].T
    cv[:, C_LAM:C_LAM + 8] = _pcol(f("lru_lambda")[0])
    bg = f("b_gate")[0]
    for b in range(3):
        cv[:, C_BG + 8 * b:C_BG + 8 * b + 8] = _pcol(bg[b])
    sh["cvec"] = cv
    pos = np.arange(SEQ, dtype=np.float32)
    inv_freq = (1.0 / (np.float32(10000.0) ** (np.arange(0, 64, 2, dtype=np.float32) / np.float32(64.0)))).astype(np.float32)
    ang = (pos[:, None] * inv_freq[None, :]).astype(np.float32)
    cos, sin = np.cos(ang).astype(np.float32), np.sin(ang).astype(np.float32)
    rope = np.zeros((128, 2, SEQ), np.float32)
    for p in range(128):
        j = p % 64
        rope[p, 0] = cos[:, j % 32]
        rope[p, 1] = sin[:, j % 32] * (-1.0 if j < 32 else 1.0)
    sh["rope"] = rope
    return sh


_NC_CACHE = {}


def kernel(**inputs):
    x = np.asarray(inputs["x"], dtype=np.float32)
    mem = np.asarray(inputs["mem"], dtype=np.float32)
    sh = _prep_shared(inputs)
    if "nc" not in _NC_CACHE:
        _NC_CACHE["nc"] = build_program()
    nc = _NC_CACHE["nc"]
    in_maps = []
    for c in range(NCORES):
        d = dict(sh)
        d["xT"] = np.ascontiguousarray(x[c * SPC:(c + 1) * SPC].transpose(2, 0, 1).reshape(D, SPC * SEQ))
        d["memT"] = np.ascontiguousarray(mem[c * SPC:(c + 1) * SPC].transpose(2, 0, 1).reshape(D, SPC * NMEM))
        in_maps.append(d)
    res = run_bass_kernel_spmd(nc, in_maps, core_ids=list(range(NCORES)))
    out = np.empty((NB, SEQ, D), np.float32)
    for c in range(NCORES):
        o = np.asarray(res.results[c]["outT"]).reshape(D, SPC, SEQ)
        out[c * SPC:(c + 1) * SPC] = o.transpose(1, 2, 0)
    return out
```
